# Optimizing a Trainium2 kernel written in Bass

```python
import math
import jax
import jax.numpy as jnp
from jax import lax
import numpy as np

D_MODEL = 2048
BATCH = 2
SEQ = 4096
DEPTH = 2

N_EVEN = (DEPTH + 1) // 2
N_ODD = DEPTH // 2

DEEPNORM_ALPHA = (2.0 * DEPTH) ** 0.25
DEEPNORM_BETA = (8.0 * DEPTH) ** -0.25
MACARON_WEIGHT = 0.5
LN_EPS = 1e-5
RMS_EPS = 1e-6
L2_EPS = 1e-6

D_FF = 5504

GLA_HEADS = 4
GLA_DK = 128
GLA_DV = 256
GLA_RANK = 16
GLA_TAU = 16.0
GLA_CHUNK = 64
GLA_KW = GLA_HEADS * GLA_DK
GLA_VW = GLA_HEADS * GLA_DV

S5_WIDTH = 1024
S5_GROUP = 16
S5_GROUPS = S5_WIDTH // S5_GROUP
S5_STATE = 64
S5_DT_MIN = 1e-3
S5_DT_MAX = 1e-1

AB_Q = 0
AB_K = AB_Q + GLA_KW
AB_V = AB_K + GLA_KW
AB_G = AB_V + GLA_VW
AB_A = AB_G + GLA_VW
AB_U = AB_A + GLA_RANK
AB_IN = AB_U + S5_WIDTH
AB_MIX = GLA_VW + S5_WIDTH

GDN_QK_HEADS = 16
GDN_V_HEADS = 32
GDN_DK = 128
GDN_DV = 128
GDN_CONV = 4
GDN_CHUNK = 64
GDN_KW = GDN_QK_HEADS * GDN_DK
GDN_VW = GDN_V_HEADS * GDN_DV
GDN_QKV = 2 * GDN_KW + GDN_VW
GDN_Z = GDN_QKV
GDN_B = GDN_Z + GDN_VW
GDN_A = GDN_B + GDN_V_HEADS
GDN_IN = GDN_A + GDN_V_HEADS

kernel_name = 'hybrid_gla_s5_gdn_macaron_deepnorm'


def layer_norm(x, g, b):
    xf = x.astype(jnp.float32)
    mu = jnp.mean(xf, axis=-1, keepdims=True)
    var = jnp.mean(jnp.square(xf - mu), axis=-1, keepdims=True)
    return (xf - mu) * lax.rsqrt(var + LN_EPS) * g.astype(jnp.float32) + b.astype(jnp.float32)


def rms_norm(x, g):
    xf = x.astype(jnp.float32)
    return xf * lax.rsqrt(jnp.mean(xf * xf, axis=-1, keepdims=True) + RMS_EPS) * g.astype(jnp.float32)


def l2_normalize(x):
    return x * lax.rsqrt(jnp.sum(x * x, axis=-1, keepdims=True) + L2_EPS)


def swiglu(x, w_gate, w_up, w_down):
    return (jax.nn.silu(x @ w_gate) * (x @ w_up)) @ w_down


def _heads(t, n_heads):
    b, l, w = t.shape
    return t.reshape(b, l, n_heads, w // n_heads).transpose(0, 2, 1, 3)


def _merge(t):
    b, h, l, d = t.shape
    return t.transpose(0, 2, 1, 3).reshape(b, l, h * d)


def gla_chunked(q, k, v, log_a):
    bn, h, l, dk = q.shape
    dv = v.shape[-1]
    c = GLA_CHUNK
    n = l // c
    q = (q * dk ** -0.5).reshape(bn, h, n, c, dk)
    k = k.reshape(bn, h, n, c, dk)
    v = v.reshape(bn, h, n, c, dv)
    b = jnp.cumsum(log_a.reshape(bn, h, n, c, dk), axis=3)
    b_last = b[:, :, :, -1:, :]
    q_dec = q * jnp.exp(b)
    k_inv = k * jnp.exp(-b)
    causal = jnp.tril(jnp.ones((c, c), bool))
    scores = jnp.where(causal, jnp.einsum('bhnik,bhnjk->bhnij', q_dec, k_inv), 0.0)
    o_intra = jnp.einsum('bhnij,bhnjv->bhniv', scores, v)
    k_end = k * jnp.exp(b_last - b)
    d_state = jnp.einsum('bhnck,bhncv->bhnkv', k_end, v)
    chunk_decay = jnp.exp(b_last[:, :, :, 0, :])

    def step(s, inp):
        dec, ds = inp
        return dec[..., None] * s + ds, s

    s0 = jnp.zeros((bn, h, dk, dv), jnp.float32)
    _, s_start = lax.scan(step, s0, (jnp.moveaxis(chunk_decay, 2, 0), jnp.moveaxis(d_state, 2, 0)))
    s_start = jnp.moveaxis(s_start, 0, 2)
    o_inter = jnp.einsum('bhnck,bhnkv->bhncv', q_dec, s_start)
    return (o_intra + o_inter).reshape(bn, h, l, dv)


def _complex_affine_combine(e1, e2):
    a1r, a1i, b1r, b1i = e1
    a2r, a2i, b2r, b2i = e2
    return (a2r * a1r - a2i * a1i,
            a2r * a1i + a2i * a1r,
            a2r * b1r - a2i * b1i + b2r,
            a2r * b1i + a2i * b1r + b2i)


def s5_ssm(u, lam_re, lam_im, b_re, b_im, c_re, c_im, d, log_step):
    f32 = jnp.float32
    bn, l, _ = u.shape
    uf = u.astype(f32).reshape(bn, l, S5_GROUPS, S5_GROUP)
    step = jnp.exp(log_step.astype(f32))[:, None]
    lr = lam_re.astype(f32)
    li = lam_im.astype(f32)
    mag = jnp.exp(lr * step)
    a_re = mag * jnp.cos(li * step)
    a_im = mag * jnp.sin(li * step)
    inv_den = 1.0 / (lr * lr + li * li)
    f_re = ((a_re - 1.0) * lr + a_im * li) * inv_den
    f_im = (a_im * lr - (a_re - 1.0) * li) * inv_den
    br = b_re.astype(f32)
    bi = b_im.astype(f32)
    bb_re = f_re[..., None] * br - f_im[..., None] * bi
    bb_im = f_re[..., None] * bi + f_im[..., None] * br
    bu_re = jnp.einsum('blgh,gph->blgp', uf, bb_re)
    bu_im = jnp.einsum('blgh,gph->blgp', uf, bb_im)
    shape = bu_re.shape
    _, _, s_re, s_im = lax.associative_scan(
        _complex_affine_combine,
        (jnp.broadcast_to(a_re, shape), jnp.broadcast_to(a_im, shape), bu_re, bu_im),
        axis=1)
    y = (jnp.einsum('blgp,ghp->blgh', s_re, c_re.astype(f32))
         - jnp.einsum('blgp,ghp->blgh', s_im, c_im.astype(f32)))
    y = y + d.astype(f32).reshape(S5_GROUPS, S5_GROUP) * uf
    return y.reshape(bn, l, S5_WIDTH)


def gla_s5_mixer(x, w_in, gla_w_lr, gla_b_lr, gla_norm_g, lam_re, lam_im, b_re, b_im,
                 c_re, c_im, s5_d, s5_log_step, glu_w, glu_b, w_out):
    f32 = jnp.float32
    proj = x @ w_in
    q = _heads(proj[..., AB_Q:AB_K], GLA_HEADS).astype(f32)
    k = _heads(proj[..., AB_K:AB_V], GLA_HEADS).astype(f32)
    v = _heads(proj[..., AB_V:AB_G], GLA_HEADS).astype(f32)
    g_out = proj[..., AB_G:AB_A].astype(f32)
    gate_lr = proj[..., AB_A:AB_U]
    u = proj[..., AB_U:AB_IN]
    log_a = jax.nn.log_sigmoid((gate_lr @ gla_w_lr + gla_b_lr).astype(f32)) / GLA_TAU
    o = gla_chunked(q, k, v, _heads(log_a, GLA_HEADS))
    o_gla = _merge(rms_norm(o, gla_norm_g)) * jax.nn.silu(g_out)
    y = jax.nn.gelu(s5_ssm(u, lam_re, lam_im, b_re, b_im, c_re, c_im, s5_d, s5_log_step))
    o_s5 = y * jax.nn.sigmoid(y @ glu_w + glu_b)
    return jnp.concatenate([o_gla, o_s5], axis=-1) @ w_out


def causal_depthwise_conv(x, w):
    kw = w.shape[0]
    xp = jnp.pad(x, ((0, 0), (kw - 1, 0), (0, 0)))
    return lax.conv_general_dilated(xp, w[:, None, :], window_strides=(1,), padding='VALID',
                                    dimension_numbers=('NWC', 'WIO', 'NWC'),
                                    feature_group_count=x.shape[-1])


def gdn_chunked(q, k, v, beta, g):
    bn, h, l, dk = q.shape
    dv = v.shape[-1]
    c = GDN_CHUNK
    n = l // c
    q = (q * dk ** -0.5).reshape(bn, h, n, c, dk)
    k = k.reshape(bn, h, n, c, dk)
    v = v.reshape(bn, h, n, c, dv)
    beta = beta.reshape(bn, h, n, c, 1)
    gc = jnp.cumsum(g.reshape(bn, h, n, c), axis=-1)
    incl = jnp.tril(jnp.ones((c, c), bool))
    strict = jnp.tril(jnp.ones((c, c), bool), -1)
    decay = jnp.exp(jnp.where(incl, gc[..., :, None] - gc[..., None, :], -jnp.inf))
    k_beta = k * beta
    a_strict = jnp.where(strict, jnp.einsum('bhnik,bhnjk->bhnij', k_beta, k) * decay, 0.0)
    rhs = jnp.concatenate([v * beta, k_beta * jnp.exp(gc)[..., None]], axis=-1)
    sol = lax.linalg.triangular_solve(a_strict + jnp.eye(c, dtype=jnp.float32), rhs,
                                      left_side=True, lower=True)
    u_val = sol[..., :dv]
    w_dec = sol[..., dv:]
    attn = jnp.einsum('bhnik,bhnjk->bhnij', q, k) * decay
    q_dec = q * jnp.exp(gc)[..., None]
    g_last = gc[..., -1:]
    k_end = k * jnp.exp(g_last - gc)[..., None]
    d_last = jnp.exp(g_last[..., 0])

    def step(s, inp):
        u_n, w_n, qd_n, at_n, ke_n, dl_n = inp
        v_new = u_n - jnp.einsum('bhck,bhkv->bhcv', w_n, s)
        o_n = jnp.einsum('bhck,bhkv->bhcv', qd_n, s) + jnp.einsum('bhij,bhjv->bhiv', at_n, v_new)
        s = dl_n[..., None, None] * s + jnp.einsum('bhck,bhcv->bhkv', ke_n, v_new)
        return s, o_n

    xs = tuple(jnp.moveaxis(t, 2, 0) for t in (u_val, w_dec, q_dec, attn, k_end, d_last))
    s0 = jnp.zeros((bn, h, dk, dv), jnp.float32)
    _, o = lax.scan(step, s0, xs)
    return jnp.moveaxis(o, 0, 2).reshape(bn, h, l, dv)


def gdn_mixer(x, w_in, conv_w, a_log, dt_bias, norm_g, w_out):
    f32 = jnp.float32
    proj = x @ w_in
    qkv = jax.nn.silu(causal_depthwise_conv(proj[..., :GDN_QKV].astype(f32), conv_w.astype(f32)))
    rep = GDN_V_HEADS // GDN_QK_HEADS
    q = jnp.repeat(l2_normalize(_heads(qkv[..., :GDN_KW], GDN_QK_HEADS)), rep, axis=1)
    k = jnp.repeat(l2_normalize(_heads(qkv[..., GDN_KW:2 * GDN_KW], GDN_QK_HEADS)), rep, axis=1)
    v = _heads(qkv[..., 2 * GDN_KW:], GDN_V_HEADS)
    z = _heads(proj[..., GDN_Z:GDN_B].astype(f32), GDN_V_HEADS)
    beta = jax.nn.sigmoid(proj[..., GDN_B:GDN_A].astype(f32)).transpose(0, 2, 1)
    g = (-jnp.exp(a_log.astype(f32))
         * jax.nn.softplus(proj[..., GDN_A:GDN_IN].astype(f32) + dt_bias.astype(f32))).transpose(0, 2, 1)
    o = gdn_chunked(q, k, v, beta, g)
    o = rms_norm(o, norm_g) * jax.nn.silu(z)
    return _merge(o) @ w_out


def setup_inputs(seed: int = 0) -> dict:
    key = jax.random.key(seed)
    ks = jax.random.split(key, 32)
    f32 = jnp.float32
    D = D_MODEL

    def nrm(i, shape, scale):
        return scale * jax.random.normal(ks[i], shape, f32)

    def uni(i, shape, lo, hi):
        return jax.random.uniform(ks[i], shape, f32, lo, hi)

    x = nrm(0, (BATCH, SEQ, D), 1.0)
    ffn_a_gate = nrm(1, (DEPTH, D, D_FF), D ** -0.5)
    ffn_a_up = nrm(2, (DEPTH, D, D_FF), D ** -0.5)
    ffn_a_down = nrm(3, (DEPTH, D_FF, D), DEEPNORM_BETA * D_FF ** -0.5)
    ffn_b_gate = nrm(4, (DEPTH, D, D_FF), D ** -0.5)
    ffn_b_up = nrm(5, (DEPTH, D, D_FF), D ** -0.5)
    ffn_b_down = nrm(6, (DEPTH, D_FF, D), DEEPNORM_BETA * D_FF ** -0.5)
    ln_g = 1.0 + nrm(7, (DEPTH, 3, D), 0.02)
    ln_b = nrm(8, (DEPTH, 3, D), 0.02)
    ab_w_in = nrm(9, (N_EVEN, D, AB_IN), D ** -0.5)
    gla_w_lr = nrm(10, (N_EVEN, GLA_RANK, GLA_KW), GLA_RANK ** -0.5)
    gla_b_lr = nrm(11, (N_EVEN, GLA_KW), 0.1)
    gla_norm_g = 1.0 + nrm(12, (N_EVEN, GLA_DV), 0.02)
    s5_lam_re = -0.5 * jnp.exp(nrm(13, (N_EVEN, S5_GROUPS, S5_STATE), 0.02))
    s5_lam_im = jnp.broadcast_to(jnp.pi * jnp.arange(S5_STATE, dtype=f32), (N_EVEN, S5_GROUPS, S5_STATE))
    s5_b_re = nrm(14, (N_EVEN, S5_GROUPS, S5_STATE, S5_GROUP), (2 * S5_GROUP) ** -0.5)
    s5_b_im = nrm(15, (N_EVEN, S5_GROUPS, S5_STATE, S5_GROUP), (2 * S5_GROUP) ** -0.5)
    s5_c_re = nrm(16, (N_EVEN, S5_GROUPS, S5_GROUP, S5_STATE), (2 * S5_STATE) ** -0.5)
    s5_c_im = nrm(17, (N_EVEN, S5_GROUPS, S5_GROUP, S5_STATE), (2 * S5_STATE) ** -0.5)
    s5_d = nrm(18, (N_EVEN, S5_WIDTH), 1.0)
    s5_log_step = uni(19, (N_EVEN, S5_GROUPS), math.log(S5_DT_MIN), math.log(S5_DT_MAX))
    s5_glu_w = nrm(20, (N_EVEN, S5_WIDTH, S5_WIDTH), S5_WIDTH ** -0.5)
    s5_glu_b = nrm(21, (N_EVEN, S5_WIDTH), 0.02)
    ab_w_out = nrm(22, (N_EVEN, AB_MIX, D), DEEPNORM_BETA * AB_MIX ** -0.5)
    gdn_w_in = nrm(23, (N_ODD, D, GDN_IN), D ** -0.5)
    gdn_conv_w = nrm(24, (N_ODD, GDN_CONV, GDN_QKV), GDN_CONV ** -0.5)
    gdn_a_log = jnp.log(uni(25, (N_ODD, GDN_V_HEADS), 1.0, 16.0))
    dt = jnp.exp(uni(26, (N_ODD, GDN_V_HEADS), math.log(1e-3), math.log(1e-1)))
    gdn_dt_bias = dt + jnp.log(-jnp.expm1(-dt))
    gdn_norm_g = 1.0 + nrm(27, (N_ODD, GDN_DV), 0.02)
    gdn_w_out = nrm(28, (N_ODD, GDN_VW, D), DEEPNORM_BETA * GDN_VW ** -0.5)
    return {'x': x,
            'ffn_a_gate': ffn_a_gate, 'ffn_a_up': ffn_a_up, 'ffn_a_down': ffn_a_down,
            'ffn_b_gate': ffn_b_gate, 'ffn_b_up': ffn_b_up, 'ffn_b_down': ffn_b_down,
            'ln_g': ln_g, 'ln_b': ln_b,
            'ab_w_in': ab_w_in, 'gla_w_lr': gla_w_lr, 'gla_b_lr': gla_b_lr, 'gla_norm_g': gla_norm_g,
            's5_lam_re': s5_lam_re, 's5_lam_im': s5_lam_im, 's5_b_re': s5_b_re, 's5_b_im': s5_b_im,
            's5_c_re': s5_c_re, 's5_c_im': s5_c_im, 's5_d': s5_d, 's5_log_step': s5_log_step,
            's5_glu_w': s5_glu_w, 's5_glu_b': s5_glu_b, 'ab_w_out': ab_w_out,
            'gdn_w_in': gdn_w_in, 'gdn_conv_w': gdn_conv_w, 'gdn_a_log': gdn_a_log,
            'gdn_dt_bias': gdn_dt_bias, 'gdn_norm_g': gdn_norm_g, 'gdn_w_out': gdn_w_out}


def reference(x, ffn_a_gate, ffn_a_up, ffn_a_down, ffn_b_gate, ffn_b_up, ffn_b_down, ln_g, ln_b,
              ab_w_in, gla_w_lr, gla_b_lr, gla_norm_g, s5_lam_re, s5_lam_im, s5_b_re, s5_b_im,
              s5_c_re, s5_c_im, s5_d, s5_log_step, s5_glu_w, s5_glu_b, ab_w_out,
              gdn_w_in, gdn_conv_w, gdn_a_log, gdn_dt_bias, gdn_norm_g, gdn_w_out):
    for layer in range(DEPTH):
        ffn = swiglu(x, ffn_a_gate[layer], ffn_a_up[layer], ffn_a_down[layer])
        x = layer_norm(DEEPNORM_ALPHA * x + MACARON_WEIGHT * ffn, ln_g[layer, 0], ln_b[layer, 0])
        i = layer // 2
        if layer % 2 == 0:
            mix = gla_s5_mixer(x, ab_w_in[i], gla_w_lr[i], gla_b_lr[i], gla_norm_g[i],
                               s5_lam_re[i], s5_lam_im[i], s5_b_re[i], s5_b_im[i],
                               s5_c_re[i], s5_c_im[i], s5_d[i], s5_log_step[i],
                               s5_glu_w[i], s5_glu_b[i], ab_w_out[i])
        else:
            mix = gdn_mixer(x, gdn_w_in[i], gdn_conv_w[i], gdn_a_log[i], gdn_dt_bias[i],
                            gdn_norm_g[i], gdn_w_out[i])
        x = layer_norm(DEEPNORM_ALPHA * x + mix, ln_g[layer, 1], ln_b[layer, 1])
        ffn = swiglu(x, ffn_b_gate[layer], ffn_b_up[layer], ffn_b_down[layer])
        x = layer_norm(DEEPNORM_ALPHA * x + MACARON_WEIGHT * ffn, ln_g[layer, 2], ln_b[layer, 2])
    return x
```

```python
import contextlib
import numpy as np
import concourse.bass as bass
import concourse.mybir as mybir
from concourse.bass_utils import run_bass_kernel_spmd

F32 = mybir.dt.float32
BF16 = mybir.dt.bfloat16
AF = mybir.ActivationFunctionType
ALU = mybir.AluOpType

NCORES = 8
D = 2048
DFF = 5504
ALPHA = (2.0 * 2) ** 0.25
LN_EPS = 1e-5


class Sched:
    KDMA = 8

    def __init__(self, nc, st):
        self.nc = nc
        self.st = st
        self.E = dict(pe=nc.tensor, act=nc.scalar, dve=nc.vector, pool=nc.gpsimd, sp=nc.sync)
        self.stream = {e: [] for e in self.E}
        self.W = {}
        self.R = {}
        self.ndma = {e: 0 for e in self.E}

    def kd(self, e):
        return 4 if e == 'pool' else self.KDMA

    def op(self, e, fn, r=(), w=(), dma=False):
        rec = dict(e=e, fn=fn, deps=[], dma=dma, sig=False, idx=len(self.stream[e]))
        seen = set()
        isps = lambda x: x == 'psb' or (isinstance(x, tuple) and x[0] == 'ps')
        w = list(w) + [x for x in r if isps(x)]
        r = [x for x in r if not isps(x)]

        def need(p):
            if p is None or id(p) in seen:
                return
            if (not p['dma']) and p['e'] == e and e == 'pe':
                return
            seen.add(id(p))
            rec['deps'].append(p)
            p['sig'] = True

        for x in r:
            need(self.W.get(x))
        for x in w:
            need(self.W.get(x))
            for p in self.R.get(x, {}).values():
                need(p)
        if dma:
            rec['dn'] = self.ndma[e]
            self.ndma[e] += 1
            rec['sig'] = True
        self.stream[e].append(rec)
        for x in r:
            self.R.setdefault(x, {})[e + ('d' if dma else '')] = rec
        for x in w:
            self.W[x] = rec
            self.R[x] = {}
        return rec

    def emit(self, final_waits=()):
        nc, st = self.nc, self.st
        esem = {e: st.enter_context(nc.semaphore("s_" + e)) for e in self.E}
        dsem = {e: [st.enter_context(nc.semaphore("d_%s%d" % (e, i))) for i in range(self.KDMA)]
                for e in self.E if self.ndma[e] > 0}
        for e, lst in self.stream.items():
            c = 0
            dl = []
            for rec in lst:
                if rec['dma']:
                    n = rec['dn']
                    kd = self.kd(e)
                    rec['sem'] = dsem[e][n % kd]
                    rec['val'] = 16 * (n // kd + 1)
                    dl.append(rec)
                elif rec['sig']:
                    c += 1
                    rec['sem'] = esem[e]
                    rec['val'] = c
        nwait = 0
        for e in self.E:
            eng = self.E[e]
            known = {}
            dlist = [r_ for r_ in self.stream[e] if r_['dma']]
            for rec in self.stream[e]:
                waits = []
                for p in rec['deps']:
                    waits.append((p['sem'], p['val']))
                if rec['dma'] and rec['dn'] >= self.kd(e):
                    p = dlist[rec['dn'] - self.kd(e)]
                    waits.append((p['sem'], p['val']))
                for sem, val in waits:
                    k = id(sem)
                    if known.get(k, 0) >= val:
                        continue
                    known[k] = val
                    eng.wait_ge(sem, val)
                    nwait += 1
                ins = rec['fn']()
                if rec['dma']:
                    ins.then_inc(rec['sem'], 16)
                elif rec['sig']:
                    ins.then_inc(rec['sem'], 1)
            if e == 'sp':
                for rec in final_waits:
                    if known.get(id(rec['sem']), 0) < rec['val']:
                        known[id(rec['sem'])] = rec['val']
                        eng.wait_ge(rec['sem'], rec['val'])
        return nwait


def _mk(shape_name_dtype, nc, st, psum=False):
    name, shape, dt = shape_name_dtype
    if psum:
        return st.enter_context(nc.psum_tensor(name, shape, dt))
    return st.enter_context(nc.sbuf_tensor(name, shape, dt))


def emit_ffn_ln(nc, st, S, T, x_tm, x_fm, wg, wu, wd, lng, lnb, y_tm, pfx="f"):
    NT = T // 128
    NTB = T // 512
    KC = D // 128
    NF = DFF // 128
    parts = [list(range(0, 15)), list(range(15, 29)), list(range(29, 43))]
    PF = 15
    xa = _mk((pfx + "xa", [128, NT, D], F32), nc, st)
    xT = _mk((pfx + "xT", [128, KC, T], BF16), nc, st)
    hT = _mk((pfx + "hT", [128, PF, T], BF16), nc, st)
    wgu = _mk((pfx + "wgu", [128, 2, 2, KC, 256], BF16), nc, st)
    wdb = _mk((pfx + "wdb", [128, 2, PF, 256], BF16), nc, st)
    sg = _mk((pfx + "sg", [128, 2, 512], F32), nc, st)
    stats = _mk((pfx + "stats", [128, 4, 6], F32), nc, st)
    mv = _mk((pfx + "mv", [128, 4], F32), nc, st)
    ps = _mk((pfx + "ps", [128, 8, 512], F32), nc, st, psum=True)
    gbc = _mk((pfx + "gbc", [128, 2, D], F32), nc, st)

    x_tm_v = x_tm.rearrange("(t p) d -> p t d", p=128)
    y_tm_v = y_tm.rearrange("(t p) d -> p t d", p=128)
    x_fm_v = x_fm.rearrange("(k p) t -> p k t", p=128)
    wg_v = wg.rearrange("(k p) f -> p k f", p=128)
    wu_v = wu.rearrange("(k p) f -> p k f", p=128)
    wd_v = wd.rearrange("(f p) d -> p f d", p=128)

    for t in range(NT):
        S.op('sp', lambda t=t: nc.sync.dma_start(out=xa[:, t, :], in_=x_tm_v[:, t, :]),
             w=[('xa', t)], dma=True)
        S.op('act', lambda t=t: nc.scalar.mul(xa[:, t, :], xa[:, t, :], ALPHA),
             r=[('xa', t)], w=[('xa', t)])
    for k in range(KC):
        S.op('pool', lambda k=k: nc.gpsimd.dma_start(out=xT[:, k, :], in_=x_fm_v[:, k, :]),
             w=[('xT', k)], dma=True)
    S.op('sp', lambda: nc.sync.dma_start(out=gbc[:, 0, :], in_=lng),
         w=[('gbc', 0)], dma=True)
    S.op('sp', lambda: nc.sync.dma_start(out=gbc[:, 1, :], in_=lnb),
         w=[('gbc', 1)], dma=True)

    pairs_loaded = {}
    npair = 0
    for pi, part in enumerate(parts):
        fpairs = [part[i:i + 2] for i in range(0, len(part), 2)]
        for fp in fpairs:
            b = npair % 2
            npair += 1
            ncols = 128 * len(fp)
            c0 = fp[0] * 128
            S.op('pool', lambda b=b, c0=c0, ncols=ncols: nc.gpsimd.dma_start(
                out=wgu[:, b, 0, :, 0:ncols], in_=wg_v[:, :, c0:c0 + ncols]),
                w=[('wgu', b, 0)], dma=True)
            S.op('pool', lambda b=b, c0=c0, ncols=ncols: nc.gpsimd.dma_start(
                out=wgu[:, b, 1, :, 0:ncols], in_=wu_v[:, :, c0:c0 + ncols]),
                w=[('wgu', b, 1)], dma=True)
            for j, f in enumerate(fp):
                fi = f - part[0]
                for tb in range(NTB):
                    for gu in range(2):
                        bank = gu * 4 + tb
                        for k in range(KC):
                            S.op('pe', lambda b=b, gu=gu, k=k, j=j, tb=tb, bank=bank: nc.tensor.matmul(
                                ps[:, bank, :], lhsT=wgu[:, b, gu, k, j * 128:(j + 1) * 128],
                                rhs=xT[:, k, tb * 512:(tb + 1) * 512], start=(k == 0), stop=(k == KC - 1)),
                                r=[('wgu', b, gu), ('xT', k)], w=[('ps', bank)])
                    S.op('act', lambda tb=tb: nc.scalar.activation(out=sg[:, tb, :], in_=ps[:, tb, :], func=AF.Silu),
                         r=[('ps', tb)], w=[('sg', tb)])
                    S.op('dve', lambda tb=tb, fi=fi: nc.vector.tensor_tensor(
                        out=hT[:, fi, tb * 512:(tb + 1) * 512], in0=sg[:, tb, :], in1=ps[:, 4 + tb, :], op=ALU.mult),
                        r=[('sg', tb), ('ps', 4 + tb)], w=[('hT', fi)])
        nfp = len(part)
        for cb in range(D // 256):
            b = cb % 2
            S.op('pool', lambda b=b, cb=cb, nfp=nfp, f0=part[0]: nc.gpsimd.dma_start(
                out=wdb[:, b, 0:nfp, :], in_=wd_v[:, f0:f0 + nfp, cb * 256:(cb + 1) * 256]),
                w=[('wdb', b)], dma=True)
            half = (cb % 2) * 4
            for fi in range(nfp):
                for t in range(NT):
                    bank = half + t // 2
                    off = (t % 2) * 256
                    S.op('pe', lambda b=b, fi=fi, t=t, bank=bank, off=off, nfp=nfp: nc.tensor.matmul(
                        ps[:, bank, off:off + 256], lhsT=hT[:, fi, t * 128:(t + 1) * 128],
                        rhs=wdb[:, b, fi, :], start=(fi == 0 and t % 2 == 0), stop=(fi == nfp - 1),
                        skip_group_check=True),
                        r=[('hT', fi), ('wdb', b)], w=[('ps', bank)])
            for t in range(NT):
                bank = half + t // 2
                off = (t % 2) * 256
                S.op('dve', lambda t=t, bank=bank, off=off, cb=cb: nc.vector.scalar_tensor_tensor(
                    out=xa[:, t, cb * 256:(cb + 1) * 256], in0=ps[:, bank, off:off + 256], scalar=0.5,
                    in1=xa[:, t, cb * 256:(cb + 1) * 256], op0=ALU.mult, op1=ALU.add),
                    r=[('ps', bank), ('xa', t)], w=[('xa', t)])
    outs = []
    for t in range(NT):
        for c in range(4):
            S.op('dve', lambda t=t, c=c: nc.vector.bn_stats(out=stats[:, c, :], in_=xa[:, t, c * 512:(c + 1) * 512]),
                 r=[('xa', t)], w=[('stats',)])
        S.op('dve', lambda: nc.vector.bn_aggr(out=mv[:, 0:2], in_=stats[:].rearrange("p a b -> p (a b)")),
             r=[('stats',)], w=[('mv',)])
        S.op('dve', lambda: nc.vector.tensor_scalar_add(mv[:, 3:4], mv[:, 1:2], LN_EPS),
             r=[('mv',)], w=[('mv',)])
        S.op('act', lambda: nc.scalar.activation(out=mv[:, 3:4], in_=mv[:, 3:4], func=AF.Sqrt),
             r=[('mv',)], w=[('mv',)])
        S.op('dve', lambda: nc.vector.reciprocal(mv[:, 2:3], mv[:, 3:4]),
             r=[('mv',)], w=[('mv',)])
        S.op('dve', lambda t=t: nc.vector.tensor_scalar(out=xa[:, t, :], in0=xa[:, t, :], scalar1=mv[:, 0:1],
                                                        scalar2=mv[:, 2:3], op0=ALU.subtract, op1=ALU.mult),
             r=[('mv',), ('xa', t)], w=[('xa', t)])
        S.op('dve', lambda t=t: nc.vector.tensor_tensor(out=xa[:, t, :], in0=xa[:, t, :], in1=gbc[:, 0, :], op=ALU.mult),
             r=[('gbc', 0), ('xa', t)], w=[('xa', t)])
        S.op('dve', lambda t=t: nc.vector.tensor_tensor(out=xa[:, t, :], in0=xa[:, t, :], in1=gbc[:, 1, :], op=ALU.add),
             r=[('gbc', 1), ('xa', t)], w=[('xa', t)])
        outs.append(S.op('sp', lambda t=t: nc.sync.dma_start(out=y_tm_v[:, t, :], in_=xa[:, t, :]),
                         r=[('xa', t)], dma=True))
    return outs


def build_ffn(T):
    nc = bass.Bass("TRN2", target_bir_lowering=False)
    x_tm = nc.dram_tensor("x_tm", [T, D], F32, kind="ExternalInput").ap()
    x_fm = nc.dram_tensor("x_fm", [D, T], F32, kind="ExternalInput").ap()
    wg = nc.dram_tensor("wg", [D, DFF], F32, kind="ExternalInput").ap()
    wu = nc.dram_tensor("wu", [D, DFF], F32, kind="ExternalInput").ap()
    wd = nc.dram_tensor("wd", [DFF, D], F32, kind="ExternalInput").ap()
    lng = nc.dram_tensor("lng", [128, D], F32, kind="ExternalInput").ap()
    lnb = nc.dram_tensor("lnb", [128, D], F32, kind="ExternalInput").ap()
    y = nc.dram_tensor("y", [T, D], F32, kind="ExternalOutput").ap()
    with contextlib.ExitStack() as st:
        S = Sched(nc, st)
        outs = emit_ffn_ln(nc, st, S, T, x_tm, x_fm, wg, wu, wd, lng, lnb, y)
        S.emit(final_waits=outs)
    return nc


def build_linear(T, K, N):
    nc = bass.Bass("TRN2", target_bir_lowering=False)
    xT_d = nc.dram_tensor("xT", [K, T], F32, kind="ExternalInput").ap()
    w_d = nc.dram_tensor("w", [K, N], F32, kind="ExternalInput").ap()
    y_d = nc.dram_tensor("y", [T, N], F32, kind="ExternalOutput").ap()
    KC = K // 128
    TB = 1024 if K <= 2048 else 512
    TB = min(TB, T)
    xv = xT_d.rearrange("(k p) t -> p k t", p=128)
    wv = w_d.rearrange("(k p) n -> p k n", p=128)
    with contextlib.ExitStack() as st:
        S = Sched(nc, st)
        xT = _mk(("xT_s", [128, 2, KC, TB], BF16), nc, st)
        wt = _mk(("w_s", [128, 2, KC, 512], BF16), nc, st)
        ot = _mk(("o_s", [128, 4, 512], F32), nc, st)
        ps = _mk(("ps", [128, 4, 512], F32), nc, st, psum=True)
        outs = []
        nw = 0
        no = 0
        for tb in range(T // TB):
            xb = tb % 2
            S.op('pool', lambda xb=xb, tb=tb: nc.gpsimd.dma_start(out=xT[:, xb, :, :], in_=xv[:, :, tb * TB:(tb + 1) * TB]),
                 w=[('xT', xb)], dma=True)
            for cb in range((N + 511) // 512):
                n0 = cb * 512
                ncol = min(512, N - n0)
                wb = nw % 2
                nw += 1
                S.op('pool', lambda wb=wb, n0=n0, ncol=ncol: nc.gpsimd.dma_start(out=wt[:, wb, :, 0:ncol], in_=wv[:, :, n0:n0 + ncol]),
                     w=[('w', wb)], dma=True)
                for tt in range(TB // 128):
                    ob = no % 4
                    no += 1
                    for k in range(KC):
                        S.op('pe', lambda xb=xb, wb=wb, k=k, tt=tt, ob=ob, ncol=ncol: nc.tensor.matmul(
                            ps[:, ob, 0:ncol], lhsT=xT[:, xb, k, tt * 128:(tt + 1) * 128], rhs=wt[:, wb, k, 0:ncol],
                            start=(k == 0), stop=(k == KC - 1)),
                            r=[('xT', xb), ('w', wb)], w=[('ps', ob)])
                    eng = 'act' if ob % 2 == 0 else 'dve'
                    if eng == 'act':
                        S.op('act', lambda ob=ob, ncol=ncol: nc.scalar.copy(ot[:, ob, 0:ncol], ps[:, ob, 0:ncol]),
                             r=[('ps', ob)], w=[('ot', ob)])
                    else:
                        S.op('dve', lambda ob=ob, ncol=ncol: nc.vector.tensor_copy(ot[:, ob, 0:ncol], ps[:, ob, 0:ncol]),
                             r=[('ps', ob)], w=[('ot', ob)])
                    r0 = tb * TB + tt * 128
                    outs.append(S.op('sp', lambda ob=ob, ncol=ncol, r0=r0, n0=n0: nc.sync.dma_start(
                        out=y_d[r0:r0 + 128, n0:n0 + ncol], in_=ot[:, ob, 0:ncol]),
                        r=[('ot', ob)], dma=True))
        S.emit(final_waits=outs[-8:])
    return nc


def build_gla(L):
    nc = bass.Bass("TRN2", target_bir_lowering=False)
    di = lambda n, sh: nc.dram_tensor(n, sh, F32, kind="ExternalInput").ap()
    qT_d, kT_d = di("qT", [128, L]), di("kT", [128, L])
    ktm_d, vtm_d = di("ktm", [L, 128]), di("vtm", [L, 256])
    gT_d, glr_d = di("gT", [256, L]), di("glrT", [16, L])
    wlr_d, blr_d, ng_d = di("wlr", [16, 128]), di("blr", [1, 128]), di("ng", [128, 2])
    tri_d, ones_d = di("triT", [128, 128]), di("ones", [128, 128])
    oT_d = nc.dram_tensor("oT", [256, L], F32, kind="ExternalOutput").ap()
    NT = L // 128
    with contextlib.ExitStack() as st:
        S = Sched(nc, st)
        mk = lambda n, sh, dt=F32: _mk((n, sh, dt), nc, st)
        qT, kT = mk("qT_s", [128, L]), mk("kT_s", [128, L])
        ktm = mk("ktm_s", [128, NT, 128])
        vbf = mk("v_s", [128, NT, 256], BF16)
        gT = mk("gT_s", [128, 2, L])
        glr = mk("glr_s", [16, L])
        wlr, blr, ng = mk("wlr_s", [16, 128]), mk("blr_s", [1, 128]), mk("ng_s", [128, 2])
        tri, ones = mk("tri_s", [128, 128]), mk("ones_s", [128, 128])
        St, Sbf = mk("S_s", [128, 256]), mk("Sbf_s", [128, 256], BF16)
        t_e, t_sp = mk("t_e", [128, 128]), mk("t_sp", [128, 128])
        einv_tm, e_fm, einv_fm = mk("einv_tm", [128, 128]), mk("e_fm", [128, 128]), mk("einv_fm", [128, 128])
        kinv_tm, qdec, kinv_fm = mk("kinv_tm", [128, 128], BF16), mk("qdec", [128, 128], BF16), mk("kinv_fm", [128, 128], BF16)
        scm = mk("scm", [128, 128], BF16)
        t_sq, t_r, t_sg, t_on = mk("t_sq", [128, 2, 128]), mk("t_r", [128, 128]), mk("t_sg", [128, 2, 128]), mk("t_on", [128, 2, 128])
        ps = _mk(("ps", [128, 8, 512], F32), nc, st, psum=True)
        ld = lambda e, out, in_, res: S.op(e, lambda: (nc.sync if e == 'sp' else nc.gpsimd).dma_start(out=out, in_=in_), w=[res], dma=True)
        ld('sp', qT[:], qT_d, 'qT'); ld('sp', kT[:], kT_d, 'kT')
        ld('sp', ktm[:], ktm_d.rearrange("(n p) d -> p n d", p=128), 'ktm')
        ld('pool', vbf[:], vtm_d.rearrange("(n p) d -> p n d", p=128), 'v')
        ld('sp', gT[:], gT_d.rearrange("(c p) l -> p c l", p=128), 'gT')
        ld('sp', glr[:], glr_d, 'glr'); ld('sp', wlr[:], wlr_d, 'wlr'); ld('sp', blr[:], blr_d, 'blr')
        ld('sp', ng[:], ng_d, 'ng'); ld('sp', tri[:], tri_d, 'tri'); ld('sp', ones[:], ones_d, 'ones')
        S.op('dve', lambda: nc.vector.memset(St[:], 0.0), w=['S'])
        S.op('dve', lambda: nc.vector.memset(Sbf[:], 0.0), w=['Sbf'])
        outs = []
        P = lambda i, n=128: ps[:, i, 0:n]
        for n in range(NT):
            sl = slice(n * 128, (n + 1) * 128)
            S.op('pe', lambda sl=sl: nc.tensor.matmul(P(0), lhsT=glr[0:16, sl], rhs=wlr[0:16, :], start=True, stop=False),
                 r=['glr', 'wlr'], w=[('ps', 0)])
            S.op('pe', lambda: nc.tensor.matmul(P(0), lhsT=ones[0:1, :], rhs=blr[0:1, :], start=False, stop=True),
                 r=['ones', 'blr'], w=[('ps', 0)])
            S.op('act', lambda: nc.scalar.activation(out=t_e[:], in_=P(0), func=AF.Exp, scale=-1.0), r=[('ps', 0)], w=['t_e'])
            S.op('dve', lambda: nc.vector.tensor_scalar_add(t_e[:], t_e[:], 1.0), r=['t_e'], w=['t_e'])
            S.op('act', lambda: nc.scalar.activation(out=t_sp[:], in_=t_e[:], func=AF.Ln), r=['t_e'], w=['t_sp'])
            S.op('pe', lambda: nc.tensor.matmul(P(1), lhsT=tri[:], rhs=t_sp[:], start=True, stop=True), r=['tri', 't_sp'], w=[('ps', 1)])
            S.op('pe', lambda: nc.tensor.matmul(P(2), lhsT=t_sp[:], rhs=tri[:], start=True, stop=True), r=['tri', 't_sp'], w=[('ps', 2)])
            S.op('act', lambda: nc.scalar.activation(out=einv_tm[:], in_=P(1), func=AF.Exp, scale=1.0 / 16), r=[('ps', 1)], w=['einv_tm'])
            S.op('act', lambda: nc.scalar.activation(out=e_fm[:], in_=P(2), func=AF.Exp, scale=-1.0 / 16), r=[('ps', 2)], w=['e_fm'])
            S.op('act', lambda: nc.scalar.activation(out=einv_fm[:], in_=P(2), func=AF.Exp, scale=1.0 / 16), r=[('ps', 2)], w=['einv_fm'])
            S.op('dve', lambda n=n: nc.vector.tensor_tensor(out=kinv_tm[:], in0=ktm[:, n, :], in1=einv_tm[:], op=ALU.mult),
                 r=['ktm', 'einv_tm'], w=['kinv_tm'])
            S.op('dve', lambda sl=sl: nc.vector.scalar_tensor_tensor(out=qdec[:], in0=qT[:, sl], scalar=128.0 ** -0.5, in1=e_fm[:],
                                                                   op0=ALU.mult, op1=ALU.mult), r=['qT', 'e_fm'], w=['qdec'])
            S.op('dve', lambda sl=sl: nc.vector.tensor_tensor(out=kinv_fm[:], in0=kT[:, sl], in1=einv_fm[:], op=ALU.mult),
                 r=['kT', 'einv_fm'], w=['kinv_fm'])
            S.op('pe', lambda: nc.tensor.matmul(P(3), lhsT=kinv_fm[:], rhs=qdec[:], start=True, stop=True),
                 r=['kinv_fm', 'qdec'], w=[('ps', 3)])
            S.op('dve', lambda: nc.vector.tensor_tensor(out=scm[:], in0=P(3), in1=tri[:], op=ALU.mult), r=[('ps', 3), 'tri'], w=['scm'])
            for c in range(2):
                S.op('pe', lambda c=c, n=n: nc.tensor.matmul(P(4 + c), lhsT=vbf[:, n, c * 128:(c + 1) * 128], rhs=scm[:], start=True, stop=False),
                     r=['v', 'scm'], w=[('ps', 4 + c)])
                S.op('pe', lambda c=c: nc.tensor.matmul(P(4 + c), lhsT=Sbf[:, c * 128:(c + 1) * 128], rhs=qdec[:], start=False, stop=True),
                     r=['Sbf', 'qdec'], w=[('ps', 4 + c)])
            S.op('pe', lambda n=n: nc.tensor.matmul(P(6, 256), lhsT=kinv_tm[:], rhs=vbf[:, n, :], start=True, stop=True),
                 r=['kinv_tm', 'v'], w=[('ps', 6)])
            S.op('dve', lambda: nc.vector.tensor_tensor(out=St[:], in0=St[:], in1=P(6, 256), op=ALU.add), r=['S', ('ps', 6)], w=['S'])
            S.op('dve', lambda: nc.vector.tensor_scalar_mul(St[:], St[:], e_fm[:, 127:128]), r=['S', 'e_fm'], w=['S'])
            S.op('act', lambda: nc.scalar.copy(Sbf[:], St[:]), r=['S'], w=['Sbf'])
            for c in range(2):
                S.op('act', lambda c=c: nc.scalar.activation(out=t_sq[:, c, :], in_=P(4 + c), func=AF.Square), r=[('ps', 4 + c)], w=[('t_sq', c)])
            for c in range(2):
                S.op('pe', lambda c=c: nc.tensor.matmul(P(7), lhsT=ones[:], rhs=t_sq[:, c, :], start=(c == 0), stop=(c == 1)),
                     r=['ones', ('t_sq', c)], w=[('ps', 7)])
            S.op('dve', lambda: nc.vector.tensor_scalar(out=t_r[:], in0=P(7), scalar1=1.0 / 256, scalar2=1e-6, op0=ALU.mult, op1=ALU.add),
                 r=[('ps', 7)], w=['t_r'])
            S.op('act', lambda: nc.scalar.activation(out=t_r[:], in_=t_r[:], func=AF.Sqrt), r=['t_r'], w=['t_r'])
            S.op('dve', lambda: nc.vector.reciprocal(t_r[:], t_r[:]), r=['t_r'], w=['t_r'])
            for c in range(2):
                S.op('act', lambda c=c, sl=sl: nc.scalar.activation(out=t_sg[:, c, :], in_=gT[:, c, sl], func=AF.Silu), r=['gT'], w=[('t_sg', c)])
                S.op('dve', lambda c=c: nc.vector.tensor_tensor(out=t_on[:, c, :], in0=P(4 + c), in1=t_r[:], op=ALU.mult),
                     r=[('ps', 4 + c), 't_r'], w=[('t_on', c)])
                S.op('dve', lambda c=c: nc.vector.scalar_tensor_tensor(out=t_on[:, c, :], in0=t_on[:, c, :], scalar=ng[:, c:c + 1], in1=t_sg[:, c, :],
                                                                      op0=ALU.mult, op1=ALU.mult), r=[('t_on', c), ('t_sg', c), 'ng'], w=[('t_on', c)])
                outs.append(S.op('sp', lambda c=c, sl=sl: nc.sync.dma_start(out=oT_d[c * 128:(c + 1) * 128, sl], in_=t_on[:, c, :]),
                                 r=[('t_on', c)], dma=True))
        S.emit(final_waits=outs[-8:])
    return nc


class Em:
    def __init__(self, nc, st, S):
        self.nc, self.st, self.S = nc, st, S

    def tile(self, name, shape, dt=F32):
        return _mk(("sb_" + name, shape, dt), self.nc, self.st)

    def load(self, out, in_, res, eng='sp'):
        nc = self.nc
        q = nc.sync if eng == 'sp' else nc.gpsimd
        return self.S.op(eng, lambda: q.dma_start(out=out, in_=in_), w=[res], dma=True)

    def store(self, out, in_, res, eng='sp'):
        nc = self.nc
        q = nc.sync if eng == 'sp' else nc.gpsimd
        return self.S.op(eng, lambda: q.dma_start(out=out, in_=in_), r=[res], dma=True)

    def tt(self, out, a, b, op, r, w, eng='dve'):
        e = self.nc.vector if eng == 'dve' else self.nc.gpsimd
        return self.S.op(eng, lambda: e.tensor_tensor(out=out, in0=a, in1=b, op=op), r=r, w=w)

    def ts(self, out, a, s1, op0, r, w, s2=None, op1=None):
        nc = self.nc
        if op1 is None:
            return self.S.op('dve', lambda: nc.vector.tensor_scalar(out=out, in0=a, scalar1=s1, scalar2=None, op0=op0), r=r, w=w)
        return self.S.op('dve', lambda: nc.vector.tensor_scalar(out=out, in0=a, scalar1=s1, scalar2=s2, op0=op0, op1=op1), r=r, w=w)

    def stt(self, out, a, sc, b, op0, op1, r, w):
        nc = self.nc
        return self.S.op('dve', lambda: nc.vector.scalar_tensor_tensor(out=out, in0=a, scalar=sc, in1=b, op0=op0, op1=op1), r=r, w=w)

    def act(self, out, a, func, r, w, scale=1.0):
        nc = self.nc
        return self.S.op('act', lambda: nc.scalar.activation(out=out, in_=a, func=func, scale=scale), r=r, w=w)

    def cp(self, out, a, r, w, eng='dve'):
        nc = self.nc
        if eng == 'act':
            return self.S.op('act', lambda: nc.scalar.copy(out, a), r=r, w=w)
        return self.S.op('dve', lambda: nc.vector.tensor_copy(out, a), r=r, w=w)

    def recip(self, out, a, r, w):
        nc = self.nc
        return self.S.op('dve', lambda: nc.vector.reciprocal(out, a), r=r, w=w)

    def memset(self, ap, v, w):
        nc = self.nc
        return self.S.op('dve', lambda: nc.vector.memset(ap, v), w=w)

    def mm(self, out, lhsT, rhs, r, w, start=True, stop=True, skip=False):
        nc = self.nc
        if skip:
            return self.S.op('pe', lambda: nc.tensor.matmul(out, lhsT=lhsT, rhs=rhs, start=start, stop=stop, skip_group_check=True), r=r, w=w)
        return self.S.op('pe', lambda: nc.tensor.matmul(out, lhsT=lhsT, rhs=rhs, start=start, stop=stop), r=r, w=w)


def s5_discretize(E, pfx, lr, li, lst, shape, want_f):
    import math
    mk = lambda n: E.tile(pfx + n, shape)
    step, x, mag, th, c, sn, t1, t2 = mk("step"), mk("x"), mk("mag"), mk("th"), mk("c"), mk("sn"), mk("t1"), mk("t2")
    ar, ai = mk("ar"), mk("ai")
    N = lambda n: pfx + n
    E.act(step[:], lst, AF.Exp, r=[N("lst")], w=[N("step")])
    E.tt(x[:], lr, step[:], ALU.mult, r=[N("lr"), N("step")], w=[N("x")])
    E.act(mag[:], x[:], AF.Exp, r=[N("x")], w=[N("mag")])
    E.tt(th[:], li, step[:], ALU.mult, r=[N("li"), N("step")], w=[N("th")])
    E.ts(th[:], th[:], 1.0 / 16, ALU.mult, r=[N("th")], w=[N("th")])
    E.act(sn[:], th[:], AF.Sin, r=[N("th")], w=[N("sn")])
    E.ts(th[:], th[:], math.pi / 2, ALU.add, r=[N("th")], w=[N("th")])
    E.act(c[:], th[:], AF.Sin, r=[N("th")], w=[N("c")])
    for _ in range(4):
        E.tt(t1[:], c[:], c[:], ALU.mult, r=[N("c")], w=[N("t1")])
        E.tt(t2[:], sn[:], sn[:], ALU.mult, r=[N("sn")], w=[N("t2")])
        E.stt(sn[:], c[:], 2.0, sn[:], ALU.mult, ALU.mult, r=[N("c"), N("sn")], w=[N("sn")])
        E.tt(c[:], t1[:], t2[:], ALU.subtract, r=[N("t1"), N("t2")], w=[N("c")])
    E.tt(ar[:], mag[:], c[:], ALU.mult, r=[N("mag"), N("c")], w=[N("ar")])
    E.tt(ai[:], mag[:], sn[:], ALU.mult, r=[N("mag"), N("sn")], w=[N("ai")])
    if not want_f:
        return ar, ai, None, None
    fr, fi, am1 = mk("fr"), mk("fi"), mk("am1")
    E.tt(t1[:], lr, lr, ALU.mult, r=[N("lr")], w=[N("t1")])
    E.tt(t2[:], li, li, ALU.mult, r=[N("li")], w=[N("t2")])
    E.tt(t1[:], t1[:], t2[:], ALU.add, r=[N("t1"), N("t2")], w=[N("t1")])
    E.recip(t1[:], t1[:], r=[N("t1")], w=[N("t1")])
    E.ts(am1[:], ar[:], -1.0, ALU.add, r=[N("ar")], w=[N("am1")])
    E.tt(fr[:], am1[:], lr, ALU.mult, r=[N("am1"), N("lr")], w=[N("fr")])
    E.tt(t2[:], ai[:], li, ALU.mult, r=[N("ai"), N("li")], w=[N("t2")])
    E.tt(fr[:], fr[:], t2[:], ALU.add, r=[N("fr"), N("t2")], w=[N("fr")])
    E.tt(fr[:], fr[:], t1[:], ALU.mult, r=[N("fr"), N("t1")], w=[N("fr")])
    E.tt(fi[:], ai[:], lr, ALU.mult, r=[N("ai"), N("lr")], w=[N("fi")])
    E.tt(t2[:], am1[:], li, ALU.mult, r=[N("am1"), N("li")], w=[N("t2")])
    E.tt(fi[:], fi[:], t2[:], ALU.subtract, r=[N("fi"), N("t2")], w=[N("fi")])
    E.tt(fi[:], fi[:], t1[:], ALU.mult, r=[N("fi"), N("t1")], w=[N("fi")])
    return ar, ai, fr, fi


def build_s5(L):
    import math
    nc = bass.Bass("TRN2", target_bir_lowering=False)
    di = lambda n, sh: nc.dram_tensor(n, sh, F32, kind="ExternalInput").ap()
    uT_d = di("uT", [256, L])
    lre2_d, lim2_d, lst2_d = di("lre2", [128, 16]), di("lim2", [128, 16]), di("lst2", [128, 16])
    sgn2_d, cri_d = di("sgn2", [128, 1]), di("cri", [128, 16, 16])
    lres_d, lims_d, lsts_d = di("lre_sel", [128, 2, 64]), di("lim_sel", [128, 2, 64]), di("lst_sel", [128, 2, 64])
    brs_d, bis_d = di("br_sel", [128, 2, 64]), di("bi_sel", [128, 2, 64])
    maskg_d, d2_d = di("maskg", [128, 8]), di("d2", [128, 2])
    ident_d, swap_d = di("ident", [128, 128]), di("swap", [128, 128])
    yT_d = nc.dram_tensor("yT", [256, L], F32, kind="ExternalOutput").ap()
    NL = int(math.log2(L))
    NB = L // 512
    with contextlib.ExitStack() as st:
        S = Sched(nc, st)
        E = Em(nc, st, S)
        u32, ubf = E.tile("u32", [128, 2, L]), E.tile("ubf", [128, 2, L], BF16)
        yacc = E.tile("yacc", [128, 2, L])
        sst = E.tile("sst", [128, L], BF16)
        LM = E.tile("LM", [128, 16, NL, 128], BF16)
        LB, LC = E.tile("LB", [128, 16, 128], BF16), E.tile("LC", [128, 16, 128], BF16)
        lre2, lim2, lst2 = E.tile("a_lr", [128, 16]), E.tile("a_li", [128, 16]), E.tile("a_lst", [128, 16])
        sgn2, cri = E.tile("sgn2", [128, 1]), E.tile("cri", [128, 16, 16])
        lres, lims, lsts = E.tile("b_lr", [128, 128]), E.tile("b_li", [128, 128]), E.tile("b_lst", [128, 128])
        brs, bis = E.tile("brs", [128, 128]), E.tile("bis", [128, 128])
        maskg, d2 = E.tile("maskg", [128, 8]), E.tile("d2", [128, 2])
        ident, swp = E.tile("ident", [128, 128]), E.tile("swap", [128, 128])
        ps = _mk(("ps", [128, 8, 512], F32), nc, st, psum=True)
        uv = uT_d.rearrange("(c p) l -> p c l", p=128)
        E.load(u32[:], uv, "u32")
        E.load(ubf[:], uv, "ubf", eng='pool')
        for t_, d_, n_ in [(lre2, lre2_d, "a_lr"), (lim2, lim2_d, "a_li"), (lst2, lst2_d, "a_lst"), (sgn2, sgn2_d, "sgn2"),
                           (maskg, maskg_d, "maskg"), (d2, d2_d, "d2"), (ident, ident_d, "ident"), (swp, swap_d, "swap")]:
            E.load(t_[:], d_, n_)
        E.load(cri[:], cri_d, "cri")
        for t_, d_, n_ in [(lres, lres_d, "b_lr"), (lims, lims_d, "b_li"), (lsts, lsts_d, "b_lst"), (brs, brs_d, "brs"), (bis, bis_d, "bis")]:
            E.load(t_[:], d_.rearrange("p c s -> p (c s)"), n_)
        ar, ai, _, _ = s5_discretize(E, "a_", lre2[:], lim2[:], lst2[:], [128, 16], False)
        AR, AI = E.tile("AR", [128, NL, 16]), E.tile("AI", [128, NL, 16])
        p1, p2 = E.tile("p1", [128, 16]), E.tile("p2", [128, 16])
        E.cp(AR[:, 0, :], ar[:], r=["a_ar"], w=["AR"])
        E.cp(AI[:, 0, :], ai[:], r=["a_ai"], w=["AI"])
        for l in range(1, NL):
            E.tt(p1[:], AR[:, l - 1, :], AR[:, l - 1, :], ALU.mult, r=["AR"], w=["p1"])
            E.tt(p2[:], AI[:, l - 1, :], AI[:, l - 1, :], ALU.mult, r=["AI"], w=["p2"])
            E.stt(AI[:, l, :], AR[:, l - 1, :], 2.0, AI[:, l - 1, :], ALU.mult, ALU.mult, r=["AR", "AI"], w=["AI"])
            E.tt(AR[:, l, :], p1[:], p2[:], ALU.subtract, r=["p1", "p2"], w=["AR"])
        E.ts(AI[:].rearrange("p a b -> p (a b)"), AI[:].rearrange("p a b -> p (a b)"), sgn2[:, 0:1], ALU.mult, r=["AI", "sgn2"], w=["AI"])
        tmpM = E.tile("tmpM", [128, 128])
        for g in range(16):
            for l in range(NL):
                E.ts(tmpM[:], ident[:], AR[:, l, g:g + 1], ALU.mult, r=["ident", "AR"], w=["tmpM"])
                E.stt(LM[:, g, l, :], swp[:], AI[:, l, g:g + 1], tmpM[:], ALU.mult, ALU.add, r=["swap", "AI", "tmpM"], w=[("LM", g)])
        _, _, fr, fi = s5_discretize(E, "b_", lres[:], lims[:], lsts[:], [128, 128], True)
        WB, q1 = E.tile("WB", [128, 2, 128]), E.tile("q1", [128, 128])
        frv, fiv = fr[:].rearrange("p (c s) -> p c s", c=2), fi[:].rearrange("p (c s) -> p c s", c=2)
        brv, biv = brs[:].rearrange("p (c s) -> p c s", c=2), bis[:].rearrange("p (c s) -> p c s", c=2)
        q1v = q1[:].rearrange("p (c s) -> p c s", c=2)
        E.tt(WB[:, :, 0:64], frv, brv, ALU.mult, r=["b_fr", "brs"], w=["WB"])
        E.tt(q1v, fiv, biv, ALU.mult, r=["b_fi", "bis"], w=["q1"])
        E.tt(WB[:, :, 0:64], WB[:, :, 0:64], q1v, ALU.subtract, r=["WB", "q1"], w=["WB"])
        E.tt(WB[:, :, 64:128], frv, biv, ALU.mult, r=["b_fr", "bis", "WB"], w=["WB"])
        E.tt(q1v, fiv, brv, ALU.mult, r=["b_fi", "brs", "WB"], w=["q1"])
        E.tt(WB[:, :, 64:128], WB[:, :, 64:128], q1v, ALU.add, r=["WB", "q1"], w=["WB"])
        for g in range(16):
            E.ts(LB[:, g, :], WB[:, g // 8, :], maskg[:, (g % 8):(g % 8) + 1], ALU.mult, r=["WB", "maskg"], w=[("LB", g)])
        E.memset(LC[:], 0.0, w=[("LC", g) for g in range(16)])
        for g in range(16):
            gp = g % 8
            E.ts(LC[:, g, 16 * gp:16 * gp + 16], cri[:, g, :], sgn2[:, 0:1], ALU.mult, r=["cri", "sgn2"], w=[("LC", g)])
        for c in range(2):
            E.ts(yacc[:, c, :], u32[:, c, :], d2[:, c:c + 1], ALU.mult, r=["u32", "d2"], w=[("yacc", c, b) for b in range(NB)])
        nb = 0
        for g in range(16):
            c = g // 8
            for b in range(NB):
                bank = nb % 8
                nb += 1
                E.mm(ps[:, bank, :], LB[:, g, :], ubf[:, c, b * 512:(b + 1) * 512], r=[("LB", g), "ubf"], w=[("ps", bank)])
                E.cp(sst[:, b * 512:(b + 1) * 512], ps[:, bank, :], r=[("ps", bank)], w=[("s", b)], eng='act')
            for l in range(NL):
                sh = 1 << l
                for b in range(NB - 1, -1, -1):
                    lo, hi = max(b * 512, sh), (b + 1) * 512
                    if lo >= hi:
                        continue
                    bank = nb % 8
                    nb += 1
                    rb = sorted(set([(lo - sh) // 512, (hi - sh - 1) // 512]))
                    E.mm(ps[:, bank, lo - b * 512:512], LM[:, g, l, :], sst[:, lo - sh:hi - sh],
                         r=[("LM", g)] + [("s", x) for x in rb], w=[("ps", bank)])
                    E.tt(sst[:, lo:hi], ps[:, bank, lo - b * 512:512], sst[:, lo:hi], ALU.add, r=[("ps", bank), ("s", b)], w=[("s", b)])
            for b in range(NB):
                bank = nb % 8
                nb += 1
                E.mm(ps[:, bank, :], LC[:, g, :], sst[:, b * 512:(b + 1) * 512], r=[("LC", g), ("s", b)], w=[("ps", bank)])
                E.tt(yacc[:, c, b * 512:(b + 1) * 512], ps[:, bank, :], yacc[:, c, b * 512:(b + 1) * 512], ALU.add,
                     r=[("ps", bank), ("yacc", c, b)], w=[("yacc", c, b)])
        g1, g2 = E.tile("g1", [128, 2, 512]), E.tile("g2", [128, 2, 512])
        outs = []
        k = 0
        for c in range(2):
            for b in range(NB):
                j = k % 2
                k += 1
                ysl = yacc[:, c, b * 512:(b + 1) * 512]
                E.tt(g1[:, j, :], ysl, ysl, ALU.mult, r=[("yacc", c, b)], w=[("g1", j)])
                E.ts(g1[:, j, :], g1[:, j, :], 0.044715, ALU.mult, r=[("g1", j)], w=[("g1", j)], s2=1.0, op1=ALU.add)
                E.tt(g1[:, j, :], g1[:, j, :], ysl, ALU.mult, r=[("g1", j), ("yacc", c, b)], w=[("g1", j)])
                E.act(g2[:, j, :], g1[:, j, :], AF.Sigmoid, r=[("g1", j)], w=[("g2", j)], scale=2.0 * math.sqrt(2.0 / math.pi))
                E.tt(g2[:, j, :], g2[:, j, :], ysl, ALU.mult, r=[("g2", j), ("yacc", c, b)], w=[("g2", j)])
                outs.append(E.store(yT_d[c * 128:(c + 1) * 128, b * 512:(b + 1) * 512], g2[:, j, :], ("g2", j)))
        S.emit(final_waits=outs[-8:])
    return nc


def s5_host_layout(lam_re, lam_im, b_re, b_im, c_re, c_im, d, log_step):
    r = np.arange(128)
    p = r % 64
    lre2 = np.ascontiguousarray(lam_re[:, p].T)
    lim2 = np.ascontiguousarray(lam_im[:, p].T)
    lst2 = np.ascontiguousarray(np.broadcast_to(log_step[None, :], (128, 16)))
    sgn2 = np.where(r < 64, 1.0, -1.0).astype(np.float32)[:, None]
    cri = np.concatenate([c_re.transpose(2, 0, 1), c_im.transpose(2, 0, 1)], axis=0)
    gsel = (np.arange(2)[None, :] * 8 + (r // 16)[:, None])
    lre_sel = lam_re[gsel]
    lim_sel = lam_im[gsel]
    lst_sel = np.broadcast_to(log_step[gsel][:, :, None], (128, 2, 64))
    br_sel = b_re[gsel, :, (r % 16)[:, None]]
    bi_sel = b_im[gsel, :, (r % 16)[:, None]]
    maskg = ((r // 16)[:, None] == np.arange(8)[None, :]).astype(np.float32)
    d2 = np.ascontiguousarray(d.reshape(2, 128).T)
    ident = np.eye(128, dtype=np.float32)
    swap = np.roll(ident, 64, axis=1)
    f = lambda a: np.ascontiguousarray(a, dtype=np.float32)
    return dict(lre2=f(lre2), lim2=f(lim2), lst2=f(lst2), sgn2=f(sgn2), cri=f(cri), lre_sel=f(lre_sel), lim_sel=f(lim_sel),
                lst_sel=f(lst_sel), br_sel=f(br_sel), bi_sel=f(bi_sel), maskg=f(maskg), d2=f(d2), ident=ident, swap=f(swap))


def build_gdn(L):
    nc = bass.Bass("TRN2", target_bir_lowering=False)
    di = lambda n, sh: nc.dram_tensor(n, sh, F32, kind="ExternalInput").ap()
    NT = L // 128
    NB = L // 512
    q_d, k_d, v_d, z_d = di("q_fm", [4, 128, L]), di("k_fm", [4, 128, L]), di("v_fm", [8, 128, L]), di("z_fm", [8, 128, L])
    cwq_d, cwk_d, cwv_d = di("cwq", [128, 4, 4]), di("cwk", [128, 4, 4]), di("cwv", [128, 8, 4])
    afm_d, bfm_d = di("a_fm", [8, L]), di("b_fm", [8, L])
    atm_d, btm_d = di("a_tm", [L, 8]), di("b_tm", [L, 8])
    alogb_d, dtbb_d = di("alog_b", [128, NT * 8]), di("dtb_b", [128, NT * 8])
    ng_d, sel_d, maskrep_d = di("ng", [128, 1]), di("sel", [8, 8, 128]), di("mask_rep", [128, L])
    ident_d, ones_d, tri_d, su_d, slm_d = di("ident", [128, 128]), di("ones", [128, 128]), di("tri", [128, 128]), di("msu", [128, 128]), di("msl", [128, 128])
    oT_d = nc.dram_tensor("oT", [1024, L], F32, kind="ExternalOutput").ap()
    with contextlib.ExitStack() as st:
        S = Sched(nc, st)
        E = Em(nc, st, S)
        T = E.tile
        bigA, bigB = T("bigA", [128, L]), T("bigB", [128, L])
        qn, kn, vs = T("qn", [128, L], BF16), T("kn", [128, L], BF16), T("vs", [128, L], BF16)
        gcrep, brep, mrep = T("gcrep", [128, L]), T("brep", [128, L]), T("mrep", [128, L])
        afm, bfm = T("afm", [8, L]), T("bfm", [8, L])
        atm, btm = T("atm", [128, NT * 8]), T("btm", [128, NT * 8])
        negA_b, dtb_b = T("negA_b", [128, NT * 8]), T("dtb_b", [128, NT * 8])
        gctm, gendtm, negegc, eend = T("gctm", [128, NT * 8]), T("gendtm", [128, NT * 8]), T("negegc", [128, NT * 8]), T("eend", [128, NT * 8])
        cwq, cwk, cwv = T("cwq", [128, 16]), T("cwk", [128, 16]), T("cwv", [128, 32])
        ng, sel = T("ng", [128, 1]), T("sel", [8, 8 * 128])
        ident, ones, tri, msu, msl = T("ident", [128, 128]), T("ones", [128, 128]), T("tri", [128, 128]), T("msu", [128, 128]), T("msl", [128, 128])
        identb = T("identb", [128, 128], BF16)
        St, Sbf = T("St", [128, 128]), T("Sbf", [128, 128], BF16)
        ps = _mk(("ps", [128, 7, 512], F32), nc, st, psum=True)
        psb = _mk(("psb", [128, 1024], BF16), nc, st, psum=True)
        for t_, d_, n_ in [(afm, afm_d, "afm"), (bfm, bfm_d, "bfm"), (negA_b, alogb_d, "negA_b"), (dtb_b, dtbb_d, "dtb_b"), (ng, ng_d, "ng"),
                           (mrep, maskrep_d, "mrep"), (ident, ident_d, "ident"), (ones, ones_d, "ones"), (tri, tri_d, "tri"),
                           (msu, su_d, "msu"), (msl, slm_d, "msl")]:
            E.load(t_[:], d_, n_)
        E.load(sel[:], sel_d.rearrange("k h m -> k (h m)"), "sel")
        E.load(cwq[:], cwq_d.rearrange("p h j -> p (h j)"), "cwq")
        E.load(cwk[:], cwk_d.rearrange("p h j -> p (h j)"), "cwk")
        E.load(cwv[:], cwv_d.rearrange("p h j -> p (h j)"), "cwv")
        E.load(atm[:].rearrange("p (n h) -> p n h", h=8), atm_d.rearrange("(n p) h -> p n h", p=128), "atm")
        E.load(btm[:].rearrange("p (n h) -> p n h", h=8), btm_d.rearrange("(n p) h -> p n h", p=128), "btm")
        E.cp(identb[:], ident[:], r=["ident"], w=["identb"])
        E.act(negA_b[:], negA_b[:], AF.Exp, r=["negA_b"], w=["negA_b"])
        E.ts(negA_b[:], negA_b[:], -1.0, ALU.mult, r=["negA_b"], w=["negA_b"])
        E.tt(atm[:], atm[:], dtb_b[:], ALU.add, r=["atm", "dtb_b"], w=["atm"])
        E.act(atm[:], atm[:], AF.Exp, r=["atm"], w=["atm"])
        E.ts(atm[:], atm[:], 1.0, ALU.add, r=["atm"], w=["atm"])
        E.act(atm[:], atm[:], AF.Ln, r=["atm"], w=["atm"])
        E.tt(atm[:], atm[:], negA_b[:], ALU.mult, r=["atm", "negA_b"], w=["atm"])
        E.act(btm[:], btm[:], AF.Sigmoid, r=["btm"], w=["btm"])
        for n in range(NT):
            E.mm(ps[:, 0, n * 8:(n + 1) * 8], tri[:], atm[:, n * 8:(n + 1) * 8], r=["tri", "atm"], w=[("ps", 0)], start=(n == 0), skip=True)
        for n in range(NT):
            E.mm(ps[:, 1, n * 8:(n + 1) * 8], msl[:], atm[:, n * 8:(n + 1) * 8], r=["msl", "atm"], w=[("ps", 1)], start=(n == 0), skip=True)
        E.cp(gctm[:], ps[:, 0, 0:NT * 8], r=[("ps", 0)], w=["gctm"])
        E.act(negegc[:], ps[:, 0, 0:NT * 8], AF.Exp, r=[("ps", 0)], w=["negegc"])
        E.ts(negegc[:], negegc[:], -1.0, ALU.mult, r=["negegc"], w=["negegc"])
        E.act(eend[:], ps[:, 1, 0:NT * 8], AF.Exp, r=[("ps", 1)], w=["eend"])

        def conv_silu(src_d, cw, h, dst_res):
            E.load(bigA[:], src_d, "bigA")
            E.ts(bigB[:], bigA[:], cw[:, 4 * h + 3:4 * h + 4], ALU.mult, r=["bigA", dst_res], w=["bigB"])
            for sft in (1, 2, 3):
                j = 3 - sft
                E.stt(bigB[:, sft:], bigA[:, 0:L - sft], cw[:, 4 * h + j:4 * h + j + 1], bigB[:, sft:], ALU.mult, ALU.add,
                      r=["bigA", "bigB", dst_res], w=["bigB"])
            E.act(bigB[:], bigB[:], AF.Silu, r=["bigB"], w=["bigB"])

        def l2norm_to(dst, dst_res, scale):
            E.tt(bigA[:], bigB[:], bigB[:], ALU.mult, r=["bigB"], w=["bigA"])
            for b in range(NB):
                bank = 2 + (b % 2)
                bs = slice(b * 512, (b + 1) * 512)
                E.mm(ps[:, bank, :], ones[:], bigA[:, bs], r=["ones", "bigA"], w=[("ps", bank)])
                E.ts(bigA[:, bs], ps[:, bank, :], 1e-6, ALU.add, r=[("ps", bank), "bigA"], w=["bigA"])
            E.act(bigA[:], bigA[:], AF.Sqrt, r=["bigA"], w=["bigA"])
            E.recip(bigA[:], bigA[:], r=["bigA"], w=["bigA"])
            E.stt(dst[:], bigB[:], scale, bigA[:], ALU.mult, ALU.mult, r=["bigA", "bigB"], w=[dst_res])

        outs = []
        zt = T("zt", [128, 2, 128])
        t_dd, t_dt, t_m = T("t_dd", [128, 128]), T("t_dt", [128, 128]), T("t_m", [128, 128])
        Xb, XTb, ATb, Rb = T("Xb", [128, 128], BF16), T("XTb", [128, 128], BF16), T("ATb", [128, 128], BF16), T("Rb", [128, 2, 128], BF16)
        Pb, PTb = T("Pb", [128, 2, 128], BF16), T("PTb", [128, 2, 128], BF16)
        vtm, kend, rtm, vnew, qdec = T("vtm", [128, 128]), T("kend", [128, 128], BF16), T("rtm", [128, 128], BF16), T("vnew", [128, 128], BF16), T("qdec", [128, 128], BF16)
        t_eg, t_sq, t_r, t_on, t_sz = T("t_eg", [128, 128]), T("t_sq", [128, 128]), T("t_r", [128, 128]), T("t_on", [128, 2, 128]), T("t_sz", [128, 128])
        nz = 0
        for hq in range(4):
            conv_silu(q_d[hq], cwq, hq, "qn")
            l2norm_to(qn, "qn", 128.0 ** -0.5)
            conv_silu(k_d[hq], cwk, hq, "kn")
            l2norm_to(kn, "kn", 1.0)
            for hv in (2 * hq, 2 * hq + 1):
                conv_silu(v_d[hv], cwv, hv, "vs")
                E.cp(vs[:], bigB[:], r=["bigB"], w=["vs"], eng='act')
                for b in range(NB):
                    bank = 2 + (b % 2)
                    bs = slice(b * 512, (b + 1) * 512)
                    E.mm(ps[:, bank, :], sel[:, hv * 128:(hv + 1) * 128], afm[:, bs], r=["sel", "afm"], w=[("ps", bank)])
                    E.ts(bigA[:, bs], ps[:, bank, :], dtb_b[:, hv:hv + 1], ALU.add, r=[("ps", bank), "dtb_b", "bigA"], w=["bigA"])
                E.act(bigA[:], bigA[:], AF.Exp, r=["bigA"], w=["bigA"])
                E.ts(bigA[:], bigA[:], 1.0, ALU.add, r=["bigA"], w=["bigA"])
                E.act(bigA[:], bigA[:], AF.Ln, r=["bigA"], w=["bigA"])
                E.ts(bigA[:], bigA[:], negA_b[:, hv:hv + 1], ALU.mult, r=["bigA", "negA_b"], w=["bigA"])
                S.op('dve', lambda: nc.vector.tensor_tensor_scan(out=gcrep[:], data0=mrep[:], data1=bigA[:], initial=0.0, op0=ALU.mult, op1=ALU.add),
                     r=["mrep", "bigA"], w=["gcrep"])
                for b in range(NB):
                    bank = 2 + (b % 2)
                    bs = slice(b * 512, (b + 1) * 512)
                    E.mm(ps[:, bank, :], sel[:, hv * 128:(hv + 1) * 128], bfm[:, bs], r=["sel", "bfm"], w=[("ps", bank)])
                    E.act(brep[:, bs], ps[:, bank, :], AF.Sigmoid, r=[("ps", bank)], w=["brep"])
                E.memset(St[:], 0.0, w=["St"])
                E.memset(Sbf[:], 0.0, w=["Sbf"])
                for n in range(NT):
                    sl = slice(n * 128, (n + 1) * 128)
                    col = slice(n * 8 + hv, n * 8 + hv + 1)
                    zb = nz % 2
                    nz += 1
                    E.load(zt[:, zb, :], z_d[hv, :, sl], ("zt", zb))
                    E.mm(ps[:, 0, 0:128], kn[:, sl], kn[:, sl], r=["kn"], w=[("ps", 0)])
                    E.mm(ps[:, 1, 0:128], kn[:, sl], qn[:, sl], r=["kn", "qn"], w=[("ps", 1)])
                    E.ts(t_dd[:], gcrep[:, sl], gctm[:, col], ALU.subtract, r=["gcrep", "gctm"], w=["t_dd"], s2=0.0, op1=ALU.min)
                    E.act(t_dt[:], t_dd[:], AF.Exp, r=["t_dd"], w=["t_dt"])
                    E.tt(t_m[:], t_dt[:], msu[:], ALU.mult, r=["t_dt", "msu"], w=["t_m"])
                    E.tt(t_m[:], t_m[:], brep[:, sl], ALU.mult, r=["t_m", "brep"], w=["t_m"])
                    E.tt(Xb[:], ps[:, 0, 0:128], t_m[:], ALU.mult, r=[("ps", 0), "t_m"], w=["Xb"])
                    E.tt(t_dt[:], t_dt[:], tri[:], ALU.mult, r=["t_dt", "tri"], w=["t_dt"])
                    E.tt(ATb[:], ps[:, 1, 0:128], t_dt[:], ALU.mult, r=[("ps", 1), "t_dt"], w=["ATb"])
                    S.op('pe', lambda: nc.tensor.transpose(psb[:, 0:128], Xb[:], identb[:]), r=["Xb", "identb"], w=["psb"])
                    S.op('pe', lambda sl=sl: nc.tensor.transpose(psb[:, 128:256], kn[:, sl], identb[:]), r=["kn", "identb"], w=["psb"])
                    S.op('pe', lambda sl=sl: nc.tensor.transpose(psb[:, 256:384], vs[:, sl], identb[:]), r=["vs", "identb"], w=["psb"])
                    E.cp(XTb[:], psb[:, 0:128], r=["psb"], w=["XTb"], eng='act')
                    E.ts(kend[:], psb[:, 128:256], eend[:, col], ALU.mult, r=["psb", "eend"], w=["kend"])
                    E.cp(vtm[:], psb[:, 256:384], r=["psb"], w=["vtm"], eng='act')
                    E.tt(Rb[:, 0, :], identb[:], Xb[:], ALU.subtract, r=["identb", "Xb"], w=[("Rb", 0)])
                    Pc, PTc = Xb[:], XTb[:]
                    Pres, PTres = "Xb", "XTb"
                    rc = 0
                    for lv in range(1, 7):
                        pb = lv % 2
                        if lv < 6:
                            E.mm(ps[:, 2, 0:128], PTc, Pc, r=[Pres, PTres], w=[("ps", 2)])
                        E.mm(ps[:, 3, 0:128], Pc, PTc, r=[Pres, PTres], w=[("ps", 3)])
                        if lv < 6:
                            E.cp(Pb[:, pb, :], ps[:, 2, 0:128], r=[("ps", 2)], w=[("Pb", pb)], eng='act')
                        E.cp(PTb[:, pb, :], ps[:, 3, 0:128], r=[("ps", 3)], w=[("PTb", pb)])
                        Pc, PTc = Pb[:, pb, :], PTb[:, pb, :]
                        Pres, PTres = ("Pb", pb), ("PTb", pb)
                        E.mm(ps[:, 4, 0:128], PTc, Rb[:, rc, :], r=[PTres, ("Rb", rc)], w=[("ps", 4)])
                        E.tt(Rb[:, 1 - rc, :], ps[:, 4, 0:128], Rb[:, rc, :], ALU.add, r=[("ps", 4), ("Rb", rc)], w=[("Rb", 1 - rc)])
                        rc = 1 - rc
                    E.mm(ps[:, 5, 0:128], kn[:, sl], Sbf[:], r=["kn", "Sbf"], w=[("ps", 5)])
                    E.stt(vtm[:], ps[:, 5, 0:128], negegc[:, col], vtm[:], ALU.mult, ALU.add, r=[("ps", 5), "negegc", "vtm"], w=["vtm"])
                    E.ts(rtm[:], vtm[:], btm[:, col], ALU.mult, r=["vtm", "btm"], w=["rtm"])
                    E.mm(ps[:, 5, 128:256], Rb[:, rc, :], rtm[:], r=[("Rb", rc), "rtm"], w=[("ps", 5)])
                    E.cp(vnew[:], ps[:, 5, 128:256], r=[("ps", 5)], w=["vnew"], eng='act')
                    E.act(t_eg[:], gcrep[:, sl], AF.Exp, r=["gcrep"], w=["t_eg"])
                    E.tt(qdec[:], qn[:, sl], t_eg[:], ALU.mult, r=["qn", "t_eg"], w=["qdec"])
                    E.mm(ps[:, 6, 0:128], Sbf[:], qdec[:], r=["Sbf", "qdec"], w=[("ps", 6)], start=True, stop=False)
                    E.mm(ps[:, 6, 0:128], vnew[:], ATb[:], r=["vnew", "ATb"], w=[("ps", 6)], start=False, stop=True)
                    E.mm(ps[:, 6, 128:256], kend[:], vnew[:], r=["kend", "vnew"], w=[("ps", 6)], start=False, skip=True)
                    E.stt(St[:], St[:], t_eg[:, 127:128], ps[:, 6, 128:256], ALU.mult, ALU.add, r=["St", "t_eg", ("ps", 6)], w=["St"])
                    E.cp(Sbf[:], St[:], r=["St"], w=["Sbf"], eng='act')
                    E.act(t_sq[:], ps[:, 6, 0:128], AF.Square, r=[("ps", 6)], w=["t_sq"])
                    E.mm(ps[:, 6, 256:384], ones[:], t_sq[:], r=["ones", "t_sq"], w=[("ps", 6)], start=False, skip=True)
                    E.ts(t_r[:], ps[:, 6, 256:384], 1.0 / 128, ALU.mult, r=[("ps", 6)], w=["t_r"], s2=1e-6, op1=ALU.add)
                    E.act(t_r[:], t_r[:], AF.Sqrt, r=["t_r"], w=["t_r"])
                    E.recip(t_r[:], t_r[:], r=["t_r"], w=["t_r"])
                    E.act(t_sz[:], zt[:, zb, :], AF.Silu, r=[("zt", zb)], w=["t_sz"])
                    E.tt(t_on[:, zb, :], ps[:, 6, 0:128], t_r[:], ALU.mult, r=[("ps", 6), "t_r"], w=[("t_on", zb)])
                    E.stt(t_on[:, zb, :], t_on[:, zb, :], ng[:, 0:1], t_sz[:], ALU.mult, ALU.mult, r=[("t_on", zb), "ng", "t_sz"], w=[("t_on", zb)])
                    outs.append(E.store(oT_d[hv * 128:(hv + 1) * 128, sl], t_on[:, zb, :], ("t_on", zb)))
        S.emit(final_waits=outs[-8:])
    return nc


def gdn_host_consts(L):
    NT = L // 128
    o = np.ones((128, 128), np.float32)
    sel = np.zeros((8, 8, 128), np.float32)
    for h in range(8):
        sel[h, h, :] = 1.0
    mask_rep = np.ones((128, L), np.float32)
    mask_rep[:, ::128] = 0.0
    return dict(sel=sel, mask_rep=mask_rep, ident=np.eye(128, dtype=np.float32), ones=o, tri=np.triu(o), msu=np.triu(o, 1), msl=np.tril(o, -1))


def build_glu(T):
    nc = bass.Bass("TRN2", target_bir_lowering=False)
    di = lambda n, sh: nc.dram_tensor(n, sh, F32, kind="ExternalInput").ap()
    yT_d, w_d, b_d = di("yT", [1024, T]), di("w", [1024, 1024]), di("b", [128, 8])
    oT_d = nc.dram_tensor("oT", [1024, T], F32, kind="ExternalOutput").ap()
    with contextlib.ExitStack() as st:
        S = Sched(nc, st)
        E = Em(nc, st, S)
        y32, ybf = E.tile("y32", [128, 8, T]), E.tile("ybf", [128, 8, T], BF16)
        wt, bt = E.tile("wt", [128, 8, 1024], BF16), E.tile("bt", [128, 8])
        sg, ot = E.tile("sg", [128, 2, 512]), E.tile("ot", [128, 2, 512])
        ps = _mk(("ps", [128, 8, 512], F32), nc, st, psum=True)
        yv = yT_d.rearrange("(k p) t -> p k t", p=128)
        E.load(y32[:], yv, "y32")
        E.load(ybf[:], yv, "ybf", eng='pool')
        E.load(wt[:], w_d.rearrange("(k p) n -> p k n", p=128), "wt", eng='pool')
        E.load(bt[:], b_d, "bt")
        outs = []
        i = 0
        for oc in range(8):
            for tb in range(T // 512):
                bank = i % 8
                j = i % 2
                i += 1
                tsl = slice(tb * 512, (tb + 1) * 512)
                for k in range(8):
                    E.mm(ps[:, bank, :], wt[:, k, oc * 128:(oc + 1) * 128], ybf[:, k, tsl], r=["wt", "ybf"], w=[("ps", bank)],
                         start=(k == 0), stop=(k == 7))
                S.op('act', lambda bank=bank, j=j, oc=oc: nc.scalar.activation(out=sg[:, j, :], in_=ps[:, bank, :], func=AF.Sigmoid,
                                                                             bias=bt[:, oc:oc + 1], scale=1.0),
                     r=[("ps", bank), "bt"], w=[("sg", j)])
                E.tt(ot[:, j, :], sg[:, j, :], y32[:, oc, tsl], ALU.mult, r=[("sg", j), "y32"], w=[("ot", j)])
                outs.append(E.store(oT_d[oc * 128:(oc + 1) * 128, tsl], ot[:, j, :], ("ot", j)))
        S.emit(final_waits=outs[-8:])
    return nc


def build_projln(T, KM):
    nc = bass.Bass("TRN2", target_bir_lowering=False)
    di = lambda n, sh: nc.dram_tensor(n, sh, F32, kind="ExternalInput").ap()
    x_d, cT_d, w_d = di("x_tm", [T, D]), di("cT", [KM, T]), di("w", [KM, D])
    lng_d, lnb_d = di("lng", [128, D]), di("lnb", [128, D])
    y_d = nc.dram_tensor("y", [T, D], F32, kind="ExternalOutput").ap()
    NT, KC = T // 128, KM // 128
    with contextlib.ExitStack() as st:
        S = Sched(nc, st)
        E = Em(nc, st, S)
        xa = E.tile("xa", [128, NT, D])
        cT = E.tile("cT", [128, KC, T], BF16)
        wb = E.tile("wb", [128, 2, KC, 256], BF16)
        gbc = E.tile("gbc", [128, 2, D])
        stats, mv = E.tile("stats", [128, 4, 6]), E.tile("mv", [128, 4])
        ps = _mk(("ps", [128, 8, 512], F32), nc, st, psum=True)
        xv = x_d.rearrange("(t p) d -> p t d", p=128)
        yv = y_d.rearrange("(t p) d -> p t d", p=128)
        for t in range(NT):
            E.load(xa[:, t, :], xv[:, t, :], ("xa", t))
            S.op('act', lambda t=t: nc.scalar.mul(xa[:, t, :], xa[:, t, :], ALPHA), r=[("xa", t)], w=[("xa", t)])
        E.load(cT[:], cT_d.rearrange("(k p) t -> p k t", p=128), "cT", eng='pool')
        E.load(gbc[:, 0, :], lng_d, ("gbc", 0))
        E.load(gbc[:, 1, :], lnb_d, ("gbc", 1))
        wv = w_d.rearrange("(k p) d -> p k d", p=128)
        for cb in range(D // 256):
            b = cb % 2
            E.load(wb[:, b, :, :], wv[:, :, cb * 256:(cb + 1) * 256], ("wb", b), eng='pool')
            half = (cb % 2) * 4
            for k in range(KC):
                for t in range(NT):
                    bank, off = half + t // 2, (t % 2) * 256
                    E.mm(ps[:, bank, off:off + 256], cT[:, k, t * 128:(t + 1) * 128], wb[:, b, k, :], r=["cT", ("wb", b)], w=[("ps", bank)],
                         start=(k == 0 and t % 2 == 0), stop=(k == KC - 1), skip=True)
            for t in range(NT):
                bank, off = half + t // 2, (t % 2) * 256
                csl = slice(cb * 256, (cb + 1) * 256)
                E.tt(xa[:, t, csl], ps[:, bank, off:off + 256], xa[:, t, csl], ALU.add, r=[("ps", bank), ("xa", t)], w=[("xa", t)])
        outs = []
        for t in range(NT):
            for c in range(4):
                S.op('dve', lambda t=t, c=c: nc.vector.bn_stats(out=stats[:, c, :], in_=xa[:, t, c * 512:(c + 1) * 512]),
                     r=[("xa", t)], w=["stats"])
            S.op('dve', lambda: nc.vector.bn_aggr(out=mv[:, 0:2], in_=stats[:].rearrange("p a b -> p (a b)")), r=["stats"], w=["mv"])
            E.ts(mv[:, 3:4], mv[:, 1:2], LN_EPS, ALU.add, r=["mv"], w=["mv"])
            E.act(mv[:, 3:4], mv[:, 3:4], AF.Sqrt, r=["mv"], w=["mv"])
            E.recip(mv[:, 2:3], mv[:, 3:4], r=["mv"], w=["mv"])
            E.ts(xa[:, t, :], xa[:, t, :], mv[:, 0:1], ALU.subtract, r=["mv", ("xa", t)], w=[("xa", t)], s2=mv[:, 2:3], op1=ALU.mult)
            E.tt(xa[:, t, :], xa[:, t, :], gbc[:, 0, :], ALU.mult, r=[("gbc", 0), ("xa", t)], w=[("xa", t)])
            E.tt(xa[:, t, :], xa[:, t, :], gbc[:, 1, :], ALU.add, r=[("gbc", 1), ("xa", t)], w=[("xa", t)])
            outs.append(E.store(yv[:, t, :], xa[:, t, :], ("xa", t)))
        S.emit(final_waits=outs)
    return nc


_PROGS = {}


def _prog(key, fn, *a):
    if key not in _PROGS:
        _PROGS[key] = fn(*a)
    return _PROGS[key]


def _run(nc, maps):
    res = run_bass_kernel_spmd(nc, maps, core_ids=list(range(NCORES)))
    return res.results


def _bc(v):
    return np.ascontiguousarray(np.broadcast_to(np.asarray(v, np.float32)[None, :], (128, v.shape[0])))


def _c(a):
    return np.ascontiguousarray(a, dtype=np.float32)


def kernel(**inp):
    TP = 1024
    L = 4096
    x = _c(inp['x']).reshape(8192, D)

    def ffn(xc, wg, wu, wd, g, b):
        nc = _prog(('ffn', TP), build_ffn, TP)
        maps = []
        for c in range(NCORES):
            xs = xc[c * TP:(c + 1) * TP]
            maps.append(dict(x_tm=_c(xs), x_fm=_c(xs.T), wg=_c(wg), wu=_c(wu), wd=_c(wd), lng=_bc(g), lnb=_bc(b)))
        return np.concatenate([r['y'] for r in _run(nc, maps)], 0)

    def projln(xc, cT_list, w, g, b):
        KM = w.shape[0]
        nc = _prog(('projln', TP, KM), build_projln, TP, KM)
        maps = []
        for c in range(NCORES):
            maps.append(dict(x_tm=_c(xc[c * TP:(c + 1) * TP]), cT=_c(cT_list[c]), w=_c(w), lng=_bc(g), lnb=_bc(b)))
        return np.concatenate([r['y'] for r in _run(nc, maps)], 0)

    ones = np.ones((128, 128), np.float32)
    x1 = ffn(x, inp['ffn_a_gate'][0], inp['ffn_a_up'][0], inp['ffn_a_down'][0], inp['ln_g'][0, 0], inp['ln_b'][0, 0])
    W0 = inp['ab_w_in'][0]
    maps = []
    for c in range(NCORES):
        b, h = c // 4, c % 4
        cols = np.concatenate([np.arange(h * 128, (h + 1) * 128), 512 + np.arange(h * 128, (h + 1) * 128),
                               1024 + np.arange(h * 256, (h + 1) * 256), 2048 + np.arange(h * 256, (h + 1) * 256),
                               3072 + np.arange(16), 3088 + np.arange(h * 256, (h + 1) * 256)])
        maps.append(dict(xT=_c(x1[b * L:(b + 1) * L].T), w=_c(W0[:, cols])))
    proj = [r['y'] for r in _run(_prog(('lin', L, D, 1040), build_linear, L, D, 1040), maps)]
    maps_g, maps_s = [], []
    for c in range(NCORES):
        h = c % 4
        p = proj[c]
        q, k, v, g, glr, u = p[:, 0:128], p[:, 128:256], p[:, 256:512], p[:, 512:768], p[:, 768:784], p[:, 784:1040]
        maps_g.append(dict(qT=_c(q.T), kT=_c(k.T), ktm=_c(k), vtm=_c(v), gT=_c(g.T), glrT=_c(glr.T),
                           wlr=_c(inp['gla_w_lr'][0][:, h * 128:(h + 1) * 128]), blr=_c(inp['gla_b_lr'][0][h * 128:(h + 1) * 128][None, :]),
                           ng=_c(np.asarray(inp['gla_norm_g'][0]).reshape(2, 128).T), triT=np.triu(ones), ones=ones))
        gs = slice(16 * h, 16 * h + 16)
        ms = s5_host_layout(np.asarray(inp['s5_lam_re'][0])[gs], np.asarray(inp['s5_lam_im'][0])[gs], np.asarray(inp['s5_b_re'][0])[gs],
                            np.asarray(inp['s5_b_im'][0])[gs], np.asarray(inp['s5_c_re'][0])[gs], np.asarray(inp['s5_c_im'][0])[gs],
                            np.asarray(inp['s5_d'][0])[h * 256:(h + 1) * 256], np.asarray(inp['s5_log_step'][0])[gs])
        ms['uT'] = _c(u.T)
        maps_s.append(ms)
    ogla = [r['oT'] for r in _run(_prog(('gla', L), build_gla, L), maps_g)]
    ys5 = [r['yT'] for r in _run(_prog(('s5', L), build_s5, L), maps_s)]
    maps = []
    for c in range(NCORES):
        b, qd = c // 4, c % 4
        tsl = slice(qd * TP, (qd + 1) * TP)
        yT = np.concatenate([ys5[b * 4 + h][:, tsl] for h in range(4)], 0)
        maps.append(dict(yT=_c(yT), w=_c(inp['s5_glu_w'][0]), b=_c(np.asarray(inp['s5_glu_b'][0]).reshape(8, 128).T)))
    os5 = [r['oT'] for r in _run(_prog(('glu', TP), build_glu, TP), maps)]
    cTs = []
    for c in range(NCORES):
        b, qd = c // 4, c % 4
        tsl = slice(qd * TP, (qd + 1) * TP)
        cTs.append(np.concatenate([ogla[b * 4 + h][:, tsl] for h in range(4)] + [os5[c]], 0))
    x2 = projln(x1, cTs, inp['ab_w_out'][0], inp['ln_g'][0, 1], inp['ln_b'][0, 1])
    x3 = ffn(x2, inp['ffn_b_gate'][0], inp['ffn_b_up'][0], inp['ffn_b_down'][0], inp['ln_g'][0, 2], inp['ln_b'][0, 2])
    x4 = ffn(x3, inp['ffn_a_gate'][1], inp['ffn_a_up'][1], inp['ffn_a_down'][1], inp['ln_g'][1, 0], inp['ln_b'][1, 0])
    W1 = inp['gdn_w_in'][0]
    maps = []
    for c in range(NCORES):
        b, hg = c // 4, c % 4
        cols = np.concatenate([np.arange(hg * 512, (hg + 1) * 512), 2048 + np.arange(hg * 512, (hg + 1) * 512),
                               4096 + np.arange(hg * 1024, (hg + 1) * 1024), 8192 + np.arange(hg * 1024, (hg + 1) * 1024),
                               12288 + np.arange(hg * 8, (hg + 1) * 8), 12320 + np.arange(hg * 8, (hg + 1) * 8)])
        maps.append(dict(xT=_c(x4[b * L:(b + 1) * L].T), w=_c(W1[:, cols])))
    proj = [r['y'] for r in _run(_prog(('lin', L, D, 3088), build_linear, L, D, 3088), maps)]
    NT = L // 128
    cw = np.asarray(inp['gdn_conv_w'][0])
    consts = gdn_host_consts(L)
    maps = []
    for c in range(NCORES):
        hg = c % 4
        p = proj[c]
        fm = lambda a, nh: _c(a.reshape(L, nh, 128).transpose(1, 2, 0))
        a_log = np.asarray(inp['gdn_a_log'][0])[hg * 8:(hg + 1) * 8]
        dtb = np.asarray(inp['gdn_dt_bias'][0])[hg * 8:(hg + 1) * 8]
        m = dict(q_fm=fm(p[:, 0:512], 4), k_fm=fm(p[:, 512:1024], 4), v_fm=fm(p[:, 1024:2048], 8), z_fm=fm(p[:, 2048:3072], 8),
                 cwq=_c(cw[:, hg * 512:(hg + 1) * 512].reshape(4, 4, 128).transpose(2, 1, 0)),
                 cwk=_c(cw[:, 2048 + hg * 512:2048 + (hg + 1) * 512].reshape(4, 4, 128).transpose(2, 1, 0)),
                 cwv=_c(cw[:, 4096 + hg * 1024:4096 + (hg + 1) * 1024].reshape(4, 8, 128).transpose(2, 1, 0)),
                 a_fm=_c(p[:, 3080:3088].T), b_fm=_c(p[:, 3072:3080].T), a_tm=_c(p[:, 3080:3088]), b_tm=_c(p[:, 3072:3080]),
                 alog_b=_c(np.broadcast_to(np.tile(a_log, NT)[None, :], (128, NT * 8))),
                 dtb_b=_c(np.broadcast_to(np.tile(dtb, NT)[None, :], (128, NT * 8))),
                 ng=_c(np.asarray(inp['gdn_norm_g'][0])[:, None]))
        m.update(consts)
        maps.append(m)
    ogdn = [r['oT'] for r in _run(_prog(('gdn', L), build_gdn, L), maps)]
    cTs = []
    for c in range(NCORES):
        b, qd = c // 4, c % 4
        tsl = slice(qd * TP, (qd + 1) * TP)
        cTs.append(np.concatenate([ogdn[b * 4 + hg][:, tsl] for hg in range(4)], 0))
    x5 = projln(x4, cTs, inp['gdn_w_out'][0], inp['ln_g'][1, 1], inp['ln_b'][1, 1])
    x6 = ffn(x5, inp['ffn_b_gate'][1], inp['ffn_b_up'][1], inp['ffn_b_down'][1], inp['ln_g'][1, 2], inp['ln_b'][1, 2])
    return x6.reshape(2, L, D).astype(np.float32)
```

```python
import contextlib
import numpy as np
import concourse.bass as bass
import concourse.mybir as mybir
from concourse.bass_utils import run_bass_kernel_spmd

F32 = mybir.dt.float32
BF16 = mybir.dt.bfloat16
AF = mybir.ActivationFunctionType
ALU = mybir.AluOpType

NCORES = 8
D = 2048
DFF = 5504
ALPHA = (2.0 * 2) ** 0.25
LN_EPS = 1e-5


SAME_ENGINE_INORDER = ('pe',)


class Sched:
    KDMA = 8

    def __init__(self, nc, st):
        self.nc = nc
        self.st = st
        self.E = dict(pe=nc.tensor, act=nc.scalar, dve=nc.vector, pool=nc.gpsimd, sp=nc.sync)
        self.esem = {e: st.enter_context(nc.semaphore("s_" + e)) for e in self.E}
        self.dsem = {e: [st.enter_context(nc.semaphore("d_%s%d" % (e, i))) for i in range(self.kd(e))]
                     for e in ('pool', 'sp')}
        self.cnt = {e: 0 for e in self.E}
        self.ndma = {e: 0 for e in self.E}
        self.dlist = {e: [] for e in self.E}
        self.known = {e: {} for e in self.E}
        self.stream = {e: [] for e in self.E}
        self.W = {}
        self.R = {}
        self.bar = []
        self.need_bar = set()
        self.gate_mod = 0
        self._gate = {}

    def gate_reg(self, e):
        if e not in self._gate:
            eng = self.E[e]
            self._gate[e] = eng.to_reg(eng.partition_id() % self.gate_mod)
        return self._gate[e]

    def kd(self, e):
        return 4 if e == 'pool' else self.KDMA

    def op(self, e, fn, r=(), w=(), dma=False):
        rec = dict(e=e, fn=fn, deps=[], dma=dma, sig=False)
        seen = set()
        isps = lambda x: x == 'psb' or (isinstance(x, tuple) and x[0] in ('ps', 'psb'))
        w = list(w) + [x for x in r if isps(x)]
        r = [x for x in r if not isps(x)]

        def need(p):
            if p is None or id(p) in seen:
                return
            if (not p['dma']) and p['e'] == e and e in SAME_ENGINE_INORDER:
                return
            seen.add(id(p))
            rec['deps'].append(p)
            p['sig'] = True

        if e in self.need_bar:
            for p in self.bar:
                need(p)
            self.need_bar.discard(e)
        for x in r:
            need(self.W.get(x))
        for x in w:
            need(self.W.get(x))
            for p in self.R.get(x, {}).values():
                need(p)
        if dma:
            rec['dn'] = self.ndma[e]
            self.ndma[e] += 1
            rec['sig'] = True
        self.stream[e].append(rec)
        for x in r:
            self.R.setdefault(x, {})[e + ('d' if dma else '')] = rec
        for x in w:
            self.W[x] = rec
            self.R[x] = {}
        return rec

    def flush(self, barrier=True):
        tails = []
        if barrier:
            for e in self.E:
                comp = [r_ for r_ in self.stream[e] if not r_['dma']]
                if comp:
                    comp[-1]['sig'] = True
                    tails.append(comp[-1])
        for e in self.E:
            for rec in self.stream[e]:
                if rec['dma']:
                    n = rec['dn']
                    kd = self.kd(e)
                    rec['sem'] = self.dsem[e][n % kd]
                    rec['val'] = 16 * (n // kd + 1)
                    self.dlist[e].append(rec)
                elif rec['sig']:
                    self.cnt[e] += 1
                    rec['sem'] = self.esem[e]
                    rec['val'] = self.cnt[e]
        for e in self.E:
            eng = self.E[e]
            known = self.known[e]
            if not self.stream[e]:
                continue
            with (eng.If_eq(self.gate_reg(e), 0) if self.gate_mod else contextlib.nullcontext()):
              for rec in self.stream[e]:
                  waits = [(p['sem'], p['val']) for p in rec['deps']]
                  if rec['dma'] and rec['dn'] >= self.kd(e):
                      p = self.dlist[e][rec['dn'] - self.kd(e)]
                      waits.append((p['sem'], p['val']))
                  for sem, val in waits:
                      k = id(sem)
                      if known.get(k, 0) >= val:
                          continue
                      known[k] = val
                      eng.wait_ge(sem, val)
                  if rec.get('self_inc'):
                      rec['fn'](rec['sem'])
                      continue
                  ins = rec['fn']()
                  if rec['dma']:
                      ins.then_inc(rec['sem'], 16)
                  elif rec['sig']:
                      ins.then_inc(rec['sem'], 1)
        if barrier:
            for e in ('pool', 'sp'):
                tails += self.dlist[e][-self.kd(e):]
            self.bar = tails
            self.need_bar = set(self.E)
            self.W = {}
            self.R = {}
        self.stream = {e: [] for e in self.E}

    def finish(self):
        self.flush(barrier=True)
        known = self.known['sp']
        with (self.nc.sync.If_eq(self.gate_reg('sp'), 0) if self.gate_mod else contextlib.nullcontext()):
            for p in self.bar:
                if known.get(id(p['sem']), 0) < p['val']:
                    known[id(p['sem'])] = p['val']
                    self.nc.sync.wait_ge(p['sem'], p['val'])
        if self.gate_mod:
            for e in self.E:
                self.E[e].nop()

    def emit(self, final_waits=()):
        self.finish()


_UID = [0]


def _mk(shape_name_dtype, nc, st, psum=False):
    name, shape, dt = shape_name_dtype
    _UID[0] += 1
    name = "%s_%d" % (name, _UID[0])
    if psum:
        return st.enter_context(nc.psum_tensor(name, shape, dt))
    return st.enter_context(nc.sbuf_tensor(name, shape, dt))


class Em:
    def __init__(self, nc, st, S):
        self.nc, self.st, self.S = nc, st, S

    def tile(self, name, shape, dt=F32):
        return _mk(("sb_" + name, shape, dt), self.nc, self.st)

    def load(self, out, in_, res, eng='sp'):
        nc = self.nc
        q = nc.sync if eng == 'sp' else nc.gpsimd
        return self.S.op(eng, lambda: q.dma_start(out=out, in_=in_), w=[res], dma=True)

    def store(self, out, in_, res, eng='sp'):
        nc = self.nc
        q = nc.sync if eng == 'sp' else nc.gpsimd
        return self.S.op(eng, lambda: q.dma_start(out=out, in_=in_), r=[res], dma=True)

    def tt(self, out, a, b, op, r, w, eng='dve'):
        e = self.nc.vector if eng == 'dve' else self.nc.gpsimd
        return self.S.op(eng, lambda: e.tensor_tensor(out=out, in0=a, in1=b, op=op), r=r, w=w)

    def ts(self, out, a, s1, op0, r, w, s2=None, op1=None):
        nc = self.nc
        if op1 is None:
            return self.S.op('dve', lambda: nc.vector.tensor_scalar(out=out, in0=a, scalar1=s1, scalar2=None, op0=op0), r=r, w=w)
        return self.S.op('dve', lambda: nc.vector.tensor_scalar(out=out, in0=a, scalar1=s1, scalar2=s2, op0=op0, op1=op1), r=r, w=w)

    def stt(self, out, a, sc, b, op0, op1, r, w):
        nc = self.nc
        return self.S.op('dve', lambda: nc.vector.scalar_tensor_tensor(out=out, in0=a, scalar=sc, in1=b, op0=op0, op1=op1), r=r, w=w)

    def act(self, out, a, func, r, w, scale=1.0):
        nc = self.nc
        return self.S.op('act', lambda: nc.scalar.activation(out=out, in_=a, func=func, scale=scale), r=r, w=w)

    def cp(self, out, a, r, w, eng='dve'):
        nc = self.nc
        if eng == 'act':
            return self.S.op('act', lambda: nc.scalar.copy(out, a), r=r, w=w)
        return self.S.op('dve', lambda: nc.vector.tensor_copy(out, a), r=r, w=w)

    def recip(self, out, a, r, w):
        nc = self.nc
        return self.S.op('dve', lambda: nc.vector.reciprocal(out, a), r=r, w=w)

    def memset(self, ap, v, w):
        nc = self.nc
        return self.S.op('dve', lambda: nc.vector.memset(ap, v), w=w)

    def mm(self, out, lhsT, rhs, r, w, start=True, stop=True, skip=False):
        nc = self.nc
        if skip:
            return self.S.op('pe', lambda: nc.tensor.matmul(out, lhsT=lhsT, rhs=rhs, start=start, stop=stop, skip_group_check=True), r=r, w=w)
        return self.S.op('pe', lambda: nc.tensor.matmul(out, lhsT=lhsT, rhs=rhs, start=start, stop=stop), r=r, w=w)


def s5_discretize(E, pfx, lr, li, lst, shape, want_f):
    import math
    mk = lambda n: E.tile(pfx + n, shape)
    step, x, mag, th, c, sn, t1, t2 = mk("step"), mk("x"), mk("mag"), mk("th"), mk("c"), mk("sn"), mk("t1"), mk("t2")
    ar, ai = mk("ar"), mk("ai")
    N = lambda n: pfx + n
    E.act(step[:], lst, AF.Exp, r=[N("lst")], w=[N("step")])
    E.tt(x[:], lr, step[:], ALU.mult, r=[N("lr"), N("step")], w=[N("x")])
    E.act(mag[:], x[:], AF.Exp, r=[N("x")], w=[N("mag")])
    E.tt(th[:], li, step[:], ALU.mult, r=[N("li"), N("step")], w=[N("th")])
    E.ts(th[:], th[:], 1.0 / 16, ALU.mult, r=[N("th")], w=[N("th")])
    E.act(sn[:], th[:], AF.Sin, r=[N("th")], w=[N("sn")])
    E.ts(th[:], th[:], math.pi / 2, ALU.add, r=[N("th")], w=[N("th")])
    E.act(c[:], th[:], AF.Sin, r=[N("th")], w=[N("c")])
    for _ in range(4):
        E.tt(t1[:], c[:], c[:], ALU.mult, r=[N("c")], w=[N("t1")])
        E.tt(t2[:], sn[:], sn[:], ALU.mult, r=[N("sn")], w=[N("t2")])
        E.stt(sn[:], c[:], 2.0, sn[:], ALU.mult, ALU.mult, r=[N("c"), N("sn")], w=[N("sn")])
        E.tt(c[:], t1[:], t2[:], ALU.subtract, r=[N("t1"), N("t2")], w=[N("c")])
    E.tt(ar[:], mag[:], c[:], ALU.mult, r=[N("mag"), N("c")], w=[N("ar")])
    E.tt(ai[:], mag[:], sn[:], ALU.mult, r=[N("mag"), N("sn")], w=[N("ai")])
    if not want_f:
        return ar, ai, None, None
    fr, fi, am1 = mk("fr"), mk("fi"), mk("am1")
    E.tt(t1[:], lr, lr, ALU.mult, r=[N("lr")], w=[N("t1")])
    E.tt(t2[:], li, li, ALU.mult, r=[N("li")], w=[N("t2")])
    E.tt(t1[:], t1[:], t2[:], ALU.add, r=[N("t1"), N("t2")], w=[N("t1")])
    E.recip(t1[:], t1[:], r=[N("t1")], w=[N("t1")])
    E.ts(am1[:], ar[:], -1.0, ALU.add, r=[N("ar")], w=[N("am1")])
    E.tt(fr[:], am1[:], lr, ALU.mult, r=[N("am1"), N("lr")], w=[N("fr")])
    E.tt(t2[:], ai[:], li, ALU.mult, r=[N("ai"), N("li")], w=[N("t2")])
    E.tt(fr[:], fr[:], t2[:], ALU.add, r=[N("fr"), N("t2")], w=[N("fr")])
    E.tt(fr[:], fr[:], t1[:], ALU.mult, r=[N("fr"), N("t1")], w=[N("fr")])
    E.tt(fi[:], ai[:], lr, ALU.mult, r=[N("ai"), N("lr")], w=[N("fi")])
    E.tt(t2[:], am1[:], li, ALU.mult, r=[N("am1"), N("li")], w=[N("t2")])
    E.tt(fi[:], fi[:], t2[:], ALU.subtract, r=[N("fi"), N("t2")], w=[N("fi")])
    E.tt(fi[:], fi[:], t1[:], ALU.mult, r=[N("fi"), N("t1")], w=[N("fi")])
    return ar, ai, fr, fi


def emit_gla(nc, S, L, io):
    qT_d, kT_d, ktm_d, vtm_d, gT_d, glr_d = io["qT"], io["kT"], io["ktm"], io["vtm"], io["gT"], io["glrT"]
    wlr_d, blr_d, ng_d, tri_d, ones_d, oT_d = io["wlr"], io["blr"], io["ng"], io["triT"], io["ones"], io["oT"]
    NT = L // 128
    with contextlib.ExitStack() as st:
        mk = lambda n, sh, dt=F32: _mk((n, sh, dt), nc, st)
        qT, kT = mk("qT_s", [128, L]), mk("kT_s", [128, L])
        ktm = mk("ktm_s", [128, NT, 128])
        vbf = mk("v_s", [128, NT, 256], BF16)
        gT = mk("gT_s", [128, 2, L])
        glr = mk("glr_s", [16, L])
        wlr, blr, ng = mk("wlr_s", [16, 128]), mk("blr_s", [1, 128]), mk("ng_s", [128, 2])
        tri, ones = mk("tri_s", [128, 128]), mk("ones_s", [128, 128])
        St, Sbf = mk("S_s", [128, 256]), mk("Sbf_s", [128, 256], BF16)
        t_e, t_sp = mk("t_e", [128, 128]), mk("t_sp", [128, 128])
        einv_tm, e_fm, einv_fm = mk("einv_tm", [128, 128]), mk("e_fm", [128, 128]), mk("einv_fm", [128, 128])
        kinv_tm, qdec, kinv_fm = mk("kinv_tm", [128, 128], BF16), mk("qdec", [128, 128], BF16), mk("kinv_fm", [128, 128], BF16)
        scm = mk("scm", [128, 128], BF16)
        t_sq, t_r, t_sg, t_on = mk("t_sq", [128, 2, 128]), mk("t_r", [128, 128]), mk("t_sg", [128, 2, 128]), mk("t_on", [128, 2, 128])
        ps = _mk(("ps", [128, 8, 512], F32), nc, st, psum=True)
        ld = lambda e, out, in_, res: S.op(e, lambda: (nc.sync if e == 'sp' else nc.gpsimd).dma_start(out=out, in_=in_), w=[res], dma=True)
        ld('sp', qT[:], qT_d, 'qT'); ld('sp', kT[:], kT_d, 'kT')
        ld('sp', ktm[:], ktm_d.rearrange("(n p) d -> p n d", p=128), 'ktm')
        ld('pool', vbf[:], vtm_d.rearrange("(n p) d -> p n d", p=128), 'v')
        ld('sp', gT[:], gT_d.rearrange("(c p) l -> p c l", p=128), 'gT')
        ld('sp', glr[:], glr_d, 'glr'); ld('sp', wlr[:], wlr_d, 'wlr'); ld('sp', blr[:], blr_d, 'blr')
        ld('sp', ng[:], ng_d, 'ng'); ld('sp', tri[:], tri_d, 'tri'); ld('sp', ones[:], ones_d, 'ones')
        S.op('dve', lambda: nc.vector.memset(St[:], 0.0), w=['S'])
        S.op('dve', lambda: nc.vector.memset(Sbf[:], 0.0), w=['Sbf'])
        outs = []
        P = lambda i, n=128: ps[:, i, 0:n]
        for n in range(NT):
            sl = slice(n * 128, (n + 1) * 128)
            S.op('pe', lambda sl=sl: nc.tensor.matmul(P(0), lhsT=glr[0:16, sl], rhs=wlr[0:16, :], start=True, stop=False),
                 r=['glr', 'wlr'], w=[('ps', 0)])
            S.op('pe', lambda: nc.tensor.matmul(P(0), lhsT=ones[0:1, :], rhs=blr[0:1, :], start=False, stop=True),
                 r=['ones', 'blr'], w=[('ps', 0)])
            S.op('act', lambda: nc.scalar.activation(out=t_e[:], in_=P(0), func=AF.Exp, scale=-1.0), r=[('ps', 0)], w=['t_e'])
            S.op('dve', lambda: nc.vector.tensor_scalar_add(t_e[:], t_e[:], 1.0), r=['t_e'], w=['t_e'])
            S.op('act', lambda: nc.scalar.activation(out=t_sp[:], in_=t_e[:], func=AF.Ln), r=['t_e'], w=['t_sp'])
            S.op('pe', lambda: nc.tensor.matmul(P(1), lhsT=tri[:], rhs=t_sp[:], start=True, stop=True), r=['tri', 't_sp'], w=[('ps', 1)])
            S.op('pe', lambda: nc.tensor.matmul(P(2), lhsT=t_sp[:], rhs=tri[:], start=True, stop=True), r=['tri', 't_sp'], w=[('ps', 2)])
            S.op('act', lambda: nc.scalar.activation(out=einv_tm[:], in_=P(1), func=AF.Exp, scale=1.0 / 16), r=[('ps', 1)], w=['einv_tm'])
            S.op('act', lambda: nc.scalar.activation(out=e_fm[:], in_=P(2), func=AF.Exp, scale=-1.0 / 16), r=[('ps', 2)], w=['e_fm'])
            S.op('act', lambda: nc.scalar.activation(out=einv_fm[:], in_=P(2), func=AF.Exp, scale=1.0 / 16), r=[('ps', 2)], w=['einv_fm'])
            S.op('dve', lambda n=n: nc.vector.tensor_tensor(out=kinv_tm[:], in0=ktm[:, n, :], in1=einv_tm[:], op=ALU.mult),
                 r=['ktm', 'einv_tm'], w=['kinv_tm'])
            S.op('dve', lambda sl=sl: nc.vector.scalar_tensor_tensor(out=qdec[:], in0=qT[:, sl], scalar=128.0 ** -0.5, in1=e_fm[:],
                                                                   op0=ALU.mult, op1=ALU.mult), r=['qT', 'e_fm'], w=['qdec'])
            S.op('dve', lambda sl=sl: nc.vector.tensor_tensor(out=kinv_fm[:], in0=kT[:, sl], in1=einv_fm[:], op=ALU.mult),
                 r=['kT', 'einv_fm'], w=['kinv_fm'])
            S.op('pe', lambda: nc.tensor.matmul(P(3), lhsT=kinv_fm[:], rhs=qdec[:], start=True, stop=True),
                 r=['kinv_fm', 'qdec'], w=[('ps', 3)])
            S.op('dve', lambda: nc.vector.tensor_tensor(out=scm[:], in0=P(3), in1=tri[:], op=ALU.mult), r=[('ps', 3), 'tri'], w=['scm'])
            for c in range(2):
                S.op('pe', lambda c=c, n=n: nc.tensor.matmul(P(4 + c), lhsT=vbf[:, n, c * 128:(c + 1) * 128], rhs=scm[:], start=True, stop=False),
                     r=['v', 'scm'], w=[('ps', 4 + c)])
                S.op('pe', lambda c=c: nc.tensor.matmul(P(4 + c), lhsT=Sbf[:, c * 128:(c + 1) * 128], rhs=qdec[:], start=False, stop=True),
                     r=['Sbf', 'qdec'], w=[('ps', 4 + c)])
            S.op('pe', lambda n=n: nc.tensor.matmul(P(6, 256), lhsT=kinv_tm[:], rhs=vbf[:, n, :], start=True, stop=True),
                 r=['kinv_tm', 'v'], w=[('ps', 6)])
            S.op('dve', lambda: nc.vector.tensor_tensor(out=St[:], in0=St[:], in1=P(6, 256), op=ALU.add), r=['S', ('ps', 6)], w=['S'])
            S.op('dve', lambda: nc.vector.tensor_scalar_mul(St[:], St[:], e_fm[:, 127:128]), r=['S', 'e_fm'], w=['S'])
            S.op('act', lambda: nc.scalar.copy(Sbf[:], St[:]), r=['S'], w=['Sbf'])
            for c in range(2):
                S.op('act', lambda c=c: nc.scalar.activation(out=t_sq[:, c, :], in_=P(4 + c), func=AF.Square), r=[('ps', 4 + c)], w=[('t_sq', c)])
            for c in range(2):
                S.op('pe', lambda c=c: nc.tensor.matmul(P(7), lhsT=ones[:], rhs=t_sq[:, c, :], start=(c == 0), stop=(c == 1)),
                     r=['ones', ('t_sq', c)], w=[('ps', 7)])
            S.op('dve', lambda: nc.vector.tensor_scalar(out=t_r[:], in0=P(7), scalar1=1.0 / 256, scalar2=1e-6, op0=ALU.mult, op1=ALU.add),
                 r=[('ps', 7)], w=['t_r'])
            S.op('act', lambda: nc.scalar.activation(out=t_r[:], in_=t_r[:], func=AF.Sqrt), r=['t_r'], w=['t_r'])
            S.op('dve', lambda: nc.vector.reciprocal(t_r[:], t_r[:]), r=['t_r'], w=['t_r'])
            for c in range(2):
                S.op('act', lambda c=c, sl=sl: nc.scalar.activation(out=t_sg[:, c, :], in_=gT[:, c, sl], func=AF.Silu), r=['gT'], w=[('t_sg', c)])
                S.op('dve', lambda c=c: nc.vector.tensor_tensor(out=t_on[:, c, :], in0=P(4 + c), in1=t_r[:], op=ALU.mult),
                     r=[('ps', 4 + c), 't_r'], w=[('t_on', c)])
                S.op('dve', lambda c=c: nc.vector.scalar_tensor_tensor(out=t_on[:, c, :], in0=t_on[:, c, :], scalar=ng[:, c:c + 1], in1=t_sg[:, c, :],
                                                                      op0=ALU.mult, op1=ALU.mult), r=[('t_on', c), ('t_sg', c), 'ng'], w=[('t_on', c)])
                outs.append(S.op('sp', lambda c=c, sl=sl: nc.sync.dma_start(out=oT_d[c * 128:(c + 1) * 128, sl], in_=t_on[:, c, :]),
                                 r=[('t_on', c)], dma=True))
        S.flush()


def emit_s5(nc, S, L, io):
    import math
    uT_d, yT_d = io["uT"], io["yT"]
    lre2_d, lim2_d, lst2_d, sgn2_d, cri_d = io["lre2"], io["lim2"], io["lst2"], io["sgn2"], io["cri"]
    lres_d, lims_d, lsts_d, brs_d, bis_d = io["lre_sel"], io["lim_sel"], io["lst_sel"], io["br_sel"], io["bi_sel"]
    maskg_d, d2_d, ident_d, swap_d = io["maskg"], io["d2"], io["ident"], io["swap"]
    NL = int(math.log2(L))
    NB = L // 512
    with contextlib.ExitStack() as st:
        E = Em(nc, st, S)
        u32, ubf = E.tile("u32", [128, 2, L]), E.tile("ubf", [128, 2, L], BF16)
        yacc = E.tile("yacc", [128, 2, L])
        sst = E.tile("sst", [128, L], BF16)
        LM = E.tile("LM", [128, 16, NL, 128], BF16)
        LB, LC = E.tile("LB", [128, 16, 128], BF16), E.tile("LC", [128, 16, 128], BF16)
        lre2, lim2, lst2 = E.tile("a_lr", [128, 16]), E.tile("a_li", [128, 16]), E.tile("a_lst", [128, 16])
        sgn2, cri = E.tile("sgn2", [128, 1]), E.tile("cri", [128, 16, 16])
        lres, lims, lsts = E.tile("b_lr", [128, 128]), E.tile("b_li", [128, 128]), E.tile("b_lst", [128, 128])
        brs, bis = E.tile("brs", [128, 128]), E.tile("bis", [128, 128])
        maskg, d2 = E.tile("maskg", [128, 8]), E.tile("d2", [128, 2])
        ident, swp = E.tile("ident", [128, 128]), E.tile("swap", [128, 128])
        ps = _mk(("ps", [128, 8, 512], F32), nc, st, psum=True)
        uv = uT_d.rearrange("(c p) l -> p c l", p=128)
        E.load(u32[:], uv, "u32")
        E.load(ubf[:], uv, "ubf", eng='pool')
        for t_, d_, n_ in [(lre2, lre2_d, "a_lr"), (lim2, lim2_d, "a_li"), (lst2, lst2_d, "a_lst"), (sgn2, sgn2_d, "sgn2"),
                           (maskg, maskg_d, "maskg"), (d2, d2_d, "d2"), (ident, ident_d, "ident"), (swp, swap_d, "swap")]:
            E.load(t_[:], d_, n_)
        E.load(cri[:], cri_d, "cri")
        for t_, d_, n_ in [(lres, lres_d, "b_lr"), (lims, lims_d, "b_li"), (lsts, lsts_d, "b_lst"), (brs, brs_d, "brs"), (bis, bis_d, "bis")]:
            E.load(t_[:], d_.rearrange("p c s -> p (c s)"), n_)
        ar, ai, _, _ = s5_discretize(E, "a_", lre2[:], lim2[:], lst2[:], [128, 16], False)
        AR, AI = E.tile("AR", [128, NL, 16]), E.tile("AI", [128, NL, 16])
        p1, p2 = E.tile("p1", [128, 16]), E.tile("p2", [128, 16])
        E.cp(AR[:, 0, :], ar[:], r=["a_ar"], w=["AR"])
        E.cp(AI[:, 0, :], ai[:], r=["a_ai"], w=["AI"])
        for l in range(1, NL):
            E.tt(p1[:], AR[:, l - 1, :], AR[:, l - 1, :], ALU.mult, r=["AR"], w=["p1"])
            E.tt(p2[:], AI[:, l - 1, :], AI[:, l - 1, :], ALU.mult, r=["AI"], w=["p2"])
            E.stt(AI[:, l, :], AR[:, l - 1, :], 2.0, AI[:, l - 1, :], ALU.mult, ALU.mult, r=["AR", "AI"], w=["AI"])
            E.tt(AR[:, l, :], p1[:], p2[:], ALU.subtract, r=["p1", "p2"], w=["AR"])
        E.ts(AI[:].rearrange("p a b -> p (a b)"), AI[:].rearrange("p a b -> p (a b)"), sgn2[:, 0:1], ALU.mult, r=["AI", "sgn2"], w=["AI"])
        tmpM = E.tile("tmpM", [128, 128])
        for g in range(16):
            for l in range(NL):
                E.ts(tmpM[:], ident[:], AR[:, l, g:g + 1], ALU.mult, r=["ident", "AR"], w=["tmpM"])
                E.stt(LM[:, g, l, :], swp[:], AI[:, l, g:g + 1], tmpM[:], ALU.mult, ALU.add, r=["swap", "AI", "tmpM"], w=[("LM", g)])
        _, _, fr, fi = s5_discretize(E, "b_", lres[:], lims[:], lsts[:], [128, 128], True)
        WB, q1 = E.tile("WB", [128, 2, 128]), E.tile("q1", [128, 128])
        frv, fiv = fr[:].rearrange("p (c s) -> p c s", c=2), fi[:].rearrange("p (c s) -> p c s", c=2)
        brv, biv = brs[:].rearrange("p (c s) -> p c s", c=2), bis[:].rearrange("p (c s) -> p c s", c=2)
        q1v = q1[:].rearrange("p (c s) -> p c s", c=2)
        E.tt(WB[:, :, 0:64], frv, brv, ALU.mult, r=["b_fr", "brs"], w=["WB"])
        E.tt(q1v, fiv, biv, ALU.mult, r=["b_fi", "bis"], w=["q1"])
        E.tt(WB[:, :, 0:64], WB[:, :, 0:64], q1v, ALU.subtract, r=["WB", "q1"], w=["WB"])
        E.tt(WB[:, :, 64:128], frv, biv, ALU.mult, r=["b_fr", "bis", "WB"], w=["WB"])
        E.tt(q1v, fiv, brv, ALU.mult, r=["b_fi", "brs", "WB"], w=["q1"])
        E.tt(WB[:, :, 64:128], WB[:, :, 64:128], q1v, ALU.add, r=["WB", "q1"], w=["WB"])
        for g in range(16):
            E.ts(LB[:, g, :], WB[:, g // 8, :], maskg[:, (g % 8):(g % 8) + 1], ALU.mult, r=["WB", "maskg"], w=[("LB", g)])
        E.memset(LC[:], 0.0, w=[("LC", g) for g in range(16)])
        for g in range(16):
            gp = g % 8
            E.ts(LC[:, g, 16 * gp:16 * gp + 16], cri[:, g, :], sgn2[:, 0:1], ALU.mult, r=["cri", "sgn2"], w=[("LC", g)])
        for c in range(2):
            E.ts(yacc[:, c, :], u32[:, c, :], d2[:, c:c + 1], ALU.mult, r=["u32", "d2"], w=[("yacc", c, b) for b in range(NB)])
        def group_steps(ch, g):
            c = g // 8
            sbuf = sst2[ch]
            st_ = []
            A = st_.append
            cnt = [0]

            def bank():
                cnt[0] += 1
                return 4 * ch + (cnt[0] % 4)
            for b in range(NB):
                bk = bank()
                A(lambda bk=bk, b=b: E.mm(ps[:, bk, :], LB[:, g, :], ubf[:, c, b * 512:(b + 1) * 512], r=[("LB", g), "ubf"], w=[("ps", bk)]))
                A(lambda bk=bk, b=b: E.cp(sbuf[:, b * 512:(b + 1) * 512], ps[:, bk, :], r=[("ps", bk)], w=[("s", ch, b)], eng='act'))
            for l in range(NL):
                sh = 1 << l
                for b in range(NB - 1, -1, -1):
                    lo, hi = max(b * 512, sh), (b + 1) * 512
                    if lo >= hi:
                        continue
                    bk = bank()
                    rb = sorted(set([(lo - sh) // 512, (hi - sh - 1) // 512]))
                    A(lambda bk=bk, b=b, l=l, lo=lo, hi=hi, sh=sh, rb=rb: E.mm(
                        ps[:, bk, lo - b * 512:512], LM[:, g, l, :], sbuf[:, lo - sh:hi - sh],
                        r=[("LM", g)] + [("s", ch, x) for x in rb], w=[("ps", bk)]))
                    A(lambda bk=bk, b=b, lo=lo, hi=hi: E.tt(sbuf[:, lo:hi], ps[:, bk, lo - b * 512:512], sbuf[:, lo:hi], ALU.add,
                                                             r=[("ps", bk), ("s", ch, b)], w=[("s", ch, b)]))
            for b in range(NB):
                bk = bank()
                A(lambda bk=bk, b=b: E.mm(ps[:, bk, :], LC[:, g, :], sbuf[:, b * 512:(b + 1) * 512], r=[("LC", g), ("s", ch, b)], w=[("ps", bk)]))
                A(lambda bk=bk, b=b: E.tt(yacc[:, c, b * 512:(b + 1) * 512], ps[:, bk, :], yacc[:, c, b * 512:(b + 1) * 512], ALU.add,
                                          r=[("ps", bk), ("yacc", c, b)], w=[("yacc", c, b)]))
            return st_

        sst2 = [sst, E.tile("sstB", [128, L], BF16)]
        for g in range(0, 16, 2):
            sa, sb = group_steps(0, g), group_steps(1, g + 1)
            for fa, fb in zip(sa, sb):
                fa()
                fb()
        g1, g2 = E.tile("g1", [128, 2, 512]), E.tile("g2", [128, 2, 512])
        outs = []
        k = 0
        for c in range(2):
            for b in range(NB):
                j = k % 2
                k += 1
                ysl = yacc[:, c, b * 512:(b + 1) * 512]
                E.tt(g1[:, j, :], ysl, ysl, ALU.mult, r=[("yacc", c, b)], w=[("g1", j)])
                E.ts(g1[:, j, :], g1[:, j, :], 0.044715, ALU.mult, r=[("g1", j)], w=[("g1", j)], s2=1.0, op1=ALU.add)
                E.tt(g1[:, j, :], g1[:, j, :], ysl, ALU.mult, r=[("g1", j), ("yacc", c, b)], w=[("g1", j)])
                E.act(g2[:, j, :], g1[:, j, :], AF.Sigmoid, r=[("g1", j)], w=[("g2", j)], scale=2.0 * math.sqrt(2.0 / math.pi))
                E.tt(g2[:, j, :], g2[:, j, :], ysl, ALU.mult, r=[("g2", j), ("yacc", c, b)], w=[("g2", j)])
                outs.append(E.store(yT_d[c * 128:(c + 1) * 128, b * 512:(b + 1) * 512], g2[:, j, :], ("g2", j)))
        S.flush()


def s5_host_layout(lam_re, lam_im, b_re, b_im, c_re, c_im, d, log_step):
    r = np.arange(128)
    p = r % 64
    lre2 = np.ascontiguousarray(lam_re[:, p].T)
    lim2 = np.ascontiguousarray(lam_im[:, p].T)
    lst2 = np.ascontiguousarray(np.broadcast_to(log_step[None, :], (128, 16)))
    sgn2 = np.where(r < 64, 1.0, -1.0).astype(np.float32)[:, None]
    cri = np.concatenate([c_re.transpose(2, 0, 1), c_im.transpose(2, 0, 1)], axis=0)
    gsel = (np.arange(2)[None, :] * 8 + (r // 16)[:, None])
    lre_sel = lam_re[gsel]
    lim_sel = lam_im[gsel]
    lst_sel = np.broadcast_to(log_step[gsel][:, :, None], (128, 2, 64))
    br_sel = b_re[gsel, :, (r % 16)[:, None]]
    bi_sel = b_im[gsel, :, (r % 16)[:, None]]
    maskg = ((r // 16)[:, None] == np.arange(8)[None, :]).astype(np.float32)
    d2 = np.ascontiguousarray(d.reshape(2, 128).T)
    ident = np.eye(128, dtype=np.float32)
    swap = np.roll(ident, 64, axis=1)
    f = lambda a: np.ascontiguousarray(a, dtype=np.float32)
    return dict(lre2=f(lre2), lim2=f(lim2), lst2=f(lst2), sgn2=f(sgn2), cri=f(cri), lre_sel=f(lre_sel), lim_sel=f(lim_sel),
                lst_sel=f(lst_sel), br_sel=f(br_sel), bi_sel=f(bi_sel), maskg=f(maskg), d2=f(d2), ident=ident, swap=f(swap))


def emit_gdn(nc, S, L, io):
    NT = L // 128
    NB = L // 512
    q_d, k_d, v_d, z_d = io["q_fm"], io["k_fm"], io["v_fm"], io["z_fm"]
    cwq_d, cwk_d, cwv_d = io["cwq"], io["cwk"], io["cwv"]
    afm_d, bfm_d, atm_d, btm_d = io["a_fm"], io["b_fm"], io["a_tm"], io["b_tm"]
    alogb_d, dtbb_d, ng_d, sel_d, maskrep_d = io["alog_b"], io["dtb_b"], io["ng"], io["sel"], io["mask_rep"]
    ident_d, ones_d, tri_d, su_d, slm_d, oT_d = io["ident"], io["ones"], io["tri"], io["msu"], io["msl"], io["oT"]
    with contextlib.ExitStack() as st:
        E = Em(nc, st, S)
        T = E.tile
        bigA, bigB = T("bigA", [128, L]), T("bigB", [128, L])
        qn, kn = T("qn", [128, L], BF16), T("kn", [128, L], BF16)
        vs2 = [T("vs%d" % c, [128, L], BF16) for c in range(2)]
        gcrep2, brep2 = [T("gcrep%d" % c, [128, L]) for c in range(2)], [T("brep%d" % c, [128, L]) for c in range(2)]
        mrep = T("mrep", [128, L])
        abfm = T("abfm", [40, L])
        afm, bfm = abfm[0:8, :], abfm[32:40, :]
        atm, btm = T("atm", [128, NT * 8]), T("btm", [128, NT * 8])
        negA_b, dtb_b = T("negA_b", [128, NT * 8]), T("dtb_b", [128, NT * 8])
        gctm, gendtm, negegc, eend = T("gctm", [128, NT * 8]), T("gendtm", [128, NT * 8]), T("negegc", [128, NT * 8]), T("eend", [128, NT * 8])
        cwq, cwk, cwv = T("cwq", [128, 16]), T("cwk", [128, 16]), T("cwv", [128, 32])
        ng, sel2 = T("ng", [128, 1]), T("sel", [40, 8 * 128])
        sel, selb = sel2[0:8, :], sel2[32:40, :]
        ident, ones, tri, msu, msl = T("ident", [128, 128]), T("ones", [128, 128]), T("tri", [128, 128]), T("msu", [128, 128]), T("msl", [128, 128])
        identb = T("identb", [128, 128], BF16)
        ps = _mk(("ps", [128, 6, 512], F32), nc, st, psum=True)
        psb2 = [_mk(("psb%d" % c, [128, 1024], BF16), nc, st, psum=True) for c in range(2)]
        E.load(afm, afm_d, "afm")
        E.load(bfm, bfm_d, "bfm")
        for t_, d_, n_ in [(negA_b, alogb_d, "negA_b"), (dtb_b, dtbb_d, "dtb_b"), (ng, ng_d, "ng"),
                           (mrep, maskrep_d, "mrep"), (ident, ident_d, "ident"), (ones, ones_d, "ones"), (tri, tri_d, "tri"),
                           (msu, su_d, "msu"), (msl, slm_d, "msl")]:
            E.load(t_[:], d_, n_)
        E.load(sel, sel_d.rearrange("k h m -> k (h m)"), "sel")
        E.load(selb, sel_d.rearrange("k h m -> k (h m)"), "selb")
        E.load(cwq[:], cwq_d.rearrange("p h j -> p (h j)"), "cwq")
        E.load(cwk[:], cwk_d.rearrange("p h j -> p (h j)"), "cwk")
        E.load(cwv[:], cwv_d.rearrange("p h j -> p (h j)"), "cwv")
        E.load(atm[:].rearrange("p (n h) -> p n h", h=8), atm_d.rearrange("(n p) h -> p n h", p=128), "atm")
        E.load(btm[:].rearrange("p (n h) -> p n h", h=8), btm_d.rearrange("(n p) h -> p n h", p=128), "btm")
        E.cp(identb[:], ident[:], r=["ident"], w=["identb"])
        E.act(negA_b[:], negA_b[:], AF.Exp, r=["negA_b"], w=["negA_b"])
        E.ts(negA_b[:], negA_b[:], -1.0, ALU.mult, r=["negA_b"], w=["negA_b"])
        E.tt(atm[:], atm[:], dtb_b[:], ALU.add, r=["atm", "dtb_b"], w=["atm"])
        E.act(atm[:], atm[:], AF.Exp, r=["atm"], w=["atm"])
        E.ts(atm[:], atm[:], 1.0, ALU.add, r=["atm"], w=["atm"])
        E.act(atm[:], atm[:], AF.Ln, r=["atm"], w=["atm"])
        E.tt(atm[:], atm[:], negA_b[:], ALU.mult, r=["atm", "negA_b"], w=["atm"])
        E.act(btm[:], btm[:], AF.Sigmoid, r=["btm"], w=["btm"])
        for n in range(NT):
            E.mm(ps[:, 0, n * 8:(n + 1) * 8], tri[:], atm[:, n * 8:(n + 1) * 8], r=["tri", "atm"], w=[("ps", 0)], start=(n == 0), skip=True)
        for n in range(NT):
            E.mm(ps[:, 1, n * 8:(n + 1) * 8], msl[:], atm[:, n * 8:(n + 1) * 8], r=["msl", "atm"], w=[("ps", 1)], start=(n == 0), skip=True)
        E.cp(gctm[:], ps[:, 0, 0:NT * 8], r=[("ps", 0)], w=["gctm"])
        E.act(negegc[:], ps[:, 0, 0:NT * 8], AF.Exp, r=[("ps", 0)], w=["negegc"])
        E.ts(negegc[:], negegc[:], -1.0, ALU.mult, r=["negegc"], w=["negegc"])
        E.act(eend[:], ps[:, 1, 0:NT * 8], AF.Exp, r=[("ps", 1)], w=["eend"])

        def conv_silu(src_d, cw, h, dst_res):
            E.load(bigA[:], src_d, "bigA")
            E.ts(bigB[:], bigA[:], cw[:, 4 * h + 3:4 * h + 4], ALU.mult, r=["bigA", dst_res], w=["bigB"])
            for sft in (1, 2, 3):
                j = 3 - sft
                E.stt(bigB[:, sft:], bigA[:, 0:L - sft], cw[:, 4 * h + j:4 * h + j + 1], bigB[:, sft:], ALU.mult, ALU.add,
                      r=["bigA", "bigB", dst_res], w=["bigB"])
            E.act(bigB[:], bigB[:], AF.Silu, r=["bigB"], w=["bigB"])

        def l2norm_to(dst, dst_res, scale):
            E.tt(bigA[:], bigB[:], bigB[:], ALU.mult, r=["bigB"], w=["bigA"])
            for b in range(NB):
                bank = 2 + (b % 2)
                bs = slice(b * 512, (b + 1) * 512)
                E.mm(ps[:, bank, :], ones[:], bigA[:, bs], r=["ones", "bigA"], w=[("ps", bank)])
                E.ts(bigA[:, bs], ps[:, bank, :], 1e-6, ALU.add, r=[("ps", bank), "bigA"], w=["bigA"])
            E.act(bigA[:], bigA[:], AF.Sqrt, r=["bigA"], w=["bigA"])
            E.recip(bigA[:], bigA[:], r=["bigA"], w=["bigA"])
            E.stt(dst[:], bigB[:], scale, bigA[:], ALU.mult, ALU.mult, r=["bigA", "bigB"], w=[dst_res])

        def gates(c, hv):
            gcrep, brep = gcrep2[c], brep2[c]
            for b in range(NB):
                bank = 2 + (b % 2)
                bs = slice(b * 512, (b + 1) * 512)
                E.mm(ps[:, bank, :], sel[:, hv * 128:(hv + 1) * 128], afm[:, bs], r=["sel", "afm"], w=[("ps", bank)])
                E.ts(bigA[:, bs], ps[:, bank, :], dtb_b[:, hv:hv + 1], ALU.add, r=[("ps", bank), "dtb_b", "bigA"], w=["bigA"])
            E.act(bigA[:], bigA[:], AF.Exp, r=["bigA"], w=["bigA"])
            E.ts(bigA[:], bigA[:], 1.0, ALU.add, r=["bigA"], w=["bigA"])
            E.act(bigA[:], bigA[:], AF.Ln, r=["bigA"], w=["bigA"])
            E.ts(bigA[:], bigA[:], negA_b[:, hv:hv + 1], ALU.mult, r=["bigA", "negA_b"], w=["bigA"])
            S.op('dve', lambda: nc.vector.tensor_tensor_scan(out=gcrep[:], data0=mrep[:], data1=bigA[:], initial=0.0, op0=ALU.mult, op1=ALU.add),
                 r=["mrep", "bigA"], w=[("gcrep", c)])
            for b in range(NB):
                bank = 2 + (b % 2)
                bs = slice(b * 512, (b + 1) * 512)
                E.mm(ps[:, bank, :], selb[:, hv * 128:(hv + 1) * 128], bfm[:, bs], r=["selb", "bfm"], w=[("ps", bank)])
                E.act(brep[:, bs], ps[:, bank, :], AF.Sigmoid, r=[("ps", bank)], w=[("brep", c)])

        mkc = lambda n, sh, dt=F32: [T("%s_%d" % (n, c), sh, dt) for c in range(2)]
        zt, t_on = mkc("zt", [128, 2, 128]), mkc("t_on", [128, 2, 128])
        t_dd, t_dt, t_m = mkc("t_dd", [128, 128]), mkc("t_dt", [128, 128]), mkc("t_m", [128, 128])
        Xb, XTb, ATb = mkc("Xb", [128, 128], BF16), mkc("XTb", [128, 128], BF16), mkc("ATb", [128, 128], BF16)
        Rb, Pb, PTb = mkc("Rb", [128, 2, 128], BF16), mkc("Pb", [128, 2, 128], BF16), mkc("PTb", [128, 2, 128], BF16)
        vtm, kend, rtm = mkc("vtm", [128, 128]), mkc("kend", [128, 128], BF16), mkc("rtm", [128, 128], BF16)
        vnew, qdec = mkc("vnew", [128, 128], BF16), mkc("qdec", [128, 128], BF16)
        t_eg, t_sq, t_r, t_sz = mkc("t_eg", [128, 128]), mkc("t_sq", [128, 128]), mkc("t_r", [128, 128]), mkc("t_sz", [128, 128])
        St2, Sbf2 = mkc("St", [128, 128]), mkc("Sbf", [128, 128], BF16)

        def tile_steps(c, hv, n):
            sl = slice(n * 128, (n + 1) * 128)
            col = slice(n * 8 + hv, n * 8 + hv + 1)
            zb = n % 2
            vs, gcrep, brep, psb, St, Sbf = vs2[c], gcrep2[c], brep2[c], psb2[c], St2[c], Sbf2[c]
            B0, B1, B2 = 3 * c, 3 * c + 1, 3 * c + 2
            R = lambda x: (x, c)
            PB = ("psb", c)
            KK, QK, KS, VN = ps[:, B0, 0:128], ps[:, B0, 128:256], ps[:, B0, 256:384], ps[:, B0, 384:512]
            PP, PT, PR = ps[:, B1, 0:128], ps[:, B1, 128:256], ps[:, B1, 256:384]
            PO, PS_, SS = ps[:, B2, 0:128], ps[:, B2, 128:256], ps[:, B2, 256:384]
            st_ = []
            A = st_.append
            A(lambda: S.op('sp', lambda: nc.sync.dma_start(out=zt[c][:, zb, :], in_=z_d[hv, :, sl]), r=[("zdram", c)], w=[R(("zt", zb))], dma=True))
            A(lambda: E.mm(KK, kn[:, sl], kn[:, sl], r=["kn"], w=[("ps", B0)]))
            A(lambda: E.mm(QK, kn[:, sl], qn[:, sl], r=["kn", "qn"], w=[("ps", B0)], start=False, skip=True))
            A(lambda: E.ts(t_dd[c][:], gcrep[:, sl], gctm[:, col], ALU.subtract, r=[("gcrep", c), "gctm"], w=[R("t_dd")], s2=0.0, op1=ALU.min))
            A(lambda: E.act(t_dt[c][:], t_dd[c][:], AF.Exp, r=[R("t_dd")], w=[R("t_dt")]))
            A(lambda: E.tt(t_m[c][:], t_dt[c][:], msu[:], ALU.mult, r=[R("t_dt"), "msu"], w=[R("t_m")]))
            A(lambda: E.tt(t_m[c][:], t_m[c][:], brep[:, sl], ALU.mult, r=[R("t_m"), ("brep", c)], w=[R("t_m")]))
            A(lambda: E.tt(Xb[c][:], KK, t_m[c][:], ALU.mult, r=[("ps", B0), R("t_m")], w=[R("Xb")]))
            A(lambda: E.tt(t_dt[c][:], t_dt[c][:], tri[:], ALU.mult, r=[R("t_dt"), "tri"], w=[R("t_dt")]))
            A(lambda: E.tt(ATb[c][:], QK, t_dt[c][:], ALU.mult, r=[("ps", B0), R("t_dt")], w=[R("ATb")]))
            A(lambda: S.op('pe', lambda: nc.tensor.transpose(psb[:, 0:128], Xb[c][:], identb[:]), r=[R("Xb"), "identb"], w=[PB]))
            A(lambda: S.op('pe', lambda: nc.tensor.transpose(psb[:, 128:256], kn[:, sl], identb[:]), r=["kn", "identb"], w=[PB]))
            A(lambda: S.op('pe', lambda: nc.tensor.transpose(psb[:, 256:384], vs[:, sl], identb[:]), r=[("vs", c), "identb"], w=[PB]))
            A(lambda: E.cp(XTb[c][:], psb[:, 0:128], r=[PB], w=[R("XTb")], eng='act'))
            A(lambda: E.ts(kend[c][:], psb[:, 128:256], eend[:, col], ALU.mult, r=[PB, "eend"], w=[R("kend")]))
            A(lambda: E.cp(vtm[c][:], psb[:, 256:384], r=[PB], w=[R("vtm")], eng='act'))
            A(lambda: E.tt(Rb[c][:, 0, :], identb[:], Xb[c][:], ALU.subtract, r=["identb", R("Xb")], w=[R(("Rb", 0))]))
            cur = dict(P=Xb[c][:], PT=XTb[c][:], Pr=R("Xb"), PTr=R("XTb"), rc=0)
            for lv in range(1, 7):
                pb = lv % 2
                Pc, PTc, Pres, PTres, rc = cur["P"], cur["PT"], cur["Pr"], cur["PTr"], cur["rc"]
                if lv < 6:
                    A(lambda Pc=Pc, PTc=PTc, Pres=Pres, PTres=PTres: E.mm(PP, PTc, Pc, r=[Pres, PTres], w=[("ps", B1)]))
                A(lambda Pc=Pc, PTc=PTc, Pres=Pres, PTres=PTres: E.mm(PT, Pc, PTc, r=[Pres, PTres], w=[("ps", B1)], start=False, skip=True))
                if lv < 6:
                    A(lambda pb=pb: E.cp(Pb[c][:, pb, :], PP, r=[("ps", B1)], w=[R(("Pb", pb))], eng='act'))
                A(lambda pb=pb: E.cp(PTb[c][:, pb, :], PT, r=[("ps", B1)], w=[R(("PTb", pb))]))
                nP, nPT, nPr, nPTr = Pb[c][:, pb, :], PTb[c][:, pb, :], R(("Pb", pb)), R(("PTb", pb))
                A(lambda nPT=nPT, nPTr=nPTr, rc=rc: E.mm(PR, nPT, Rb[c][:, rc, :], r=[nPTr, R(("Rb", rc))], w=[("ps", B1)], start=False, skip=True))
                A(lambda rc=rc: E.tt(Rb[c][:, 1 - rc, :], PR, Rb[c][:, rc, :], ALU.add, r=[("ps", B1), R(("Rb", rc))], w=[R(("Rb", 1 - rc))]))
                cur = dict(P=nP, PT=nPT, Pr=nPr, PTr=nPTr, rc=1 - rc)
            rc = cur["rc"]
            A(lambda: E.mm(KS, kn[:, sl], Sbf[:], r=["kn", R("Sbf")], w=[("ps", B0)], start=False, skip=True))
            A(lambda: E.stt(vtm[c][:], KS, negegc[:, col], vtm[c][:], ALU.mult, ALU.add, r=[("ps", B0), "negegc", R("vtm")], w=[R("vtm")]))
            A(lambda: E.ts(rtm[c][:], vtm[c][:], btm[:, col], ALU.mult, r=[R("vtm"), "btm"], w=[R("rtm")]))
            A(lambda: E.mm(VN, Rb[c][:, rc, :], rtm[c][:], r=[R(("Rb", rc)), R("rtm")], w=[("ps", B0)], start=False, skip=True))
            A(lambda: E.cp(vnew[c][:], VN, r=[("ps", B0)], w=[R("vnew")], eng='act'))
            A(lambda: E.act(t_eg[c][:], gcrep[:, sl], AF.Exp, r=[("gcrep", c)], w=[R("t_eg")]))
            A(lambda: E.tt(qdec[c][:], qn[:, sl], t_eg[c][:], ALU.mult, r=["qn", R("t_eg")], w=[R("qdec")]))
            A(lambda: E.mm(PO, Sbf[:], qdec[c][:], r=[R("Sbf"), R("qdec")], w=[("ps", B2)], start=True, stop=False))
            A(lambda: E.mm(PO, vnew[c][:], ATb[c][:], r=[R("vnew"), R("ATb")], w=[("ps", B2)], start=False, stop=True))
            A(lambda: E.mm(PS_, kend[c][:], vnew[c][:], r=[R("kend"), R("vnew")], w=[("ps", B2)], start=False, skip=True))
            A(lambda: E.stt(St[:], St[:], t_eg[c][:, 127:128], PS_, ALU.mult, ALU.add, r=[R("St"), R("t_eg"), ("ps", B2)], w=[R("St")]))
            A(lambda: E.cp(Sbf[:], St[:], r=[R("St")], w=[R("Sbf")], eng='act'))
            A(lambda: E.act(t_sq[c][:], PO, AF.Square, r=[("ps", B2)], w=[R("t_sq")]))
            A(lambda: E.mm(SS, ones[:], t_sq[c][:], r=["ones", R("t_sq")], w=[("ps", B2)], start=False, skip=True))
            A(lambda: E.ts(t_r[c][:], SS, 1.0 / 128, ALU.mult, r=[("ps", B2)], w=[R("t_r")], s2=1e-6, op1=ALU.add))
            A(lambda: E.act(t_r[c][:], t_r[c][:], AF.Ln, r=[R("t_r")], w=[R("t_r")]))
            A(lambda: E.act(t_r[c][:], t_r[c][:], AF.Exp, r=[R("t_r")], w=[R("t_r")], scale=-0.5))
            A(lambda: E.tt(t_on[c][:, zb, :], PO, t_r[c][:], ALU.mult, r=[("ps", B2), R("t_r")], w=[R(("t_on", zb))]))
            A(lambda: E.stt(t_on[c][:, zb, :], t_on[c][:, zb, :], ng[:, 0:1], zt[c][:, zb, :], ALU.mult, ALU.mult,
                            r=[R(("t_on", zb)), "ng", R(("zt", zb))], w=[R(("t_on", zb))]))
            A(lambda: E.store(oT_d[hv * 128:(hv + 1) * 128, sl], t_on[c][:, zb, :], R(("t_on", zb))))
            return st_

        for hq in range(4):
            conv_silu(q_d[hq], cwq, hq, "qn")
            l2norm_to(qn, "qn", 128.0 ** -0.5)
            conv_silu(k_d[hq], cwk, hq, "kn")
            l2norm_to(kn, "kn", 1.0)
            for c in range(2):
                hv = 2 * hq + c
                conv_silu(v_d[hv], cwv, hv, ("vs", c))
                E.cp(vs2[c][:], bigB[:], r=["bigB"], w=[("vs", c)], eng='act')
                gates(c, hv)
                E.load(bigA[:], z_d[hv], "bigA")
                E.act(bigA[:], bigA[:], AF.Silu, r=["bigA"], w=["bigA"])
                S.op('sp', lambda hv=hv: nc.sync.dma_start(out=z_d[hv], in_=bigA[:]), r=["bigA"], w=[("zdram", c)], dma=True)
                E.memset(St2[c][:], 0.0, w=[("St", c)])
                E.memset(Sbf2[c][:], 0.0, w=[("Sbf", c)])
            for n in range(NT):
                sa, sb = tile_steps(0, 2 * hq, n), tile_steps(1, 2 * hq + 1, n)
                for fa, fb in zip(sa, sb):
                    fa()
                    fb()
        S.flush()


def gdn_host_consts(L):
    NT = L // 128
    o = np.ones((128, 128), np.float32)
    sel = np.zeros((8, 8, 128), np.float32)
    for h in range(8):
        sel[h, h, :] = 1.0
    mask_rep = np.ones((128, L), np.float32)
    mask_rep[:, ::128] = 0.0
    return dict(sel=sel, mask_rep=mask_rep, ident=np.eye(128, dtype=np.float32), ones=o, tri=np.triu(o), msu=np.triu(o, 1), msl=np.tril(o, -1))


def _ln_tile(nc, S, E, xa, t, stats, mv, gbc):
    for c in range(4):
        S.op('dve', lambda c=c: nc.vector.bn_stats(out=stats[:, c, :], in_=xa[:, t, c * 512:(c + 1) * 512]),
             r=[("xa", t)], w=["stats"])
    S.op('dve', lambda: nc.vector.bn_aggr(out=mv[:, 0:2], in_=stats[:].rearrange("p a b -> p (a b)")), r=["stats"], w=["mv"])
    E.ts(mv[:, 3:4], mv[:, 1:2], LN_EPS, ALU.add, r=["mv"], w=["mv"])
    E.act(mv[:, 3:4], mv[:, 3:4], AF.Sqrt, r=["mv"], w=["mv"])
    E.recip(mv[:, 2:3], mv[:, 3:4], r=["mv"], w=["mv"])
    E.ts(xa[:, t, :], xa[:, t, :], mv[:, 0:1], ALU.subtract, r=["mv", ("xa", t)], w=[("xa", t)], s2=mv[:, 2:3], op1=ALU.mult)
    E.tt(xa[:, t, :], xa[:, t, :], gbc[:, 0, :], ALU.mult, r=[("gbc", 0), ("xa", t)], w=[("xa", t)])
    E.tt(xa[:, t, :], xa[:, t, :], gbc[:, 1, :], ALU.add, r=[("gbc", 1), ("xa", t)], w=[("xa", t)])


def _transpose_tile(nc, S, E, xa, t, ps, ident, xTo, xTo_res):
    for j in range(4):
        for kk in range(4):
            k = 4 * j + kk
            S.op('pe', lambda j=j, kk=kk, k=k: nc.tensor.transpose(ps[:, j, kk * 128:(kk + 1) * 128], xa[:, t, k * 128:(k + 1) * 128], ident[:]),
                 r=[("xa", t), "identf"], w=[("ps", j)])
        src = ps[:, j, :].rearrange("p (a b) -> p a b", a=4)
        dst = xTo[:, 4 * j:4 * j + 4, t * 128:(t + 1) * 128]
        E.cp(dst, src, r=[("ps", j)], w=[xTo_res(j)], eng=('act' if j % 2 == 0 else 'dve'))


def emit_ffn(nc, S, blocks, wg, wu, wd, lng, lnb, ident_d):
    T = 1024
    NT, NTB, KC = T // 128, T // 512, D // 128
    parts = [list(range(0, 15)), list(range(15, 29)), list(range(29, 43))]
    PF = 15
    with contextlib.ExitStack() as st:
        E = Em(nc, st, S)
        xa = E.tile("xa", [128, NT, D])
        xT = E.tile("xT", [128, KC, T], BF16)
        hT = E.tile("hT", [128, PF, T], BF16)
        wgu = E.tile("wgu", [128, 2, 2, KC, 256], BF16)
        wdb = E.tile("wdb", [128, 2, PF, 256], BF16)
        sg = E.tile("sg", [128, 2, 512])
        stats, mv = E.tile("stats", [128, 4, 6]), E.tile("mv", [128, 4])
        gbc = E.tile("gbc", [128, 2, D])
        ident = E.tile("identf", [128, 128])
        ps = _mk(("ps", [128, 8, 512], F32), nc, st, psum=True)
        xTo = wgu[:].rearrange("p a b c d -> p (a b c d)").rearrange("p (k t) -> p k t", k=KC)
        wg_v = wg.rearrange("(k p) f -> p k f", p=128)
        wu_v = wu.rearrange("(k p) f -> p k f", p=128)
        wd_v = wd.rearrange("(f p) d -> p f d", p=128)
        E.load(gbc[:, 0, :], lng, ("gbc", 0))
        E.load(gbc[:, 1, :], lnb, ("gbc", 1))
        E.load(ident[:], ident_d, "identf")
        npair = 0
        ncb = 0
        for blk in blocks:
            x_tm_v = blk["x_tm"].rearrange("(t p) d -> p t d", p=128)
            y_tm_v = blk["y_tm"].rearrange("(t p) d -> p t d", p=128)
            xsrc = blk["xT"].rearrange("(k p) t -> p k t", p=128)
            for t in range(NT):
                E.load(xa[:, t, :], x_tm_v[:, t, :], ("xa", t))
                S.op('act', lambda t=t: nc.scalar.mul(xa[:, t, :], xa[:, t, :], ALPHA), r=[("xa", t)], w=[("xa", t)])
            for k in range(KC):
                E.load(xT[:, k, :], xsrc[:, k, :], ("xT", k), eng=('pool' if blk["xT_f32"] else 'sp'))
            for part in parts:
                fpairs = [part[i:i + 2] for i in range(0, len(part), 2)]
                for fp in fpairs:
                    b = npair % 2
                    npair += 1
                    ncols = 128 * len(fp)
                    c0 = fp[0] * 128
                    E.load(wgu[:, b, 0, :, 0:ncols], wg_v[:, :, c0:c0 + ncols], ("wgu", b, 0), eng='pool')
                    E.load(wgu[:, b, 1, :, 0:ncols], wu_v[:, :, c0:c0 + ncols], ("wgu", b, 1), eng='pool')
                    for j, f in enumerate(fp):
                        fi = f - part[0]
                        for tb in range(NTB):
                            for gu in range(2):
                                bank = gu * 4 + tb
                                for k in range(KC):
                                    E.mm(ps[:, bank, :], wgu[:, b, gu, k, j * 128:(j + 1) * 128], xT[:, k, tb * 512:(tb + 1) * 512],
                                         r=[("wgu", b, gu), ("xT", k)], w=[("ps", bank)], start=(k == 0), stop=(k == KC - 1))
                            E.act(sg[:, tb, :], ps[:, tb, :], AF.Silu, r=[("ps", tb)], w=[("sg", tb)])
                            E.tt(hT[:, fi, tb * 512:(tb + 1) * 512], sg[:, tb, :], ps[:, 4 + tb, :], ALU.mult,
                                 r=[("sg", tb), ("ps", 4 + tb)], w=[("hT", fi)])
                nfp = len(part)
                for cb in range(D // 256):
                    b = ncb % 2
                    ncb += 1
                    E.load(wdb[:, b, 0:nfp, :], wd_v[:, part[0]:part[0] + nfp, cb * 256:(cb + 1) * 256], ("wdb", b), eng='pool')
                    half = b * 4
                    for fi in range(nfp):
                        for t in range(NT):
                            bank, off = half + t // 2, (t % 2) * 256
                            E.mm(ps[:, bank, off:off + 256], hT[:, fi, t * 128:(t + 1) * 128], wdb[:, b, fi, :],
                                 r=[("hT", fi), ("wdb", b)], w=[("ps", bank)], start=(fi == 0 and t % 2 == 0), stop=(fi == nfp - 1), skip=True)
                    for t in range(NT):
                        bank, off = half + t // 2, (t % 2) * 256
                        csl = slice(cb * 256, (cb + 1) * 256)
                        E.stt(xa[:, t, csl], ps[:, bank, off:off + 256], 0.5, xa[:, t, csl], ALU.mult, ALU.add,
                              r=[("ps", bank), ("xa", t)], w=[("xa", t)])
            for t in range(NT):
                _ln_tile(nc, S, E, xa, t, stats, mv, gbc)
                E.store(y_tm_v[:, t, :], xa[:, t, :], ("xa", t))
                if blk["yT"] is not None:
                    _transpose_tile(nc, S, E, xa, t, ps, ident, xTo, lambda j: ("wgu", j // 2, j % 2))
            if blk["yT"] is not None:
                yT_v = blk["yT"].rearrange("(k p) t -> p k t", p=128)
                for j in range(4):
                    E.store(yT_v[:, 4 * j:4 * j + 4, :], xTo[:, 4 * j:4 * j + 4, :], ("wgu", j // 2, j % 2))
        S.flush()


def emit_proj(nc, S, L, xT_blocks, w_d, fm_specs, tm_specs):
    KC = D // 128
    TB = 1024
    with contextlib.ExitStack() as st:
        E = Em(nc, st, S)
        xTb = E.tile("xTb", [128, 2, KC, TB], BF16)
        wf = E.tile("wf", [128, 2, KC, 128], BF16)
        wt = E.tile("wt", [128, 2, KC, 512], BF16)
        ot = E.tile("ot", [128, 4, 512])
        ps = _mk(("ps", [128, 8, 512], F32), nc, st, psum=True)
        wv = w_d.rearrange("(k p) n -> p k n", p=128)
        nf, nt_, no = 0, 0, 0
        for tb in range(L // TB):
            xb = tb % 2
            E.load(xTb[:, xb, :, :], xT_blocks[tb].rearrange("(k p) t -> p k t", p=128), ("xTb", xb))
            for (c0, ncol, dst) in fm_specs:
                wb = nf % 2
                nf += 1
                E.load(wf[:, wb, :, 0:ncol], wv[:, :, c0:c0 + ncol], ("wf", wb), eng='pool')
                for sub in range(TB // 512):
                    ob = no % 4
                    no += 1
                    for k in range(KC):
                        E.mm(ps[0:ncol, ob, :], wf[:, wb, k, 0:ncol], xTb[:, xb, k, sub * 512:(sub + 1) * 512],
                             r=[("wf", wb), ("xTb", xb)], w=[("ps", ob)], start=(k == 0), stop=(k == KC - 1))
                    E.cp(ot[0:ncol, ob, :], ps[0:ncol, ob, :], r=[("ps", ob)], w=[("ot", ob)], eng=('act' if ob % 2 == 0 else 'dve'))
                    t0 = tb * TB + sub * 512
                    E.store(dst[0:ncol, t0:t0 + 512], ot[0:ncol, ob, :], ("ot", ob))
            for (c0, ncol, dst) in tm_specs:
                wb = nt_ % 2
                nt_ += 1
                E.load(wt[:, wb, :, 0:ncol], wv[:, :, c0:c0 + ncol], ("wt", wb), eng='pool')
                for tt in range(TB // 128):
                    ob = no % 4
                    no += 1
                    for k in range(KC):
                        E.mm(ps[:, ob, 0:ncol], xTb[:, xb, k, tt * 128:(tt + 1) * 128], wt[:, wb, k, 0:ncol],
                             r=[("wt", wb), ("xTb", xb)], w=[("ps", ob)], start=(k == 0), stop=(k == KC - 1))
                    E.cp(ot[:, ob, 0:ncol], ps[:, ob, 0:ncol], r=[("ps", ob)], w=[("ot", ob)], eng=('act' if ob % 2 == 0 else 'dve'))
                    r0 = tb * TB + tt * 128
                    E.store(dst[r0:r0 + 128, 0:ncol], ot[:, ob, 0:ncol], ("ot", ob))
        S.flush()


def emit_glu(nc, S, L, yT_d, w_d, b_d, oT_d):
    T = 1024
    with contextlib.ExitStack() as st:
        E = Em(nc, st, S)
        y32, ybf = E.tile("y32", [128, 8, T]), E.tile("ybf", [128, 8, T], BF16)
        wt, bt = E.tile("wt", [128, 8, 1024], BF16), E.tile("bt", [128, 8])
        sg, ot = E.tile("sg", [128, 2, 512]), E.tile("ot", [128, 2, 512])
        ps = _mk(("ps", [128, 8, 512], F32), nc, st, psum=True)
        yv = yT_d.rearrange("(k p) t -> p k t", p=128)
        E.load(wt[:], w_d.rearrange("(k p) n -> p k n", p=128), "wt", eng='pool')
        E.load(bt[:], b_d, "bt")
        i = 0
        for blk in range(L // T):
            bsl = slice(blk * T, (blk + 1) * T)
            E.load(y32[:], yv[:, :, bsl], "y32")
            E.load(ybf[:], yv[:, :, bsl], "ybf", eng='pool')
            for oc in range(8):
                for tb in range(T // 512):
                    bank = i % 8
                    j = i % 2
                    i += 1
                    tsl = slice(tb * 512, (tb + 1) * 512)
                    for k in range(8):
                        E.mm(ps[:, bank, :], wt[:, k, oc * 128:(oc + 1) * 128], ybf[:, k, tsl], r=["wt", "ybf"], w=[("ps", bank)],
                             start=(k == 0), stop=(k == 7))
                    S.op('act', lambda bank=bank, j=j, oc=oc: nc.scalar.activation(out=sg[:, j, :], in_=ps[:, bank, :], func=AF.Sigmoid,
                                                                                 bias=bt[:, oc:oc + 1], scale=1.0),
                         r=[("ps", bank), "bt"], w=[("sg", j)])
                    E.tt(ot[:, j, :], sg[:, j, :], y32[:, oc, tsl], ALU.mult, r=[("sg", j), "y32"], w=[("ot", j)])
                    E.store(oT_d[oc * 128:(oc + 1) * 128, blk * T + tb * 512:blk * T + (tb + 1) * 512], ot[:, j, :], ("ot", j))
        S.flush()


def emit_projln(nc, S, L, KM, x_d, cT_d, w_d, lng_d, lnb_d, ident_d, y_d, yT_d):
    T = 1024
    NT, KC = T // 128, KM // 128
    with contextlib.ExitStack() as st:
        E = Em(nc, st, S)
        xa = E.tile("xa", [128, NT, D])
        cT = E.tile("cT", [128, KC, T], BF16)
        wb = E.tile("wb", [128, 2, KC, 256], BF16)
        gbc = E.tile("gbc", [128, 2, D])
        stats, mv = E.tile("stats", [128, 4, 6]), E.tile("mv", [128, 4])
        ident = E.tile("identf", [128, 128])
        ps = _mk(("ps", [128, 8, 512], F32), nc, st, psum=True)
        E.load(gbc[:, 0, :], lng_d, ("gbc", 0))
        E.load(gbc[:, 1, :], lnb_d, ("gbc", 1))
        E.load(ident[:], ident_d, "identf")
        wv = w_d.rearrange("(k p) d -> p k d", p=128)
        cv = cT_d.rearrange("(k p) t -> p k t", p=128)
        yTv = yT_d.rearrange("(k p) t -> p k t", p=128)
        nw = 0
        for blk in range(L // T):
            bsl = slice(blk * T, (blk + 1) * T)
            xv = x_d[bsl, :].rearrange("(t p) d -> p t d", p=128)
            yv = y_d[bsl, :].rearrange("(t p) d -> p t d", p=128)
            for t in range(NT):
                E.load(xa[:, t, :], xv[:, t, :], ("xa", t))
                S.op('act', lambda t=t: nc.scalar.mul(xa[:, t, :], xa[:, t, :], ALPHA), r=[("xa", t)], w=[("xa", t)])
            E.load(cT[:], cv[:, :, bsl], "cT", eng='pool')
            for cb in range(D // 256):
                b = nw % 2
                nw += 1
                E.load(wb[:, b, :, :], wv[:, :, cb * 256:(cb + 1) * 256], ("wb", b), eng='pool')
                half = b * 4
                for k in range(KC):
                    for t in range(NT):
                        bank, off = half + t // 2, (t % 2) * 256
                        E.mm(ps[:, bank, off:off + 256], cT[:, k, t * 128:(t + 1) * 128], wb[:, b, k, :], r=["cT", ("wb", b)], w=[("ps", bank)],
                             start=(k == 0 and t % 2 == 0), stop=(k == KC - 1), skip=True)
                for t in range(NT):
                    bank, off = half + t // 2, (t % 2) * 256
                    csl = slice(cb * 256, (cb + 1) * 256)
                    E.tt(xa[:, t, csl], ps[:, bank, off:off + 256], xa[:, t, csl], ALU.add, r=[("ps", bank), ("xa", t)], w=[("xa", t)])
            for t in range(NT):
                _ln_tile(nc, S, E, xa, t, stats, mv, gbc)
                E.store(yv[:, t, :], xa[:, t, :], ("xa", t))
                _transpose_tile(nc, S, E, xa, t, ps, ident, cT, lambda j: "cT")
            E.store(yTv[:, :, bsl], cT[:, 0:16, :], "cT")
        S.flush()


SEQ = 4096
NRUN = 8


def build_fused():
    L = SEQ
    NT = L // 128
    nc = bass.Bass("TRN2", target_bir_lowering=False)
    di = lambda n, sh: nc.dram_tensor(n, sh, F32, kind="ExternalInput").ap()
    dt_ = lambda n, sh, dt=F32: nc.dram_tensor(n, sh, dt).ap()
    x_tm, x_fm = di("x_tm", [L, D]), di("x_fm", [D, L])
    fw = {k: di(k, [2, D, DFF] if k[-4:] != "down" else [2, DFF, D])
          for k in ["ffn_a_gate", "ffn_a_up", "ffn_a_down", "ffn_b_gate", "ffn_b_up", "ffn_b_down"]}
    lng, lnb = di("lng_b", [6, 128, D]), di("lnb_b", [6, 128, D])
    ab_w_in, gla_w_lr, gla_b_lr, gla_ng = di("ab_w_in", [D, 4112]), di("gla_w_lr", [16, 512]), di("gla_b_lr", [1, 512]), di("gla_ng", [128, 2])
    ident, ones, tri, msu, msl, swap = [di(n, [128, 128]) for n in ["ident", "ones", "tri", "msu", "msl", "swap"]]
    s5shapes = dict(lre2=[128, 16], lim2=[128, 16], lst2=[128, 16], sgn2=[128, 1], cri=[128, 16, 16], lre_sel=[128, 2, 64],
                    lim_sel=[128, 2, 64], lst_sel=[128, 2, 64], br_sel=[128, 2, 64], bi_sel=[128, 2, 64], maskg=[128, 8], d2=[128, 2])
    s5in = {k: di("s5_" + k, [4] + v) for k, v in s5shapes.items()}
    glu_w, glu_b, ab_w_out = di("s5_glu_w", [1024, 1024]), di("glu_b", [128, 8]), di("ab_w_out", [2048, D])
    gdn_w_in, gdn_w_out = di("gdn_w_in", [D, 12352]), di("gdn_w_out", [4096, D])
    cwq, cwk, cwv = di("cwq", [4, 128, 4, 4]), di("cwk", [4, 128, 4, 4]), di("cwv", [4, 128, 8, 4])
    alog_b, dtb_b, gdn_ng = di("alog_b", [4, 128, NT * 8]), di("dtb_b", [4, 128, NT * 8]), di("gdn_ng", [128, 1])
    sel, mask_rep = di("sel", [8, 8, 128]), di("mask_rep", [128, L])
    y_out = nc.dram_tensor("y", [L, D], F32, kind="ExternalOutput").ap()
    xres, xT = dt_("xres", [L, D]), dt_("xT", [D, L], BF16)
    qT, kT, ktm, vtm = dt_("sc_qT", [128, L]), dt_("sc_kT", [128, L]), dt_("sc_ktm", [L, 128]), dt_("sc_vtm", [L, 256])
    gT, glrT, uT = dt_("sc_gT", [256, L]), dt_("sc_glrT", [16, L]), dt_("sc_uT", [256, L])
    mix0T, ys5T = dt_("mix0T", [2048, L]), dt_("ys5T", [1024, L])
    q_fm, k_fm, v_fm, z_fm = dt_("sc_qfm", [4, 128, L]), dt_("sc_kfm", [4, 128, L]), dt_("sc_vfm", [8, 128, L]), dt_("sc_zfm", [8, 128, L])
    a_fm, b_fm, a_tm, b_tm = dt_("sc_afm", [8, L]), dt_("sc_bfm", [8, L]), dt_("sc_atm", [L, 8]), dt_("sc_btm", [L, 8])
    mix1T = dt_("mix1T", [4096, L])
    NBK = L // 1024
    rows = lambda ap, b: ap[b * 1024:(b + 1) * 1024, :]
    cols = lambda ap, b: ap[:, b * 1024:(b + 1) * 1024]
    with contextlib.ExitStack() as st:
        S = Sched(nc, st)

        def ffn(first, last, wkey, l, lni):
            blocks = []
            for b in range(NBK):
                blocks.append(dict(x_tm=rows(x_tm if first else xres, b), xT=cols(x_fm if first else xT, b), xT_f32=first,
                                   y_tm=rows(y_out if last else xres, b), yT=None if last else cols(xT, b)))
            emit_ffn(nc, S, blocks, fw[wkey + "_gate"][l], fw[wkey + "_up"][l], fw[wkey + "_down"][l], lng[lni], lnb[lni], ident)

        ffn(True, False, "ffn_a", 0, 0)
        for h in range(4):
            fm = [(h * 128, 128, qT), (512 + h * 128, 128, kT), (2048 + h * 256, 128, gT[0:128, :]), (2048 + h * 256 + 128, 128, gT[128:256, :]),
                  (3072, 16, glrT), (3088 + h * 256, 128, uT[0:128, :]), (3088 + h * 256 + 128, 128, uT[128:256, :])]
            tm = [(512 + h * 128, 128, ktm), (1024 + h * 256, 256, vtm)]
            emit_proj(nc, S, L, [cols(xT, b) for b in range(NBK)], ab_w_in, fm, tm)
            emit_gla(nc, S, L, dict(qT=qT, kT=kT, ktm=ktm, vtm=vtm, gT=gT, glrT=glrT, wlr=gla_w_lr[:, h * 128:(h + 1) * 128],
                                    blr=gla_b_lr[:, h * 128:(h + 1) * 128], ng=gla_ng, triT=tri, ones=ones, oT=mix0T[h * 256:(h + 1) * 256, :]))
            io = {k: v[h] for k, v in s5in.items()}
            io.update(uT=uT, yT=ys5T[h * 256:(h + 1) * 256, :], ident=ident, swap=swap)
            emit_s5(nc, S, L, io)
        emit_glu(nc, S, L, ys5T, glu_w, glu_b, mix0T[1024:2048, :])
        emit_projln(nc, S, L, 2048, xres, mix0T, ab_w_out, lng[1], lnb[1], ident, xres, xT)
        ffn(False, False, "ffn_b", 0, 2)
        ffn(False, False, "ffn_a", 1, 3)
        for hg in range(4):
            fm = [(hg * 512 + j * 128, 128, q_fm[j]) for j in range(4)] + [(2048 + hg * 512 + j * 128, 128, k_fm[j]) for j in range(4)]
            fm += [(4096 + hg * 1024 + j * 128, 128, v_fm[j]) for j in range(8)] + [(8192 + hg * 1024 + j * 128, 128, z_fm[j]) for j in range(8)]
            fm += [(12288 + hg * 8, 8, b_fm), (12320 + hg * 8, 8, a_fm)]
            tm = [(12288 + hg * 8, 8, b_tm), (12320 + hg * 8, 8, a_tm)]
            emit_proj(nc, S, L, [cols(xT, b) for b in range(NBK)], gdn_w_in, fm, tm)
            emit_gdn(nc, S, L, dict(q_fm=q_fm, k_fm=k_fm, v_fm=v_fm, z_fm=z_fm, cwq=cwq[hg], cwk=cwk[hg], cwv=cwv[hg], a_fm=a_fm, b_fm=b_fm,
                                    a_tm=a_tm, b_tm=b_tm, alog_b=alog_b[hg], dtb_b=dtb_b[hg], ng=gdn_ng, sel=sel, mask_rep=mask_rep,
                                    ident=ident, ones=ones, tri=tri, msu=msu, msl=msl, oT=mix1T[hg * 1024:(hg + 1) * 1024, :]))
        emit_projln(nc, S, L, 4096, xres, mix1T, gdn_w_out, lng[4], lnb[4], ident, xres, xT)
        ffn(False, True, "ffn_b", 1, 5)
        S.finish()
    return nc


_PROG = {}


def _c(a):
    return np.ascontiguousarray(a, dtype=np.float32)


def kernel(**inp):
    L = SEQ
    NT = L // 128
    if "nc" not in _PROG:
        _PROG["nc"] = build_fused()
    nc = _PROG["nc"]
    A = lambda k: np.asarray(inp[k], dtype=np.float32)
    bcr = lambda v: np.broadcast_to(v[:, None, :], (v.shape[0], 128, v.shape[1]))
    o = np.ones((128, 128), np.float32)
    common = {k: _c(A(k)) for k in ["ffn_a_gate", "ffn_a_up", "ffn_a_down", "ffn_b_gate", "ffn_b_up", "ffn_b_down"]}
    common.update(lng_b=_c(bcr(A("ln_g").reshape(6, D))), lnb_b=_c(bcr(A("ln_b").reshape(6, D))))
    common.update(ab_w_in=_c(A("ab_w_in")[0]), gla_w_lr=_c(A("gla_w_lr")[0]), gla_b_lr=_c(A("gla_b_lr")[0][None, :]),
                  gla_ng=_c(A("gla_norm_g")[0].reshape(2, 128).T))
    gc = gdn_host_consts(L)
    common.update(ident=gc["ident"], ones=o, tri=gc["tri"], msu=gc["msu"], msl=gc["msl"], swap=_c(np.roll(np.eye(128), 64, axis=1)),
                  sel=gc["sel"], mask_rep=gc["mask_rep"])
    lays = []
    for h in range(4):
        gs = slice(16 * h, 16 * h + 16)
        lays.append(s5_host_layout(A("s5_lam_re")[0][gs], A("s5_lam_im")[0][gs], A("s5_b_re")[0][gs], A("s5_b_im")[0][gs],
                                   A("s5_c_re")[0][gs], A("s5_c_im")[0][gs], A("s5_d")[0][h * 256:(h + 1) * 256], A("s5_log_step")[0][gs]))
    for k in ["lre2", "lim2", "lst2", "sgn2", "cri", "lre_sel", "lim_sel", "lst_sel", "br_sel", "bi_sel", "maskg", "d2"]:
        common["s5_" + k] = _c(np.stack([lays[h][k] for h in range(4)], 0))
    common.update(s5_glu_w=_c(A("s5_glu_w")[0]), glu_b=_c(A("s5_glu_b")[0].reshape(8, 128).T), ab_w_out=_c(A("ab_w_out")[0]),
                  gdn_w_in=_c(A("gdn_w_in")[0]), gdn_w_out=_c(A("gdn_w_out")[0]))
    cw = A("gdn_conv_w")[0]
    common.update(cwq=_c(np.stack([cw[:, hg * 512:(hg + 1) * 512].reshape(4, 4, 128).transpose(2, 1, 0) for hg in range(4)], 0)),
                  cwk=_c(np.stack([cw[:, 2048 + hg * 512:2048 + (hg + 1) * 512].reshape(4, 4, 128).transpose(2, 1, 0) for hg in range(4)], 0)),
                  cwv=_c(np.stack([cw[:, 4096 + hg * 1024:4096 + (hg + 1) * 1024].reshape(4, 8, 128).transpose(2, 1, 0) for hg in range(4)], 0)))
    al, db = A("gdn_a_log")[0], A("gdn_dt_bias")[0]
    common.update(alog_b=_c(np.stack([np.broadcast_to(np.tile(al[hg * 8:(hg + 1) * 8], NT)[None, :], (128, NT * 8)) for hg in range(4)], 0)),
                  dtb_b=_c(np.stack([np.broadcast_to(np.tile(db[hg * 8:(hg + 1) * 8], NT)[None, :], (128, NT * 8)) for hg in range(4)], 0)),
                  gdn_ng=_c(A("gdn_norm_g")[0][:, None]))
    x = A("x")
    nb = x.shape[0]
    maps = []
    for c in range(NRUN):
        b = (c * nb) // NRUN
        m = dict(common)
        m.update(x_tm=_c(x[b]), x_fm=_c(x[b].T))
        maps.append(m)
    res = run_bass_kernel_spmd(nc, maps, core_ids=list(range(NRUN))).results
    out = np.stack([res[(b * NRUN) // nb]["y"] for b in range(nb)], 0)
    return out.astype(np.float32)
```

```python
import contextlib
import numpy as np
import concourse.bass as bass
import concourse.mybir as mybir
from concourse.bass_utils import run_bass_kernel_spmd

F32 = mybir.dt.float32
BF16 = mybir.dt.bfloat16
AF = mybir.ActivationFunctionType
ALU = mybir.AluOpType

NCORES = 8
D = 2048
DFF = 5504
ALPHA = (2.0 * 2) ** 0.25
LN_EPS = 1e-5


SAME_ENGINE_INORDER = ('pe',)


class Sched:
    KDMA = 8

    def __init__(self, nc, st):
        self.nc = nc
        self.st = st
        self.E = dict(pe=nc.tensor, act=nc.scalar, dve=nc.vector, pool=nc.gpsimd, sp=nc.sync)
        self.esem = {e: st.enter_context(nc.semaphore("s_" + e)) for e in self.E}
        self.dsem = {e: [st.enter_context(nc.semaphore("d_%s%d" % (e, i))) for i in range(self.kd(e))]
                     for e in ('pool', 'sp')}
        self.cnt = {e: 0 for e in self.E}
        self.ndma = {e: 0 for e in self.E}
        self.dlist = {e: [] for e in self.E}
        self.known = {e: {} for e in self.E}
        self.stream = {e: [] for e in self.E}
        self.W = {}
        self.R = {}
        self.bar = []
        self.need_bar = set()
        self.gate_mod = 0
        self._gate = {}

    def gate_reg(self, e):
        if e not in self._gate:
            eng = self.E[e]
            self._gate[e] = eng.to_reg(eng.partition_id() % self.gate_mod)
        return self._gate[e]

    def kd(self, e):
        return 4 if e == 'pool' else self.KDMA

    def op(self, e, fn, r=(), w=(), dma=False):
        rec = dict(e=e, fn=fn, deps=[], dma=dma, sig=False)
        seen = set()
        isps = lambda x: x == 'psb' or (isinstance(x, tuple) and x[0] in ('ps', 'psb'))
        w = list(w) + [x for x in r if isps(x)]
        r = [x for x in r if not isps(x)]

        def need(p):
            if p is None or id(p) in seen:
                return
            if (not p['dma']) and p['e'] == e and e in SAME_ENGINE_INORDER:
                return
            seen.add(id(p))
            rec['deps'].append(p)
            p['sig'] = True

        if e in self.need_bar:
            for p in self.bar:
                need(p)
            self.need_bar.discard(e)
        for x in r:
            need(self.W.get(x))
        for x in w:
            need(self.W.get(x))
            for p in self.R.get(x, {}).values():
                need(p)
        if dma:
            rec['dn'] = self.ndma[e]
            self.ndma[e] += 1
            rec['sig'] = True
        self.stream[e].append(rec)
        for x in r:
            self.R.setdefault(x, {})[e + ('d' if dma else '')] = rec
        for x in w:
            self.W[x] = rec
            self.R[x] = {}
        return rec

    def flush(self, barrier=True):
        tails = []
        if barrier:
            for e in self.E:
                comp = [r_ for r_ in self.stream[e] if not r_['dma']]
                if comp:
                    comp[-1]['sig'] = True
                    tails.append(comp[-1])
        for e in self.E:
            for rec in self.stream[e]:
                if rec['dma']:
                    n = rec['dn']
                    kd = self.kd(e)
                    rec['sem'] = self.dsem[e][n % kd]
                    rec['val'] = 16 * (n // kd + 1)
                    self.dlist[e].append(rec)
                elif rec['sig']:
                    self.cnt[e] += 1
                    rec['sem'] = self.esem[e]
                    rec['val'] = self.cnt[e]
        for e in self.E:
            eng = self.E[e]
            known = self.known[e]
            if not self.stream[e]:
                continue
            with (eng.If_eq(self.gate_reg(e), 0) if self.gate_mod else contextlib.nullcontext()):
              for rec in self.stream[e]:
                  waits = [(p['sem'], p['val']) for p in rec['deps']]
                  if rec['dma'] and rec['dn'] >= self.kd(e):
                      p = self.dlist[e][rec['dn'] - self.kd(e)]
                      waits.append((p['sem'], p['val']))
                  for sem, val in waits:
                      k = id(sem)
                      if known.get(k, 0) >= val:
                          continue
                      known[k] = val
                      eng.wait_ge(sem, val)
                  if rec.get('self_inc'):
                      rec['fn'](rec['sem'])
                      continue
                  ins = rec['fn']()
                  if rec['dma']:
                      ins.then_inc(rec['sem'], 16)
                  elif rec['sig']:
                      ins.then_inc(rec['sem'], 1)
        if barrier:
            for e in ('pool', 'sp'):
                tails += self.dlist[e][-self.kd(e):]
            self.bar = tails
            self.need_bar = set(self.E)
            self.W = {}
            self.R = {}
        self.stream = {e: [] for e in self.E}

    def finish(self):
        self.flush(barrier=True)
        known = self.known['sp']
        with (self.nc.sync.If_eq(self.gate_reg('sp'), 0) if self.gate_mod else contextlib.nullcontext()):
            for p in self.bar:
                if known.get(id(p['sem']), 0) < p['val']:
                    known[id(p['sem'])] = p['val']
                    self.nc.sync.wait_ge(p['sem'], p['val'])
        if self.gate_mod:
            for e in self.E:
                self.E[e].nop()

    def emit(self, final_waits=()):
        self.finish()


_UID = [0]


def _mk(shape_name_dtype, nc, st, psum=False):
    name, shape, dt = shape_name_dtype
    _UID[0] += 1
    name = "%s_%d" % (name, _UID[0])
    if psum:
        return st.enter_context(nc.psum_tensor(name, shape, dt))
    return st.enter_context(nc.sbuf_tensor(name, shape, dt))


class Em:
    def __init__(self, nc, st, S):
        self.nc, self.st, self.S = nc, st, S

    def tile(self, name, shape, dt=F32):
        return _mk(("sb_" + name, shape, dt), self.nc, self.st)

    def load(self, out, in_, res, eng='sp'):
        nc = self.nc
        q = nc.sync if eng == 'sp' else nc.gpsimd
        return self.S.op(eng, lambda: q.dma_start(out=out, in_=in_), w=[res], dma=True)

    def store(self, out, in_, res, eng='sp'):
        nc = self.nc
        q = nc.sync if eng == 'sp' else nc.gpsimd
        return self.S.op(eng, lambda: q.dma_start(out=out, in_=in_), r=[res], dma=True)

    def tt(self, out, a, b, op, r, w, eng='dve'):
        e = self.nc.vector if eng == 'dve' else self.nc.gpsimd
        return self.S.op(eng, lambda: e.tensor_tensor(out=out, in0=a, in1=b, op=op), r=r, w=w)

    def ts(self, out, a, s1, op0, r, w, s2=None, op1=None):
        nc = self.nc
        if op1 is None:
            return self.S.op('dve', lambda: nc.vector.tensor_scalar(out=out, in0=a, scalar1=s1, scalar2=None, op0=op0), r=r, w=w)
        return self.S.op('dve', lambda: nc.vector.tensor_scalar(out=out, in0=a, scalar1=s1, scalar2=s2, op0=op0, op1=op1), r=r, w=w)

    def stt(self, out, a, sc, b, op0, op1, r, w):
        nc = self.nc
        return self.S.op('dve', lambda: nc.vector.scalar_tensor_tensor(out=out, in0=a, scalar=sc, in1=b, op0=op0, op1=op1), r=r, w=w)

    def act(self, out, a, func, r, w, scale=1.0):
        nc = self.nc
        return self.S.op('act', lambda: nc.scalar.activation(out=out, in_=a, func=func, scale=scale), r=r, w=w)

    def cp(self, out, a, r, w, eng='dve'):
        nc = self.nc
        if eng == 'act':
            return self.S.op('act', lambda: nc.scalar.copy(out, a), r=r, w=w)
        return self.S.op('dve', lambda: nc.vector.tensor_copy(out, a), r=r, w=w)

    def recip(self, out, a, r, w):
        nc = self.nc
        return self.S.op('dve', lambda: nc.vector.reciprocal(out, a), r=r, w=w)

    def memset(self, ap, v, w):
        nc = self.nc
        return self.S.op('dve', lambda: nc.vector.memset(ap, v), w=w)

    def mm(self, out, lhsT, rhs, r, w, start=True, stop=True, skip=False):
        nc = self.nc
        if skip:
            return self.S.op('pe', lambda: nc.tensor.matmul(out, lhsT=lhsT, rhs=rhs, start=start, stop=stop, skip_group_check=True), r=r, w=w)
        return self.S.op('pe', lambda: nc.tensor.matmul(out, lhsT=lhsT, rhs=rhs, start=start, stop=stop), r=r, w=w)


def s5_discretize(E, pfx, lr, li, lst, shape, want_f):
    import math
    mk = lambda n: E.tile(pfx + n, shape)
    step, x, mag, th, c, sn, t1, t2 = mk("step"), mk("x"), mk("mag"), mk("th"), mk("c"), mk("sn"), mk("t1"), mk("t2")
    ar, ai = mk("ar"), mk("ai")
    N = lambda n: pfx + n
    E.act(step[:], lst, AF.Exp, r=[N("lst")], w=[N("step")])
    E.tt(x[:], lr, step[:], ALU.mult, r=[N("lr"), N("step")], w=[N("x")])
    E.act(mag[:], x[:], AF.Exp, r=[N("x")], w=[N("mag")])
    E.tt(th[:], li, step[:], ALU.mult, r=[N("li"), N("step")], w=[N("th")])
    E.ts(th[:], th[:], 1.0 / 16, ALU.mult, r=[N("th")], w=[N("th")])
    E.act(sn[:], th[:], AF.Sin, r=[N("th")], w=[N("sn")])
    E.ts(th[:], th[:], math.pi / 2, ALU.add, r=[N("th")], w=[N("th")])
    E.act(c[:], th[:], AF.Sin, r=[N("th")], w=[N("c")])
    for _ in range(4):
        E.tt(t1[:], c[:], c[:], ALU.mult, r=[N("c")], w=[N("t1")])
        E.tt(t2[:], sn[:], sn[:], ALU.mult, r=[N("sn")], w=[N("t2")])
        E.stt(sn[:], c[:], 2.0, sn[:], ALU.mult, ALU.mult, r=[N("c"), N("sn")], w=[N("sn")])
        E.tt(c[:], t1[:], t2[:], ALU.subtract, r=[N("t1"), N("t2")], w=[N("c")])
    E.tt(ar[:], mag[:], c[:], ALU.mult, r=[N("mag"), N("c")], w=[N("ar")])
    E.tt(ai[:], mag[:], sn[:], ALU.mult, r=[N("mag"), N("sn")], w=[N("ai")])
    if not want_f:
        return ar, ai, None, None
    fr, fi, am1 = mk("fr"), mk("fi"), mk("am1")
    E.tt(t1[:], lr, lr, ALU.mult, r=[N("lr")], w=[N("t1")])
    E.tt(t2[:], li, li, ALU.mult, r=[N("li")], w=[N("t2")])
    E.tt(t1[:], t1[:], t2[:], ALU.add, r=[N("t1"), N("t2")], w=[N("t1")])
    E.recip(t1[:], t1[:], r=[N("t1")], w=[N("t1")])
    E.ts(am1[:], ar[:], -1.0, ALU.add, r=[N("ar")], w=[N("am1")])
    E.tt(fr[:], am1[:], lr, ALU.mult, r=[N("am1"), N("lr")], w=[N("fr")])
    E.tt(t2[:], ai[:], li, ALU.mult, r=[N("ai"), N("li")], w=[N("t2")])
    E.tt(fr[:], fr[:], t2[:], ALU.add, r=[N("fr"), N("t2")], w=[N("fr")])
    E.tt(fr[:], fr[:], t1[:], ALU.mult, r=[N("fr"), N("t1")], w=[N("fr")])
    E.tt(fi[:], ai[:], lr, ALU.mult, r=[N("ai"), N("lr")], w=[N("fi")])
    E.tt(t2[:], am1[:], li, ALU.mult, r=[N("am1"), N("li")], w=[N("t2")])
    E.tt(fi[:], fi[:], t2[:], ALU.subtract, r=[N("fi"), N("t2")], w=[N("fi")])
    E.tt(fi[:], fi[:], t1[:], ALU.mult, r=[N("fi"), N("t1")], w=[N("fi")])
    return ar, ai, fr, fi


def emit_gla(nc, S, L, io):
    qT_d, kT_d, ktm_d, vtm_d, gT_d, glr_d = io["qT"], io["kT"], io["ktm"], io["vtm"], io["gT"], io["glrT"]
    wlr_d, blr_d, ng_d, tri_d, ones_d, oT_d = io["wlr"], io["blr"], io["ng"], io["triT"], io["ones"], io["oT"]
    NT = L // 128
    with contextlib.ExitStack() as st:
        mk = lambda n, sh, dt=F32: _mk((n, sh, dt), nc, st)
        qT, kT = mk("qT_s", [128, L]), mk("kT_s", [128, L])
        ktm = mk("ktm_s", [128, NT, 128])
        vbf = mk("v_s", [128, NT, 256], BF16)
        gT = mk("gT_s", [128, 2, L])
        glr = mk("glr_s", [16, L])
        wlr, blr, ng = mk("wlr_s", [16, 128]), mk("blr_s", [1, 128]), mk("ng_s", [128, 2])
        tri, ones = mk("tri_s", [128, 128]), mk("ones_s", [128, 128])
        St, Sbf = mk("S_s", [128, 256]), mk("Sbf_s", [128, 256], BF16)
        t_e, t_sp = mk("t_e", [128, 128]), mk("t_sp", [128, 128])
        einv_tm, e_fm, einv_fm = mk("einv_tm", [128, 128]), mk("e_fm", [128, 128]), mk("einv_fm", [128, 128])
        kinv_tm, qdec, kinv_fm = mk("kinv_tm", [128, 128], BF16), mk("qdec", [128, 128], BF16), mk("kinv_fm", [128, 128], BF16)
        scm = mk("scm", [128, 128], BF16)
        t_sq, t_r, t_sg, t_on = mk("t_sq", [128, 2, 128]), mk("t_r", [128, 128]), mk("t_sg", [128, 2, 128]), mk("t_on", [128, 2, 128])
        ps = _mk(("ps", [128, 8, 512], F32), nc, st, psum=True)
        ld = lambda e, out, in_, res: S.op(e, lambda: (nc.sync if e == 'sp' else nc.gpsimd).dma_start(out=out, in_=in_), w=[res], dma=True)
        ld('sp', qT[:], qT_d, 'qT'); ld('sp', kT[:], kT_d, 'kT')
        ld('sp', ktm[:], ktm_d.rearrange("(n p) d -> p n d", p=128), 'ktm')
        ld('pool', vbf[:], vtm_d.rearrange("(n p) d -> p n d", p=128), 'v')
        ld('sp', gT[:], gT_d.rearrange("(c p) l -> p c l", p=128), 'gT')
        ld('sp', glr[:], glr_d, 'glr'); ld('sp', wlr[:], wlr_d, 'wlr'); ld('sp', blr[:], blr_d, 'blr')
        ld('sp', ng[:], ng_d, 'ng'); ld('sp', tri[:], tri_d, 'tri'); ld('sp', ones[:], ones_d, 'ones')
        S.op('dve', lambda: nc.vector.memset(St[:], 0.0), w=['S'])
        S.op('dve', lambda: nc.vector.memset(Sbf[:], 0.0), w=['Sbf'])
        outs = []
        P = lambda i, n=128: ps[:, i, 0:n]
        for n in range(NT):
            sl = slice(n * 128, (n + 1) * 128)
            S.op('pe', lambda sl=sl: nc.tensor.matmul(P(0), lhsT=glr[0:16, sl], rhs=wlr[0:16, :], start=True, stop=False),
                 r=['glr', 'wlr'], w=[('ps', 0)])
            S.op('pe', lambda: nc.tensor.matmul(P(0), lhsT=ones[0:1, :], rhs=blr[0:1, :], start=False, stop=True),
                 r=['ones', 'blr'], w=[('ps', 0)])
            S.op('act', lambda: nc.scalar.activation(out=t_e[:], in_=P(0), func=AF.Exp, scale=-1.0), r=[('ps', 0)], w=['t_e'])
            S.op('dve', lambda: nc.vector.tensor_scalar_add(t_e[:], t_e[:], 1.0), r=['t_e'], w=['t_e'])
            S.op('act', lambda: nc.scalar.activation(out=t_sp[:], in_=t_e[:], func=AF.Ln), r=['t_e'], w=['t_sp'])
            S.op('pe', lambda: nc.tensor.matmul(P(1), lhsT=tri[:], rhs=t_sp[:], start=True, stop=True), r=['tri', 't_sp'], w=[('ps', 1)])
            S.op('pe', lambda: nc.tensor.matmul(P(2), lhsT=t_sp[:], rhs=tri[:], start=True, stop=True), r=['tri', 't_sp'], w=[('ps', 2)])
            S.op('act', lambda: nc.scalar.activation(out=einv_tm[:], in_=P(1), func=AF.Exp, scale=1.0 / 16), r=[('ps', 1)], w=['einv_tm'])
            S.op('act', lambda: nc.scalar.activation(out=e_fm[:], in_=P(2), func=AF.Exp, scale=-1.0 / 16), r=[('ps', 2)], w=['e_fm'])
            S.op('act', lambda: nc.scalar.activation(out=einv_fm[:], in_=P(2), func=AF.Exp, scale=1.0 / 16), r=[('ps', 2)], w=['einv_fm'])
            S.op('dve', lambda n=n: nc.vector.tensor_tensor(out=kinv_tm[:], in0=ktm[:, n, :], in1=einv_tm[:], op=ALU.mult),
                 r=['ktm', 'einv_tm'], w=['kinv_tm'])
            S.op('dve', lambda sl=sl: nc.vector.scalar_tensor_tensor(out=qdec[:], in0=qT[:, sl], scalar=128.0 ** -0.5, in1=e_fm[:],
                                                                   op0=ALU.mult, op1=ALU.mult), r=['qT', 'e_fm'], w=['qdec'])
            S.op('dve', lambda sl=sl: nc.vector.tensor_tensor(out=kinv_fm[:], in0=kT[:, sl], in1=einv_fm[:], op=ALU.mult),
                 r=['kT', 'einv_fm'], w=['kinv_fm'])
            S.op('pe', lambda: nc.tensor.matmul(P(3), lhsT=kinv_fm[:], rhs=qdec[:], start=True, stop=True),
                 r=['kinv_fm', 'qdec'], w=[('ps', 3)])
            S.op('dve', lambda: nc.vector.tensor_tensor(out=scm[:], in0=P(3), in1=tri[:], op=ALU.mult), r=[('ps', 3), 'tri'], w=['scm'])
            for c in range(2):
                S.op('pe', lambda c=c, n=n: nc.tensor.matmul(P(4 + c), lhsT=vbf[:, n, c * 128:(c + 1) * 128], rhs=scm[:], start=True, stop=False),
                     r=['v', 'scm'], w=[('ps', 4 + c)])
                S.op('pe', lambda c=c: nc.tensor.matmul(P(4 + c), lhsT=Sbf[:, c * 128:(c + 1) * 128], rhs=qdec[:], start=False, stop=True),
                     r=['Sbf', 'qdec'], w=[('ps', 4 + c)])
            S.op('pe', lambda n=n: nc.tensor.matmul(P(6, 256), lhsT=kinv_tm[:], rhs=vbf[:, n, :], start=True, stop=True),
                 r=['kinv_tm', 'v'], w=[('ps', 6)])
            S.op('dve', lambda: nc.vector.tensor_tensor(out=St[:], in0=St[:], in1=P(6, 256), op=ALU.add), r=['S', ('ps', 6)], w=['S'])
            S.op('dve', lambda: nc.vector.tensor_scalar_mul(St[:], St[:], e_fm[:, 127:128]), r=['S', 'e_fm'], w=['S'])
            S.op('act', lambda: nc.scalar.copy(Sbf[:], St[:]), r=['S'], w=['Sbf'])
            for c in range(2):
                S.op('act', lambda c=c: nc.scalar.activation(out=t_sq[:, c, :], in_=P(4 + c), func=AF.Square), r=[('ps', 4 + c)], w=[('t_sq', c)])
            for c in range(2):
                S.op('pe', lambda c=c: nc.tensor.matmul(P(7), lhsT=ones[:], rhs=t_sq[:, c, :], start=(c == 0), stop=(c == 1)),
                     r=['ones', ('t_sq', c)], w=[('ps', 7)])
            S.op('dve', lambda: nc.vector.tensor_scalar(out=t_r[:], in0=P(7), scalar1=1.0 / 256, scalar2=1e-6, op0=ALU.mult, op1=ALU.add),
                 r=[('ps', 7)], w=['t_r'])
            S.op('act', lambda: nc.scalar.activation(out=t_r[:], in_=t_r[:], func=AF.Sqrt), r=['t_r'], w=['t_r'])
            S.op('dve', lambda: nc.vector.reciprocal(t_r[:], t_r[:]), r=['t_r'], w=['t_r'])
            for c in range(2):
                S.op('act', lambda c=c, sl=sl: nc.scalar.activation(out=t_sg[:, c, :], in_=gT[:, c, sl], func=AF.Silu), r=['gT'], w=[('t_sg', c)])
                S.op('dve', lambda c=c: nc.vector.tensor_tensor(out=t_on[:, c, :], in0=P(4 + c), in1=t_r[:], op=ALU.mult),
                     r=[('ps', 4 + c), 't_r'], w=[('t_on', c)])
                S.op('dve', lambda c=c: nc.vector.scalar_tensor_tensor(out=t_on[:, c, :], in0=t_on[:, c, :], scalar=ng[:, c:c + 1], in1=t_sg[:, c, :],
                                                                      op0=ALU.mult, op1=ALU.mult), r=[('t_on', c), ('t_sg', c), 'ng'], w=[('t_on', c)])
                outs.append(S.op('sp', lambda c=c, sl=sl: nc.sync.dma_start(out=oT_d[c * 128:(c + 1) * 128, sl], in_=t_on[:, c, :]),
                                 r=[('t_on', c)], dma=True))
        S.flush()


def emit_s5(nc, S, L, io):
    import math
    uT_d, yT_d = io["uT"], io["yT"]
    lre2_d, lim2_d, lst2_d, sgn2_d, cri_d = io["lre2"], io["lim2"], io["lst2"], io["sgn2"], io["cri"]
    lres_d, lims_d, lsts_d, brs_d, bis_d = io["lre_sel"], io["lim_sel"], io["lst_sel"], io["br_sel"], io["bi_sel"]
    maskg_d, d2_d, ident_d, swap_d = io["maskg"], io["d2"], io["ident"], io["swap"]
    NL = int(math.log2(L))
    NB = L // 512
    with contextlib.ExitStack() as st:
        E = Em(nc, st, S)
        u32, ubf = E.tile("u32", [128, 2, L]), E.tile("ubf", [128, 2, L], BF16)
        yacc = E.tile("yacc", [128, 2, L])
        sst = E.tile("sst", [128, L], BF16)
        LM = E.tile("LM", [128, 16, NL, 128], BF16)
        LB, LC = E.tile("LB", [128, 16, 128], BF16), E.tile("LC", [128, 16, 128], BF16)
        lre2, lim2, lst2 = E.tile("a_lr", [128, 16]), E.tile("a_li", [128, 16]), E.tile("a_lst", [128, 16])
        sgn2, cri = E.tile("sgn2", [128, 1]), E.tile("cri", [128, 16, 16])
        lres, lims, lsts = E.tile("b_lr", [128, 128]), E.tile("b_li", [128, 128]), E.tile("b_lst", [128, 128])
        brs, bis = E.tile("brs", [128, 128]), E.tile("bis", [128, 128])
        maskg, d2 = E.tile("maskg", [128, 8]), E.tile("d2", [128, 2])
        ident, swp = E.tile("ident", [128, 128]), E.tile("swap", [128, 128])
        ps = _mk(("ps", [128, 8, 512], F32), nc, st, psum=True)
        uv = uT_d.rearrange("(c p) l -> p c l", p=128)
        E.load(u32[:], uv, "u32")
        E.load(ubf[:], uv, "ubf", eng='pool')
        for t_, d_, n_ in [(lre2, lre2_d, "a_lr"), (lim2, lim2_d, "a_li"), (lst2, lst2_d, "a_lst"), (sgn2, sgn2_d, "sgn2"),
                           (maskg, maskg_d, "maskg"), (d2, d2_d, "d2"), (ident, ident_d, "ident"), (swp, swap_d, "swap")]:
            E.load(t_[:], d_, n_)
        E.load(cri[:], cri_d, "cri")
        for t_, d_, n_ in [(lres, lres_d, "b_lr"), (lims, lims_d, "b_li"), (lsts, lsts_d, "b_lst"), (brs, brs_d, "brs"), (bis, bis_d, "bis")]:
            E.load(t_[:], d_.rearrange("p c s -> p (c s)"), n_)
        ar, ai, _, _ = s5_discretize(E, "a_", lre2[:], lim2[:], lst2[:], [128, 16], False)
        AR, AI = E.tile("AR", [128, NL, 16]), E.tile("AI", [128, NL, 16])
        p1, p2 = E.tile("p1", [128, 16]), E.tile("p2", [128, 16])
        E.cp(AR[:, 0, :], ar[:], r=["a_ar"], w=["AR"])
        E.cp(AI[:, 0, :], ai[:], r=["a_ai"], w=["AI"])
        for l in range(1, NL):
            E.tt(p1[:], AR[:, l - 1, :], AR[:, l - 1, :], ALU.mult, r=["AR"], w=["p1"])
            E.tt(p2[:], AI[:, l - 1, :], AI[:, l - 1, :], ALU.mult, r=["AI"], w=["p2"])
            E.stt(AI[:, l, :], AR[:, l - 1, :], 2.0, AI[:, l - 1, :], ALU.mult, ALU.mult, r=["AR", "AI"], w=["AI"])
            E.tt(AR[:, l, :], p1[:], p2[:], ALU.subtract, r=["p1", "p2"], w=["AR"])
        E.ts(AI[:].rearrange("p a b -> p (a b)"), AI[:].rearrange("p a b -> p (a b)"), sgn2[:, 0:1], ALU.mult, r=["AI", "sgn2"], w=["AI"])
        tmpM = E.tile("tmpM", [128, 128])
        for g in range(16):
            for l in range(NL):
                E.ts(tmpM[:], ident[:], AR[:, l, g:g + 1], ALU.mult, r=["ident", "AR"], w=["tmpM"])
                E.stt(LM[:, g, l, :], swp[:], AI[:, l, g:g + 1], tmpM[:], ALU.mult, ALU.add, r=["swap", "AI", "tmpM"], w=[("LM", g)])
        _, _, fr, fi = s5_discretize(E, "b_", lres[:], lims[:], lsts[:], [128, 128], True)
        WB, q1 = E.tile("WB", [128, 2, 128]), E.tile("q1", [128, 128])
        frv, fiv = fr[:].rearrange("p (c s) -> p c s", c=2), fi[:].rearrange("p (c s) -> p c s", c=2)
        brv, biv = brs[:].rearrange("p (c s) -> p c s", c=2), bis[:].rearrange("p (c s) -> p c s", c=2)
        q1v = q1[:].rearrange("p (c s) -> p c s", c=2)
        E.tt(WB[:, :, 0:64], frv, brv, ALU.mult, r=["b_fr", "brs"], w=["WB"])
        E.tt(q1v, fiv, biv, ALU.mult, r=["b_fi", "bis"], w=["q1"])
        E.tt(WB[:, :, 0:64], WB[:, :, 0:64], q1v, ALU.subtract, r=["WB", "q1"], w=["WB"])
        E.tt(WB[:, :, 64:128], frv, biv, ALU.mult, r=["b_fr", "bis", "WB"], w=["WB"])
        E.tt(q1v, fiv, brv, ALU.mult, r=["b_fi", "brs", "WB"], w=["q1"])
        E.tt(WB[:, :, 64:128], WB[:, :, 64:128], q1v, ALU.add, r=["WB", "q1"], w=["WB"])
        for g in range(16):
            E.ts(LB[:, g, :], WB[:, g // 8, :], maskg[:, (g % 8):(g % 8) + 1], ALU.mult, r=["WB", "maskg"], w=[("LB", g)])
        E.memset(LC[:], 0.0, w=[("LC", g) for g in range(16)])
        for g in range(16):
            gp = g % 8
            E.ts(LC[:, g, 16 * gp:16 * gp + 16], cri[:, g, :], sgn2[:, 0:1], ALU.mult, r=["cri", "sgn2"], w=[("LC", g)])
        for c in range(2):
            E.ts(yacc[:, c, :], u32[:, c, :], d2[:, c:c + 1], ALU.mult, r=["u32", "d2"], w=[("yacc", c, b) for b in range(NB)])
        def group_steps(ch, g):
            c = g // 8
            sbuf = sst2[ch]
            st_ = []
            A = st_.append
            cnt = [0]

            def bank():
                cnt[0] += 1
                return 4 * ch + (cnt[0] % 4)
            for b in range(NB):
                bk = bank()
                A(lambda bk=bk, b=b: E.mm(ps[:, bk, :], LB[:, g, :], ubf[:, c, b * 512:(b + 1) * 512], r=[("LB", g), "ubf"], w=[("ps", bk)]))
                A(lambda bk=bk, b=b: E.cp(sbuf[:, b * 512:(b + 1) * 512], ps[:, bk, :], r=[("ps", bk)], w=[("s", ch, b)], eng='act'))
            for l in range(NL):
                sh = 1 << l
                for b in range(NB - 1, -1, -1):
                    lo, hi = max(b * 512, sh), (b + 1) * 512
                    if lo >= hi:
                        continue
                    bk = bank()
                    rb = sorted(set([(lo - sh) // 512, (hi - sh - 1) // 512]))
                    A(lambda bk=bk, b=b, l=l, lo=lo, hi=hi, sh=sh, rb=rb: E.mm(
                        ps[:, bk, lo - b * 512:512], LM[:, g, l, :], sbuf[:, lo - sh:hi - sh],
                        r=[("LM", g)] + [("s", ch, x) for x in rb], w=[("ps", bk)]))
                    A(lambda bk=bk, b=b, lo=lo, hi=hi: E.tt(sbuf[:, lo:hi], ps[:, bk, lo - b * 512:512], sbuf[:, lo:hi], ALU.add,
                                                             r=[("ps", bk), ("s", ch, b)], w=[("s", ch, b)]))
            for b in range(NB):
                bk = bank()
                A(lambda bk=bk, b=b: E.mm(ps[:, bk, :], LC[:, g, :], sbuf[:, b * 512:(b + 1) * 512], r=[("LC", g), ("s", ch, b)], w=[("ps", bk)]))
                A(lambda bk=bk, b=b: E.tt(yacc[:, c, b * 512:(b + 1) * 512], ps[:, bk, :], yacc[:, c, b * 512:(b + 1) * 512], ALU.add,
                                          r=[("ps", bk), ("yacc", c, b)], w=[("yacc", c, b)]))
            return st_

        sst2 = [sst, E.tile("sstB", [128, L], BF16)]
        for g in range(0, 16, 2):
            sa, sb = group_steps(0, g), group_steps(1, g + 1)
            for fa, fb in zip(sa, sb):
                fa()
                fb()
        g1, g2 = E.tile("g1", [128, 2, 512]), E.tile("g2", [128, 2, 512])
        outs = []
        k = 0
        for c in range(2):
            for b in range(NB):
                j = k % 2
                k += 1
                ysl = yacc[:, c, b * 512:(b + 1) * 512]
                E.tt(g1[:, j, :], ysl, ysl, ALU.mult, r=[("yacc", c, b)], w=[("g1", j)])
                E.ts(g1[:, j, :], g1[:, j, :], 0.044715, ALU.mult, r=[("g1", j)], w=[("g1", j)], s2=1.0, op1=ALU.add)
                E.tt(g1[:, j, :], g1[:, j, :], ysl, ALU.mult, r=[("g1", j), ("yacc", c, b)], w=[("g1", j)])
                E.act(g2[:, j, :], g1[:, j, :], AF.Sigmoid, r=[("g1", j)], w=[("g2", j)], scale=2.0 * math.sqrt(2.0 / math.pi))
                E.tt(g2[:, j, :], g2[:, j, :], ysl, ALU.mult, r=[("g2", j), ("yacc", c, b)], w=[("g2", j)])
                outs.append(E.store(yT_d[c * 128:(c + 1) * 128, b * 512:(b + 1) * 512], g2[:, j, :], ("g2", j)))
        S.flush()


def s5_host_layout(lam_re, lam_im, b_re, b_im, c_re, c_im, d, log_step):
    r = np.arange(128)
    p = r % 64
    lre2 = np.ascontiguousarray(lam_re[:, p].T)
    lim2 = np.ascontiguousarray(lam_im[:, p].T)
    lst2 = np.ascontiguousarray(np.broadcast_to(log_step[None, :], (128, 16)))
    sgn2 = np.where(r < 64, 1.0, -1.0).astype(np.float32)[:, None]
    cri = np.concatenate([c_re.transpose(2, 0, 1), c_im.transpose(2, 0, 1)], axis=0)
    gsel = (np.arange(2)[None, :] * 8 + (r // 16)[:, None])
    lre_sel = lam_re[gsel]
    lim_sel = lam_im[gsel]
    lst_sel = np.broadcast_to(log_step[gsel][:, :, None], (128, 2, 64))
    br_sel = b_re[gsel, :, (r % 16)[:, None]]
    bi_sel = b_im[gsel, :, (r % 16)[:, None]]
    maskg = ((r // 16)[:, None] == np.arange(8)[None, :]).astype(np.float32)
    d2 = np.ascontiguousarray(d.reshape(2, 128).T)
    ident = np.eye(128, dtype=np.float32)
    swap = np.roll(ident, 64, axis=1)
    f = lambda a: np.ascontiguousarray(a, dtype=np.float32)
    return dict(lre2=f(lre2), lim2=f(lim2), lst2=f(lst2), sgn2=f(sgn2), cri=f(cri), lre_sel=f(lre_sel), lim_sel=f(lim_sel),
                lst_sel=f(lst_sel), br_sel=f(br_sel), bi_sel=f(bi_sel), maskg=f(maskg), d2=f(d2), ident=ident, swap=f(swap))


def emit_gdn(nc, S, L, io):
    NT = L // 128
    NB = L // 512
    q_d, k_d, v_d, z_d = io["q_fm"], io["k_fm"], io["v_fm"], io["z_fm"]
    cwq_d, cwk_d, cwv_d = io["cwq"], io["cwk"], io["cwv"]
    afm_d, bfm_d, atm_d, btm_d = io["a_fm"], io["b_fm"], io["a_tm"], io["b_tm"]
    alogb_d, dtbb_d, ng_d, sel_d, maskrep_d = io["alog_b"], io["dtb_b"], io["ng"], io["sel"], io["mask_rep"]
    ident_d, ones_d, tri_d, su_d, slm_d, oT_d = io["ident"], io["ones"], io["tri"], io["msu"], io["msl"], io["oT"]
    with contextlib.ExitStack() as st:
        E = Em(nc, st, S)
        T = E.tile
        bigA, bigB = T("bigA", [128, L]), T("bigB", [128, L])
        qn, kn = T("qn", [128, L], BF16), T("kn", [128, L], BF16)
        vs2 = [T("vs%d" % c, [128, L], BF16) for c in range(2)]
        gcrep2, brep2 = [T("gcrep%d" % c, [128, L]) for c in range(2)], [T("brep%d" % c, [128, L]) for c in range(2)]
        mrep = T("mrep", [128, L])
        abfm = T("abfm", [40, L])
        afm, bfm = abfm[0:8, :], abfm[32:40, :]
        atm, btm = T("atm", [128, NT * 8]), T("btm", [128, NT * 8])
        negA_b, dtb_b = T("negA_b", [128, NT * 8]), T("dtb_b", [128, NT * 8])
        gctm, gendtm, negegc, eend = T("gctm", [128, NT * 8]), T("gendtm", [128, NT * 8]), T("negegc", [128, NT * 8]), T("eend", [128, NT * 8])
        cwq, cwk, cwv = T("cwq", [128, 16]), T("cwk", [128, 16]), T("cwv", [128, 32])
        ng, sel2 = T("ng", [128, 1]), T("sel", [40, 8 * 128])
        sel, selb = sel2[0:8, :], sel2[32:40, :]
        ident, ones, tri, msu, msl = T("ident", [128, 128]), T("ones", [128, 128]), T("tri", [128, 128]), T("msu", [128, 128]), T("msl", [128, 128])
        identb = T("identb", [128, 128], BF16)
        ps = _mk(("ps", [128, 6, 512], F32), nc, st, psum=True)
        psb2 = [_mk(("psb%d" % c, [128, 1024], BF16), nc, st, psum=True) for c in range(2)]
        E.load(afm, afm_d, "afm")
        E.load(bfm, bfm_d, "bfm")
        for t_, d_, n_ in [(negA_b, alogb_d, "negA_b"), (dtb_b, dtbb_d, "dtb_b"), (ng, ng_d, "ng"),
                           (mrep, maskrep_d, "mrep"), (ident, ident_d, "ident"), (ones, ones_d, "ones"), (tri, tri_d, "tri"),
                           (msu, su_d, "msu"), (msl, slm_d, "msl")]:
            E.load(t_[:], d_, n_)
        E.load(sel, sel_d.rearrange("k h m -> k (h m)"), "sel")
        E.load(selb, sel_d.rearrange("k h m -> k (h m)"), "selb")
        E.load(cwq[:], cwq_d.rearrange("p h j -> p (h j)"), "cwq")
        E.load(cwk[:], cwk_d.rearrange("p h j -> p (h j)"), "cwk")
        E.load(cwv[:], cwv_d.rearrange("p h j -> p (h j)"), "cwv")
        E.load(atm[:].rearrange("p (n h) -> p n h", h=8), atm_d.rearrange("(n p) h -> p n h", p=128), "atm")
        E.load(btm[:].rearrange("p (n h) -> p n h", h=8), btm_d.rearrange("(n p) h -> p n h", p=128), "btm")
        E.cp(identb[:], ident[:], r=["ident"], w=["identb"])
        E.act(negA_b[:], negA_b[:], AF.Exp, r=["negA_b"], w=["negA_b"])
        E.ts(negA_b[:], negA_b[:], -1.0, ALU.mult, r=["negA_b"], w=["negA_b"])
        E.tt(atm[:], atm[:], dtb_b[:], ALU.add, r=["atm", "dtb_b"], w=["atm"])
        E.act(atm[:], atm[:], AF.Exp, r=["atm"], w=["atm"])
        E.ts(atm[:], atm[:], 1.0, ALU.add, r=["atm"], w=["atm"])
        E.act(atm[:], atm[:], AF.Ln, r=["atm"], w=["atm"])
        E.tt(atm[:], atm[:], negA_b[:], ALU.mult, r=["atm", "negA_b"], w=["atm"])
        E.act(btm[:], btm[:], AF.Sigmoid, r=["btm"], w=["btm"])
        for n in range(NT):
            E.mm(ps[:, 0, n * 8:(n + 1) * 8], tri[:], atm[:, n * 8:(n + 1) * 8], r=["tri", "atm"], w=[("ps", 0)], start=(n == 0), skip=True)
        for n in range(NT):
            E.mm(ps[:, 1, n * 8:(n + 1) * 8], msl[:], atm[:, n * 8:(n + 1) * 8], r=["msl", "atm"], w=[("ps", 1)], start=(n == 0), skip=True)
        E.cp(gctm[:], ps[:, 0, 0:NT * 8], r=[("ps", 0)], w=["gctm"])
        E.act(negegc[:], ps[:, 0, 0:NT * 8], AF.Exp, r=[("ps", 0)], w=["negegc"])
        E.ts(negegc[:], negegc[:], -1.0, ALU.mult, r=["negegc"], w=["negegc"])
        E.act(eend[:], ps[:, 1, 0:NT * 8], AF.Exp, r=[("ps", 1)], w=["eend"])

        def conv_silu(src_d, cw, h, dst_res):
            E.load(bigA[:], src_d, "bigA")
            E.ts(bigB[:], bigA[:], cw[:, 4 * h + 3:4 * h + 4], ALU.mult, r=["bigA", dst_res], w=["bigB"])
            for sft in (1, 2, 3):
                j = 3 - sft
                E.stt(bigB[:, sft:], bigA[:, 0:L - sft], cw[:, 4 * h + j:4 * h + j + 1], bigB[:, sft:], ALU.mult, ALU.add,
                      r=["bigA", "bigB", dst_res], w=["bigB"])
            E.act(bigB[:], bigB[:], AF.Silu, r=["bigB"], w=["bigB"])

        def l2norm_to(dst, dst_res, scale):
            E.tt(bigA[:], bigB[:], bigB[:], ALU.mult, r=["bigB"], w=["bigA"])
            for b in range(NB):
                bank = 2 + (b % 2)
                bs = slice(b * 512, (b + 1) * 512)
                E.mm(ps[:, bank, :], ones[:], bigA[:, bs], r=["ones", "bigA"], w=[("ps", bank)])
                E.ts(bigA[:, bs], ps[:, bank, :], 1e-6, ALU.add, r=[("ps", bank), "bigA"], w=["bigA"])
            E.act(bigA[:], bigA[:], AF.Sqrt, r=["bigA"], w=["bigA"])
            E.recip(bigA[:], bigA[:], r=["bigA"], w=["bigA"])
            E.stt(dst[:], bigB[:], scale, bigA[:], ALU.mult, ALU.mult, r=["bigA", "bigB"], w=[dst_res])

        def gates(c, hv):
            gcrep, brep = gcrep2[c], brep2[c]
            for b in range(NB):
                bank = 2 + (b % 2)
                bs = slice(b * 512, (b + 1) * 512)
                E.mm(ps[:, bank, :], sel[:, hv * 128:(hv + 1) * 128], afm[:, bs], r=["sel", "afm"], w=[("ps", bank)])
                E.ts(bigA[:, bs], ps[:, bank, :], dtb_b[:, hv:hv + 1], ALU.add, r=[("ps", bank), "dtb_b", "bigA"], w=["bigA"])
            E.act(bigA[:], bigA[:], AF.Exp, r=["bigA"], w=["bigA"])
            E.ts(bigA[:], bigA[:], 1.0, ALU.add, r=["bigA"], w=["bigA"])
            E.act(bigA[:], bigA[:], AF.Ln, r=["bigA"], w=["bigA"])
            E.ts(bigA[:], bigA[:], negA_b[:, hv:hv + 1], ALU.mult, r=["bigA", "negA_b"], w=["bigA"])
            S.op('dve', lambda: nc.vector.tensor_tensor_scan(out=gcrep[:], data0=mrep[:], data1=bigA[:], initial=0.0, op0=ALU.mult, op1=ALU.add),
                 r=["mrep", "bigA"], w=[("gcrep", c)])
            for b in range(NB):
                bank = 2 + (b % 2)
                bs = slice(b * 512, (b + 1) * 512)
                E.mm(ps[:, bank, :], selb[:, hv * 128:(hv + 1) * 128], bfm[:, bs], r=["selb", "bfm"], w=[("ps", bank)])
                E.act(brep[:, bs], ps[:, bank, :], AF.Sigmoid, r=[("ps", bank)], w=[("brep", c)])

        mkc = lambda n, sh, dt=F32: [T("%s_%d" % (n, c), sh, dt) for c in range(2)]
        zt, t_on = mkc("zt", [128, 2, 128]), mkc("t_on", [128, 2, 128])
        t_dd, t_dt, t_m = mkc("t_dd", [128, 128]), mkc("t_dt", [128, 128]), mkc("t_m", [128, 128])
        Xb, XTb, ATb = mkc("Xb", [128, 128], BF16), mkc("XTb", [128, 128], BF16), mkc("ATb", [128, 128], BF16)
        Rb, Pb, PTb = mkc("Rb", [128, 2, 128], BF16), mkc("Pb", [128, 2, 128], BF16), mkc("PTb", [128, 2, 128], BF16)
        vtm, kend, rtm = mkc("vtm", [128, 128]), mkc("kend", [128, 128], BF16), mkc("rtm", [128, 128], BF16)
        vnew, qdec = mkc("vnew", [128, 128], BF16), mkc("qdec", [128, 128], BF16)
        t_eg, t_sq, t_r, t_sz = mkc("t_eg", [128, 128]), mkc("t_sq", [128, 128]), mkc("t_r", [128, 128]), mkc("t_sz", [128, 128])
        St2, Sbf2 = mkc("St", [128, 128]), mkc("Sbf", [128, 128], BF16)

        def tile_steps(c, hv, n):
            sl = slice(n * 128, (n + 1) * 128)
            col = slice(n * 8 + hv, n * 8 + hv + 1)
            zb = n % 2
            vs, gcrep, brep, psb, St, Sbf = vs2[c], gcrep2[c], brep2[c], psb2[c], St2[c], Sbf2[c]
            B0, B1, B2 = 3 * c, 3 * c + 1, 3 * c + 2
            R = lambda x: (x, c)
            PB = ("psb", c)
            KK, QK, KS, VN = ps[:, B0, 0:128], ps[:, B0, 128:256], ps[:, B0, 256:384], ps[:, B0, 384:512]
            PP, PT, PR = ps[:, B1, 0:128], ps[:, B1, 128:256], ps[:, B1, 256:384]
            PO, PS_, SS = ps[:, B2, 0:128], ps[:, B2, 128:256], ps[:, B2, 256:384]
            st_ = []
            A = st_.append
            A(lambda: S.op('sp', lambda: nc.sync.dma_start(out=zt[c][:, zb, :], in_=z_d[hv, :, sl]), r=[("zdram", c)], w=[R(("zt", zb))], dma=True))
            A(lambda: E.mm(KK, kn[:, sl], kn[:, sl], r=["kn"], w=[("ps", B0)]))
            A(lambda: E.mm(QK, kn[:, sl], qn[:, sl], r=["kn", "qn"], w=[("ps", B0)], start=False, skip=True))
            A(lambda: E.ts(t_dd[c][:], gcrep[:, sl], gctm[:, col], ALU.subtract, r=[("gcrep", c), "gctm"], w=[R("t_dd")], s2=0.0, op1=ALU.min))
            A(lambda: E.act(t_dt[c][:], t_dd[c][:], AF.Exp, r=[R("t_dd")], w=[R("t_dt")]))
            A(lambda: E.tt(t_m[c][:], t_dt[c][:], msu[:], ALU.mult, r=[R("t_dt"), "msu"], w=[R("t_m")]))
            A(lambda: E.tt(t_m[c][:], t_m[c][:], brep[:, sl], ALU.mult, r=[R("t_m"), ("brep", c)], w=[R("t_m")]))
            A(lambda: E.tt(Xb[c][:], KK, t_m[c][:], ALU.mult, r=[("ps", B0), R("t_m")], w=[R("Xb")]))
            A(lambda: E.tt(t_dt[c][:], t_dt[c][:], tri[:], ALU.mult, r=[R("t_dt"), "tri"], w=[R("t_dt")]))
            A(lambda: E.tt(ATb[c][:], QK, t_dt[c][:], ALU.mult, r=[("ps", B0), R("t_dt")], w=[R("ATb")]))
            A(lambda: S.op('pe', lambda: nc.tensor.transpose(psb[:, 0:128], Xb[c][:], identb[:]), r=[R("Xb"), "identb"], w=[PB]))
            A(lambda: S.op('pe', lambda: nc.tensor.transpose(psb[:, 128:256], kn[:, sl], identb[:]), r=["kn", "identb"], w=[PB]))
            A(lambda: S.op('pe', lambda: nc.tensor.transpose(psb[:, 256:384], vs[:, sl], identb[:]), r=[("vs", c), "identb"], w=[PB]))
            A(lambda: E.cp(XTb[c][:], psb[:, 0:128], r=[PB], w=[R("XTb")], eng='act'))
            A(lambda: E.ts(kend[c][:], psb[:, 128:256], eend[:, col], ALU.mult, r=[PB, "eend"], w=[R("kend")]))
            A(lambda: E.cp(vtm[c][:], psb[:, 256:384], r=[PB], w=[R("vtm")], eng='act'))
            A(lambda: E.tt(Rb[c][:, 0, :], identb[:], Xb[c][:], ALU.subtract, r=["identb", R("Xb")], w=[R(("Rb", 0))]))
            cur = dict(P=Xb[c][:], PT=XTb[c][:], Pr=R("Xb"), PTr=R("XTb"), rc=0)
            for lv in range(1, 7):
                pb = lv % 2
                Pc, PTc, Pres, PTres, rc = cur["P"], cur["PT"], cur["Pr"], cur["PTr"], cur["rc"]
                if lv < 6:
                    A(lambda Pc=Pc, PTc=PTc, Pres=Pres, PTres=PTres: E.mm(PP, PTc, Pc, r=[Pres, PTres], w=[("ps", B1)]))
                A(lambda Pc=Pc, PTc=PTc, Pres=Pres, PTres=PTres: E.mm(PT, Pc, PTc, r=[Pres, PTres], w=[("ps", B1)], start=False, skip=True))
                if lv < 6:
                    A(lambda pb=pb: E.cp(Pb[c][:, pb, :], PP, r=[("ps", B1)], w=[R(("Pb", pb))], eng='act'))
                A(lambda pb=pb: E.cp(PTb[c][:, pb, :], PT, r=[("ps", B1)], w=[R(("PTb", pb))]))
                nP, nPT, nPr, nPTr = Pb[c][:, pb, :], PTb[c][:, pb, :], R(("Pb", pb)), R(("PTb", pb))
                A(lambda nPT=nPT, nPTr=nPTr, rc=rc: E.mm(PR, nPT, Rb[c][:, rc, :], r=[nPTr, R(("Rb", rc))], w=[("ps", B1)], start=False, skip=True))
                A(lambda rc=rc: E.tt(Rb[c][:, 1 - rc, :], PR, Rb[c][:, rc, :], ALU.add, r=[("ps", B1), R(("Rb", rc))], w=[R(("Rb", 1 - rc))]))
                cur = dict(P=nP, PT=nPT, Pr=nPr, PTr=nPTr, rc=1 - rc)
            rc = cur["rc"]
            A(lambda: E.mm(KS, kn[:, sl], Sbf[:], r=["kn", R("Sbf")], w=[("ps", B0)], start=False, skip=True))
            A(lambda: E.stt(vtm[c][:], KS, negegc[:, col], vtm[c][:], ALU.mult, ALU.add, r=[("ps", B0), "negegc", R("vtm")], w=[R("vtm")]))
            A(lambda: E.ts(rtm[c][:], vtm[c][:], btm[:, col], ALU.mult, r=[R("vtm"), "btm"], w=[R("rtm")]))
            A(lambda: E.mm(VN, Rb[c][:, rc, :], rtm[c][:], r=[R(("Rb", rc)), R("rtm")], w=[("ps", B0)], start=False, skip=True))
            A(lambda: E.cp(vnew[c][:], VN, r=[("ps", B0)], w=[R("vnew")], eng='act'))
            A(lambda: E.act(t_eg[c][:], gcrep[:, sl], AF.Exp, r=[("gcrep", c)], w=[R("t_eg")]))
            A(lambda: E.tt(qdec[c][:], qn[:, sl], t_eg[c][:], ALU.mult, r=["qn", R("t_eg")], w=[R("qdec")]))
            A(lambda: E.mm(PO, Sbf[:], qdec[c][:], r=[R("Sbf"), R("qdec")], w=[("ps", B2)], start=True, stop=False))
            A(lambda: E.mm(PO, vnew[c][:], ATb[c][:], r=[R("vnew"), R("ATb")], w=[("ps", B2)], start=False, stop=True))
            A(lambda: E.mm(PS_, kend[c][:], vnew[c][:], r=[R("kend"), R("vnew")], w=[("ps", B2)], start=False, skip=True))
            A(lambda: E.stt(St[:], St[:], t_eg[c][:, 127:128], PS_, ALU.mult, ALU.add, r=[R("St"), R("t_eg"), ("ps", B2)], w=[R("St")]))
            A(lambda: E.cp(Sbf[:], St[:], r=[R("St")], w=[R("Sbf")], eng='act'))
            A(lambda: E.act(t_sq[c][:], PO, AF.Square, r=[("ps", B2)], w=[R("t_sq")]))
            A(lambda: E.mm(SS, ones[:], t_sq[c][:], r=["ones", R("t_sq")], w=[("ps", B2)], start=False, skip=True))
            A(lambda: E.ts(t_r[c][:], SS, 1.0 / 128, ALU.mult, r=[("ps", B2)], w=[R("t_r")], s2=1e-6, op1=ALU.add))
            A(lambda: E.act(t_r[c][:], t_r[c][:], AF.Ln, r=[R("t_r")], w=[R("t_r")]))
            A(lambda: E.act(t_r[c][:], t_r[c][:], AF.Exp, r=[R("t_r")], w=[R("t_r")], scale=-0.5))
            A(lambda: E.tt(t_on[c][:, zb, :], PO, t_r[c][:], ALU.mult, r=[("ps", B2), R("t_r")], w=[R(("t_on", zb))]))
            A(lambda: E.stt(t_on[c][:, zb, :], t_on[c][:, zb, :], ng[:, 0:1], zt[c][:, zb, :], ALU.mult, ALU.mult,
                            r=[R(("t_on", zb)), "ng", R(("zt", zb))], w=[R(("t_on", zb))]))
            A(lambda: E.store(oT_d[hv * 128:(hv + 1) * 128, sl], t_on[c][:, zb, :], R(("t_on", zb))))
            return st_

        for hq in range(4):
            conv_silu(q_d[hq], cwq, hq, "qn")
            l2norm_to(qn, "qn", 128.0 ** -0.5)
            conv_silu(k_d[hq], cwk, hq, "kn")
            l2norm_to(kn, "kn", 1.0)
            for c in range(2):
                hv = 2 * hq + c
                conv_silu(v_d[hv], cwv, hv, ("vs", c))
                E.cp(vs2[c][:], bigB[:], r=["bigB"], w=[("vs", c)], eng='act')
                gates(c, hv)
                E.load(bigA[:], z_d[hv], "bigA")
                E.act(bigA[:], bigA[:], AF.Silu, r=["bigA"], w=["bigA"])
                S.op('sp', lambda hv=hv: nc.sync.dma_start(out=z_d[hv], in_=bigA[:]), r=["bigA"], w=[("zdram", c)], dma=True)
                E.memset(St2[c][:], 0.0, w=[("St", c)])
                E.memset(Sbf2[c][:], 0.0, w=[("Sbf", c)])
            for n in range(NT):
                sa, sb = tile_steps(0, 2 * hq, n), tile_steps(1, 2 * hq + 1, n)
                for fa, fb in zip(sa, sb):
                    fa()
                    fb()
        S.flush()


def gdn_host_consts(L):
    NT = L // 128
    o = np.ones((128, 128), np.float32)
    sel = np.zeros((8, 8, 128), np.float32)
    for h in range(8):
        sel[h, h, :] = 1.0
    mask_rep = np.ones((128, L), np.float32)
    mask_rep[:, ::128] = 0.0
    return dict(sel=sel, mask_rep=mask_rep, ident=np.eye(128, dtype=np.float32), ones=o, tri=np.triu(o), msu=np.triu(o, 1), msl=np.tril(o, -1))


def _ln_tile(nc, S, E, xa, t, stats, mv, gbc):
    for c in range(4):
        S.op('dve', lambda c=c: nc.vector.bn_stats(out=stats[:, c, :], in_=xa[:, t, c * 512:(c + 1) * 512]),
             r=[("xa", t)], w=["stats"])
    S.op('dve', lambda: nc.vector.bn_aggr(out=mv[:, 0:2], in_=stats[:].rearrange("p a b -> p (a b)")), r=["stats"], w=["mv"])
    E.ts(mv[:, 3:4], mv[:, 1:2], LN_EPS, ALU.add, r=["mv"], w=["mv"])
    E.act(mv[:, 3:4], mv[:, 3:4], AF.Sqrt, r=["mv"], w=["mv"])
    E.recip(mv[:, 2:3], mv[:, 3:4], r=["mv"], w=["mv"])
    E.ts(xa[:, t, :], xa[:, t, :], mv[:, 0:1], ALU.subtract, r=["mv", ("xa", t)], w=[("xa", t)], s2=mv[:, 2:3], op1=ALU.mult)
    E.tt(xa[:, t, :], xa[:, t, :], gbc[:, 0, :], ALU.mult, r=[("gbc", 0), ("xa", t)], w=[("xa", t)])
    E.tt(xa[:, t, :], xa[:, t, :], gbc[:, 1, :], ALU.add, r=[("gbc", 1), ("xa", t)], w=[("xa", t)])


def _transpose_tile(nc, S, E, xa, t, ps, ident, xTo, xTo_res):
    for j in range(4):
        for kk in range(4):
            k = 4 * j + kk
            S.op('pe', lambda j=j, kk=kk, k=k: nc.tensor.transpose(ps[:, j, kk * 128:(kk + 1) * 128], xa[:, t, k * 128:(k + 1) * 128], ident[:]),
                 r=[("xa", t), "identf"], w=[("ps", j)])
        src = ps[:, j, :].rearrange("p (a b) -> p a b", a=4)
        dst = xTo[:, 4 * j:4 * j + 4, t * 128:(t + 1) * 128]
        E.cp(dst, src, r=[("ps", j)], w=[xTo_res(j)], eng=('act' if j % 2 == 0 else 'dve'))


def emit_ffn(nc, S, blocks, wg, wu, wd, lng, lnb, ident_d):
    T = 1024
    NT, NTB, KC = T // 128, T // 512, D // 128
    parts = [list(range(0, 15)), list(range(15, 29)), list(range(29, 43))]
    PF = 15
    with contextlib.ExitStack() as st:
        E = Em(nc, st, S)
        xa = E.tile("xa", [128, NT, D])
        xT = E.tile("xT", [128, KC, T], BF16)
        hT = E.tile("hT", [128, PF, T], BF16)
        wgu = E.tile("wgu", [128, 2, 2, KC, 256], BF16)
        wdb = E.tile("wdb", [128, 2, PF, 256], BF16)
        sg = E.tile("sg", [128, 2, 512])
        stats, mv = E.tile("stats", [128, 4, 6]), E.tile("mv", [128, 4])
        gbc = E.tile("gbc", [128, 2, D])
        ident = E.tile("identf", [128, 128])
        ps = _mk(("ps", [128, 8, 512], F32), nc, st, psum=True)
        xTo = wgu[:].rearrange("p a b c d -> p (a b c d)").rearrange("p (k t) -> p k t", k=KC)
        wg_v = wg.rearrange("(k p) f -> p k f", p=128)
        wu_v = wu.rearrange("(k p) f -> p k f", p=128)
        wd_v = wd.rearrange("(f p) d -> p f d", p=128)
        E.load(gbc[:, 0, :], lng, ("gbc", 0))
        E.load(gbc[:, 1, :], lnb, ("gbc", 1))
        E.load(ident[:], ident_d, "identf")
        npair = 0
        ncb = 0
        for blk in blocks:
            x_tm_v = blk["x_tm"].rearrange("(t p) d -> p t d", p=128)
            y_tm_v = blk["y_tm"].rearrange("(t p) d -> p t d", p=128)
            xsrc = blk["xT"].rearrange("(k p) t -> p k t", p=128)
            for t in range(NT):
                E.load(xa[:, t, :], x_tm_v[:, t, :], ("xa", t))
                S.op('act', lambda t=t: nc.scalar.mul(xa[:, t, :], xa[:, t, :], ALPHA), r=[("xa", t)], w=[("xa", t)])
            for k in range(KC):
                E.load(xT[:, k, :], xsrc[:, k, :], ("xT", k), eng=('pool' if blk["xT_f32"] else 'sp'))
            for part in parts:
                fpairs = [part[i:i + 2] for i in range(0, len(part), 2)]
                for fp in fpairs:
                    b = npair % 2
                    npair += 1
                    ncols = 128 * len(fp)
                    c0 = fp[0] * 128
                    E.load(wgu[:, b, 0, :, 0:ncols], wg_v[:, :, c0:c0 + ncols], ("wgu", b, 0), eng='pool')
                    E.load(wgu[:, b, 1, :, 0:ncols], wu_v[:, :, c0:c0 + ncols], ("wgu", b, 1), eng='pool')
                    for j, f in enumerate(fp):
                        fi = f - part[0]
                        for tb in range(NTB):
                            for gu in range(2):
                                bank = gu * 4 + tb
                                for k in range(KC):
                                    E.mm(ps[:, bank, :], wgu[:, b, gu, k, j * 128:(j + 1) * 128], xT[:, k, tb * 512:(tb + 1) * 512],
                                         r=[("wgu", b, gu), ("xT", k)], w=[("ps", bank)], start=(k == 0), stop=(k == KC - 1))
                            E.act(sg[:, tb, :], ps[:, tb, :], AF.Silu, r=[("ps", tb)], w=[("sg", tb)])
                            E.tt(hT[:, fi, tb * 512:(tb + 1) * 512], sg[:, tb, :], ps[:, 4 + tb, :], ALU.mult,
                                 r=[("sg", tb), ("ps", 4 + tb)], w=[("hT", fi)])
                nfp = len(part)
                for cb in range(D // 256):
                    b = ncb % 2
                    ncb += 1
                    E.load(wdb[:, b, 0:nfp, :], wd_v[:, part[0]:part[0] + nfp, cb * 256:(cb + 1) * 256], ("wdb", b), eng='pool')
                    half = b * 4
                    for fi in range(nfp):
                        for t in range(NT):
                            bank, off = half + t // 2, (t % 2) * 256
                            E.mm(ps[:, bank, off:off + 256], hT[:, fi, t * 128:(t + 1) * 128], wdb[:, b, fi, :],
                                 r=[("hT", fi), ("wdb", b)], w=[("ps", bank)], start=(fi == 0 and t % 2 == 0), stop=(fi == nfp - 1), skip=True)
                    for t in range(NT):
                        bank, off = half + t // 2, (t % 2) * 256
                        csl = slice(cb * 256, (cb + 1) * 256)
                        E.stt(xa[:, t, csl], ps[:, bank, off:off + 256], 0.5, xa[:, t, csl], ALU.mult, ALU.add,
                              r=[("ps", bank), ("xa", t)], w=[("xa", t)])
            for t in range(NT):
                _ln_tile(nc, S, E, xa, t, stats, mv, gbc)
                E.store(y_tm_v[:, t, :], xa[:, t, :], ("xa", t))
                if blk["yT"] is not None:
                    _transpose_tile(nc, S, E, xa, t, ps, ident, xTo, lambda j: ("wgu", j // 2, j % 2))
            if blk["yT"] is not None:
                yT_v = blk["yT"].rearrange("(k p) t -> p k t", p=128)
                for j in range(4):
                    E.store(yT_v[:, 4 * j:4 * j + 4, :], xTo[:, 4 * j:4 * j + 4, :], ("wgu", j // 2, j % 2))
        S.flush()


def emit_proj(nc, S, L, xT_blocks, w_d, fm_specs, tm_specs):
    KC = D // 128
    TB = 1024
    with contextlib.ExitStack() as st:
        E = Em(nc, st, S)
        xTb = E.tile("xTb", [128, 2, KC, TB], BF16)
        wf = E.tile("wf", [128, 2, KC, 128], BF16)
        wt = E.tile("wt", [128, 2, KC, 512], BF16)
        ot = E.tile("ot", [128, 4, 512])
        ps = _mk(("ps", [128, 8, 512], F32), nc, st, psum=True)
        wv = w_d.rearrange("(k p) n -> p k n", p=128)
        nf, nt_, no = 0, 0, 0
        for tb in range(L // TB):
            xb = tb % 2
            E.load(xTb[:, xb, :, :], xT_blocks[tb].rearrange("(k p) t -> p k t", p=128), ("xTb", xb))
            for (c0, ncol, dst) in fm_specs:
                wb = nf % 2
                nf += 1
                E.load(wf[:, wb, :, 0:ncol], wv[:, :, c0:c0 + ncol], ("wf", wb), eng='pool')
                for sub in range(TB // 512):
                    ob = no % 4
                    no += 1
                    for k in range(KC):
                        E.mm(ps[0:ncol, ob, :], wf[:, wb, k, 0:ncol], xTb[:, xb, k, sub * 512:(sub + 1) * 512],
                             r=[("wf", wb), ("xTb", xb)], w=[("ps", ob)], start=(k == 0), stop=(k == KC - 1))
                    E.cp(ot[0:ncol, ob, :], ps[0:ncol, ob, :], r=[("ps", ob)], w=[("ot", ob)], eng=('act' if ob % 2 == 0 else 'dve'))
                    t0 = tb * TB + sub * 512
                    E.store(dst[0:ncol, t0:t0 + 512], ot[0:ncol, ob, :], ("ot", ob))
            for (c0, ncol, dst) in tm_specs:
                wb = nt_ % 2
                nt_ += 1
                E.load(wt[:, wb, :, 0:ncol], wv[:, :, c0:c0 + ncol], ("wt", wb), eng='pool')
                for tt in range(TB // 128):
                    ob = no % 4
                    no += 1
                    for k in range(KC):
                        E.mm(ps[:, ob, 0:ncol], xTb[:, xb, k, tt * 128:(tt + 1) * 128], wt[:, wb, k, 0:ncol],
                             r=[("wt", wb), ("xTb", xb)], w=[("ps", ob)], start=(k == 0), stop=(k == KC - 1))
                    E.cp(ot[:, ob, 0:ncol], ps[:, ob, 0:ncol], r=[("ps", ob)], w=[("ot", ob)], eng=('act' if ob % 2 == 0 else 'dve'))
                    r0 = tb * TB + tt * 128
                    E.store(dst[r0:r0 + 128, 0:ncol], ot[:, ob, 0:ncol], ("ot", ob))
        S.flush()


def emit_glu(nc, S, L, yT_d, w_d, b_d, oT_d):
    T = 1024
    with contextlib.ExitStack() as st:
        E = Em(nc, st, S)
        y32, ybf = E.tile("y32", [128, 8, T]), E.tile("ybf", [128, 8, T], BF16)
        wt, bt = E.tile("wt", [128, 8, 1024], BF16), E.tile("bt", [128, 8])
        sg, ot = E.tile("sg", [128, 2, 512]), E.tile("ot", [128, 2, 512])
        ps = _mk(("ps", [128, 8, 512], F32), nc, st, psum=True)
        yv = yT_d.rearrange("(k p) t -> p k t", p=128)
        E.load(wt[:], w_d.rearrange("(k p) n -> p k n", p=128), "wt", eng='pool')
        E.load(bt[:], b_d, "bt")
        i = 0
        for blk in range(L // T):
            bsl = slice(blk * T, (blk + 1) * T)
            E.load(y32[:], yv[:, :, bsl], "y32")
            E.load(ybf[:], yv[:, :, bsl], "ybf", eng='pool')
            for oc in range(8):
                for tb in range(T // 512):
                    bank = i % 8
                    j = i % 2
                    i += 1
                    tsl = slice(tb * 512, (tb + 1) * 512)
                    for k in range(8):
                        E.mm(ps[:, bank, :], wt[:, k, oc * 128:(oc + 1) * 128], ybf[:, k, tsl], r=["wt", "ybf"], w=[("ps", bank)],
                             start=(k == 0), stop=(k == 7))
                    S.op('act', lambda bank=bank, j=j, oc=oc: nc.scalar.activation(out=sg[:, j, :], in_=ps[:, bank, :], func=AF.Sigmoid,
                                                                                 bias=bt[:, oc:oc + 1], scale=1.0),
                         r=[("ps", bank), "bt"], w=[("sg", j)])
                    E.tt(ot[:, j, :], sg[:, j, :], y32[:, oc, tsl], ALU.mult, r=[("sg", j), "y32"], w=[("ot", j)])
                    E.store(oT_d[oc * 128:(oc + 1) * 128, blk * T + tb * 512:blk * T + (tb + 1) * 512], ot[:, j, :], ("ot", j))
        S.flush()


def emit_projln(nc, S, L, KM, x_d, cT_d, w_d, lng_d, lnb_d, ident_d, y_d, yT_d):
    T = 1024
    NT, KC = T // 128, KM // 128
    with contextlib.ExitStack() as st:
        E = Em(nc, st, S)
        xa = E.tile("xa", [128, NT, D])
        cT = E.tile("cT", [128, KC, T], BF16)
        wb = E.tile("wb", [128, 2, KC, 256], BF16)
        gbc = E.tile("gbc", [128, 2, D])
        stats, mv = E.tile("stats", [128, 4, 6]), E.tile("mv", [128, 4])
        ident = E.tile("identf", [128, 128])
        ps = _mk(("ps", [128, 8, 512], F32), nc, st, psum=True)
        E.load(gbc[:, 0, :], lng_d, ("gbc", 0))
        E.load(gbc[:, 1, :], lnb_d, ("gbc", 1))
        E.load(ident[:], ident_d, "identf")
        wv = w_d.rearrange("(k p) d -> p k d", p=128)
        cv = cT_d.rearrange("(k p) t -> p k t", p=128)
        yTv = yT_d.rearrange("(k p) t -> p k t", p=128)
        nw = 0
        for blk in range(L // T):
            bsl = slice(blk * T, (blk + 1) * T)
            xv = x_d[bsl, :].rearrange("(t p) d -> p t d", p=128)
            yv = y_d[bsl, :].rearrange("(t p) d -> p t d", p=128)
            for t in range(NT):
                E.load(xa[:, t, :], xv[:, t, :], ("xa", t))
                S.op('act', lambda t=t: nc.scalar.mul(xa[:, t, :], xa[:, t, :], ALPHA), r=[("xa", t)], w=[("xa", t)])
            E.load(cT[:], cv[:, :, bsl], "cT", eng='pool')
            for cb in range(D // 256):
                b = nw % 2
                nw += 1
                E.load(wb[:, b, :, :], wv[:, :, cb * 256:(cb + 1) * 256], ("wb", b), eng='pool')
                half = b * 4
                for k in range(KC):
                    for t in range(NT):
                        bank, off = half + t // 2, (t % 2) * 256
                        E.mm(ps[:, bank, off:off + 256], cT[:, k, t * 128:(t + 1) * 128], wb[:, b, k, :], r=["cT", ("wb", b)], w=[("ps", bank)],
                             start=(k == 0 and t % 2 == 0), stop=(k == KC - 1), skip=True)
                for t in range(NT):
                    bank, off = half + t // 2, (t % 2) * 256
                    csl = slice(cb * 256, (cb + 1) * 256)
                    E.tt(xa[:, t, csl], ps[:, bank, off:off + 256], xa[:, t, csl], ALU.add, r=[("ps", bank), ("xa", t)], w=[("xa", t)])
            for t in range(NT):
                _ln_tile(nc, S, E, xa, t, stats, mv, gbc)
                E.store(yv[:, t, :], xa[:, t, :], ("xa", t))
                _transpose_tile(nc, S, E, xa, t, ps, ident, cT, lambda j: "cT")
            E.store(yTv[:, :, bsl], cT[:, 0:16, :], "cT")
        S.flush()


SEQ = 4096
NRUN = 8


def build_fused():
    L = SEQ
    NT = L // 128
    nc = bass.Bass("TRN2", target_bir_lowering=False)
    di = lambda n, sh: nc.dram_tensor(n, sh, F32, kind="ExternalInput").ap()
    dt_ = lambda n, sh, dt=F32: nc.dram_tensor(n, sh, dt).ap()
    x_tm, x_fm = di("x_tm", [L, D]), di("x_fm", [D, L])
    fw = {k: di(k, [2, D, DFF] if k[-4:] != "down" else [2, DFF, D])
          for k in ["ffn_a_gate", "ffn_a_up", "ffn_a_down", "ffn_b_gate", "ffn_b_up", "ffn_b_down"]}
    lng, lnb = di("lng_b", [6, 128, D]), di("lnb_b", [6, 128, D])
    ab_w_in, gla_w_lr, gla_b_lr, gla_ng = di("ab_w_in", [D, 4112]), di("gla_w_lr", [16, 512]), di("gla_b_lr", [1, 512]), di("gla_ng", [128, 2])
    ident, ones, tri, msu, msl, swap = [di(n, [128, 128]) for n in ["ident", "ones", "tri", "msu", "msl", "swap"]]
    s5shapes = dict(lre2=[128, 16], lim2=[128, 16], lst2=[128, 16], sgn2=[128, 1], cri=[128, 16, 16], lre_sel=[128, 2, 64],
                    lim_sel=[128, 2, 64], lst_sel=[128, 2, 64], br_sel=[128, 2, 64], bi_sel=[128, 2, 64], maskg=[128, 8], d2=[128, 2])
    s5in = {k: di("s5_" + k, [4] + v) for k, v in s5shapes.items()}
    glu_w, glu_b, ab_w_out = di("s5_glu_w", [1024, 1024]), di("glu_b", [128, 8]), di("ab_w_out", [2048, D])
    gdn_w_in, gdn_w_out = di("gdn_w_in", [D, 12352]), di("gdn_w_out", [4096, D])
    cwq, cwk, cwv = di("cwq", [4, 128, 4, 4]), di("cwk", [4, 128, 4, 4]), di("cwv", [4, 128, 8, 4])
    alog_b, dtb_b, gdn_ng = di("alog_b", [4, 128, NT * 8]), di("dtb_b", [4, 128, NT * 8]), di("gdn_ng", [128, 1])
    sel, mask_rep = di("sel", [8, 8, 128]), di("mask_rep", [128, L])
    y_out = nc.dram_tensor("y", [L, D], F32, kind="ExternalOutput").ap()
    xres, xT = dt_("xres", [L, D]), dt_("xT", [D, L], BF16)
    qT, kT, ktm, vtm = dt_("sc_qT", [128, L]), dt_("sc_kT", [128, L]), dt_("sc_ktm", [L, 128]), dt_("sc_vtm", [L, 256])
    gT, glrT, uT = dt_("sc_gT", [256, L]), dt_("sc_glrT", [16, L]), dt_("sc_uT", [256, L])
    mix0T, ys5T = dt_("mix0T", [2048, L]), dt_("ys5T", [1024, L])
    q_fm, k_fm, v_fm, z_fm = dt_("sc_qfm", [4, 128, L]), dt_("sc_kfm", [4, 128, L]), dt_("sc_vfm", [8, 128, L]), dt_("sc_zfm", [8, 128, L])
    a_fm, b_fm, a_tm, b_tm = dt_("sc_afm", [8, L]), dt_("sc_bfm", [8, L]), dt_("sc_atm", [L, 8]), dt_("sc_btm", [L, 8])
    mix1T = dt_("mix1T", [4096, L])
    NBK = L // 1024
    rows = lambda ap, b: ap[b * 1024:(b + 1) * 1024, :]
    cols = lambda ap, b: ap[:, b * 1024:(b + 1) * 1024]
    with contextlib.ExitStack() as st:
        S = Sched(nc, st)

        def ffn(first, last, wkey, l, lni):
            blocks = []
            for b in range(NBK):
                blocks.append(dict(x_tm=rows(x_tm if first else xres, b), xT=cols(x_fm if first else xT, b), xT_f32=first,
                                   y_tm=rows(y_out if last else xres, b), yT=None if last else cols(xT, b)))
            emit_ffn(nc, S, blocks, fw[wkey + "_gate"][l], fw[wkey + "_up"][l], fw[wkey + "_down"][l], lng[lni], lnb[lni], ident)

        ffn(True, False, "ffn_a", 0, 0)
        for h in range(4):
            fm = [(h * 128, 128, qT), (512 + h * 128, 128, kT), (2048 + h * 256, 128, gT[0:128, :]), (2048 + h * 256 + 128, 128, gT[128:256, :]),
                  (3072, 16, glrT), (3088 + h * 256, 128, uT[0:128, :]), (3088 + h * 256 + 128, 128, uT[128:256, :])]
            tm = [(512 + h * 128, 128, ktm), (1024 + h * 256, 256, vtm)]
            emit_proj(nc, S, L, [cols(xT, b) for b in range(NBK)], ab_w_in, fm, tm)
            emit_gla(nc, S, L, dict(qT=qT, kT=kT, ktm=ktm, vtm=vtm, gT=gT, glrT=glrT, wlr=gla_w_lr[:, h * 128:(h + 1) * 128],
                                    blr=gla_b_lr[:, h * 128:(h + 1) * 128], ng=gla_ng, triT=tri, ones=ones, oT=mix0T[h * 256:(h + 1) * 256, :]))
            io = {k: v[h] for k, v in s5in.items()}
            io.update(uT=uT, yT=ys5T[h * 256:(h + 1) * 256, :], ident=ident, swap=swap)
            emit_s5(nc, S, L, io)
        emit_glu(nc, S, L, ys5T, glu_w, glu_b, mix0T[1024:2048, :])
        emit_projln(nc, S, L, 2048, xres, mix0T, ab_w_out, lng[1], lnb[1], ident, xres, xT)
        ffn(False, False, "ffn_b", 0, 2)
        ffn(False, False, "ffn_a", 1, 3)
        for hg in range(4):
            fm = [(hg * 512 + j * 128, 128, q_fm[j]) for j in range(4)] + [(2048 + hg * 512 + j * 128, 128, k_fm[j]) for j in range(4)]
            fm += [(4096 + hg * 1024 + j * 128, 128, v_fm[j]) for j in range(8)] + [(8192 + hg * 1024 + j * 128, 128, z_fm[j]) for j in range(8)]
            fm += [(12288 + hg * 8, 8, b_fm), (12320 + hg * 8, 8, a_fm)]
            tm = [(12288 + hg * 8, 8, b_tm), (12320 + hg * 8, 8, a_tm)]
            emit_proj(nc, S, L, [cols(xT, b) for b in range(NBK)], gdn_w_in, fm, tm)
            emit_gdn(nc, S, L, dict(q_fm=q_fm, k_fm=k_fm, v_fm=v_fm, z_fm=z_fm, cwq=cwq[hg], cwk=cwk[hg], cwv=cwv[hg], a_fm=a_fm, b_fm=b_fm,
                                    a_tm=a_tm, b_tm=b_tm, alog_b=alog_b[hg], dtb_b=dtb_b[hg], ng=gdn_ng, sel=sel, mask_rep=mask_rep,
                                    ident=ident, ones=ones, tri=tri, msu=msu, msl=msl, oT=mix1T[hg * 1024:(hg + 1) * 1024, :]))
        emit_projln(nc, S, L, 4096, xres, mix1T, gdn_w_out, lng[4], lnb[4], ident, xres, xT)
        ffn(False, True, "ffn_b", 1, 5)
        S.finish()
    return nc


_PROG = {}


def _c(a):
    return np.ascontiguousarray(a, dtype=np.float32)


def kernel(**inp):
    L = SEQ
    NT = L // 128
    if "nc" not in _PROG:
        _PROG["nc"] = build_fused()
    nc = _PROG["nc"]
    A = lambda k: np.asarray(inp[k], dtype=np.float32)
    bcr = lambda v: np.broadcast_to(v[:, None, :], (v.shape[0], 128, v.shape[1]))
    o = np.ones((128, 128), np.float32)
    common = {k: _c(A(k)) for k in ["ffn_a_gate", "ffn_a_up", "ffn_a_down", "ffn_b_gate", "ffn_b_up", "ffn_b_down"]}
    common.update(lng_b=_c(bcr(A("ln_g").reshape(6, D))), lnb_b=_c(bcr(A("ln_b").reshape(6, D))))
    common.update(ab_w_in=_c(A("ab_w_in")[0]), gla_w_lr=_c(A("gla_w_lr")[0]), gla_b_lr=_c(A("gla_b_lr")[0][None, :]),
                  gla_ng=_c(A("gla_norm_g")[0].reshape(2, 128).T))
    gc = gdn_host_consts(L)
    common.update(ident=gc["ident"], ones=o, tri=gc["tri"], msu=gc["msu"], msl=gc["msl"], swap=_c(np.roll(np.eye(128), 64, axis=1)),
                  sel=gc["sel"], mask_rep=gc["mask_rep"])
    lays = []
    for h in range(4):
        gs = slice(16 * h, 16 * h + 16)
        lays.append(s5_host_layout(A("s5_lam_re")[0][gs], A("s5_lam_im")[0][gs], A("s5_b_re")[0][gs], A("s5_b_im")[0][gs],
                                   A("s5_c_re")[0][gs], A("s5_c_im")[0][gs], A("s5_d")[0][h * 256:(h + 1) * 256], A("s5_log_step")[0][gs]))
    for k in ["lre2", "lim2", "lst2", "sgn2", "cri", "lre_sel", "lim_sel", "lst_sel", "br_sel", "bi_sel", "maskg", "d2"]:
        common["s5_" + k] = _c(np.stack([lays[h][k] for h in range(4)], 0))
    common.update(s5_glu_w=_c(A("s5_glu_w")[0]), glu_b=_c(A("s5_glu_b")[0].reshape(8, 128).T), ab_w_out=_c(A("ab_w_out")[0]),
                  gdn_w_in=_c(A("gdn_w_in")[0]), gdn_w_out=_c(A("gdn_w_out")[0]))
    cw = A("gdn_conv_w")[0]
    common.update(cwq=_c(np.stack([cw[:, hg * 512:(hg + 1) * 512].reshape(4, 4, 128).transpose(2, 1, 0) for hg in range(4)], 0)),
                  cwk=_c(np.stack([cw[:, 2048 + hg * 512:2048 + (hg + 1) * 512].reshape(4, 4, 128).transpose(2, 1, 0) for hg in range(4)], 0)),
                  cwv=_c(np.stack([cw[:, 4096 + hg * 1024:4096 + (hg + 1) * 1024].reshape(4, 8, 128).transpose(2, 1, 0) for hg in range(4)], 0)))
    al, db = A("gdn_a_log")[0], A("gdn_dt_bias")[0]
    common.update(alog_b=_c(np.stack([np.broadcast_to(np.tile(al[hg * 8:(hg + 1) * 8], NT)[None, :], (128, NT * 8)) for hg in range(4)], 0)),
                  dtb_b=_c(np.stack([np.broadcast_to(np.tile(db[hg * 8:(hg + 1) * 8], NT)[None, :], (128, NT * 8)) for hg in range(4)], 0)),
                  gdn_ng=_c(A("gdn_norm_g")[0][:, None]))
    x = A("x")
    nb = x.shape[0]
    big = ["ffn_a_gate", "ffn_a_up", "ffn_a_down", "ffn_b_gate", "ffn_b_up", "ffn_b_down", "ab_w_in", "ab_w_out",
           "s5_glu_w", "gdn_w_in", "gdn_w_out"]
    spare = dict(common)
    for k in big:
        spare[k] = np.zeros_like(common[k])
    spare.update(x_tm=np.zeros((L, D), np.float32), x_fm=np.zeros((D, L), np.float32))
    prim = [(b * NRUN) // nb for b in range(nb)]
    maps = []
    for c in range(NRUN):
        if c in prim:
            b = prim.index(c)
            m = dict(common)
            m.update(x_tm=_c(x[b]), x_fm=_c(x[b].T))
        else:
            m = spare
        maps.append(m)
    res = run_bass_kernel_spmd(nc, maps, core_ids=list(range(NRUN))).results
    out = np.stack([res[(b * NRUN) // nb]["y"] for b in range(nb)], 0)
    return out.astype(np.float32)
```
